# Optimizing a Trainium2 kernel written in Bass

```python
import jax
import jax.numpy as jnp
from jax import lax
import numpy as np

D_MODEL = 2048
BATCH = 4
SEQ = 2048
DEPTH = 2

CHUNK = 64
EPS = 1e-6
MIX_WIDTH = D_MODEL
N_MIXERS = 4
GROUP_WIDTH = MIX_WIDTH // N_MIXERS

CONV_A_WIDTH = 3
POOL_WINDOWS = (2, 4, 8, 16)
POOL_GROUP = GROUP_WIDTH // len(POOL_WINDOWS)
GDN_HEADS = 4
GDN_HEAD_DIM = GROUP_WIDTH // GDN_HEADS
GDN_CONV = 4
SSM_HEAD_DIM = 64
SSM_HEADS = GROUP_WIDTH // SSM_HEAD_DIM
SSM_GROUPS = 2
SSM_STATE = 128
SSM_CONV = 4
SSM_XBC = GROUP_WIDTH + 2 * SSM_GROUPS * SSM_STATE

A_COLS = 3 * GROUP_WIDTH
B_COLS = GROUP_WIDTH
C_COLS = 4 * GROUP_WIDTH + 2 * GDN_HEADS
D_COLS = GROUP_WIDTH + SSM_XBC + SSM_HEADS
IN_COLS = A_COLS + B_COLS + C_COLS + D_COLS
IN_SPLITS = (A_COLS, A_COLS + B_COLS, A_COLS + B_COLS + C_COLS)

MEM_LEN = 256
XA_HEADS = 4
XA_HEAD_DIM = D_MODEL // XA_HEADS
D_FF = -(-8 * D_MODEL // (3 * 256)) * 256

MIX_PRE = 0
MIX_POST = 1
XA_PRE = 2
XA_POST = 3
MEM_NORM = 4
FFN_PRE = 5
FFN_POST = 6
N_NORMS = 7

kernel_name = 'hybrid_parallel_mixer_stream_encoder'


def rmsnorm(x, g):
    xf = x.astype(jnp.float32)
    y = xf * lax.rsqrt(jnp.mean(xf * xf, axis=-1, keepdims=True) + EPS)
    return (y * g.astype(jnp.float32)).astype(x.dtype)


def l2norm(x):
    return x * lax.rsqrt(jnp.sum(x * x, axis=-1, keepdims=True) + EPS)


def causal_dwconv(x, w):
    width, ch = w.shape
    return lax.conv_general_dilated(
        x, w[:, None, :], window_strides=(1,), padding=[(width - 1, 0)],
        dimension_numbers=('NWC', 'WIO', 'NWC'), feature_group_count=ch)


def short_conv_mixer(u, conv_w):
    b_gate, c_gate, h = jnp.split(u, 3, axis=-1)
    return b_gate * causal_dwconv(c_gate * h, conv_w)


def multiscale_pool_mixer(u, pool_w, pool_scale):
    bsz, T, _ = u.shape
    uf = u.astype(jnp.float32).reshape(bsz, T, len(POOL_WINDOWS), POOL_GROUP)
    cs = jnp.pad(jnp.cumsum(uf, axis=1), ((0, 0), (1, 0), (0, 0), (0, 0)))
    pos = jnp.arange(1, T + 1, dtype=jnp.float32)
    pooled = []
    for gi, w in enumerate(POOL_WINDOWS):
        cg = cs[:, :, gi]
        lo = jnp.pad(cg, ((0, 0), (w - 1, 0), (0, 0)))[:, :T]
        cnt = jnp.minimum(pos, float(w))
        pooled.append((cg[:, 1:] - lo) / cnt[None, :, None])
    pooled = jnp.stack(pooled, axis=2) - uf
    y = jnp.einsum('btgc,gcd->btgd', pooled, pool_w.astype(jnp.float32))
    y = y.reshape(bsz, T, GROUP_WIDTH) * pool_scale.astype(jnp.float32)
    return y.astype(u.dtype)


def gated_delta_rule_chunked(q, k, v, g, beta):
    bsz, T, H, dk = q.shape
    dv = v.shape[-1]
    n = T // CHUNK

    def to_blocks(t):
        t = t.astype(jnp.float32).reshape((bsz, n, CHUNK, H) + t.shape[3:])
        return jnp.moveaxis(t, 3, 1)

    q, k, v, g, beta = (to_blocks(t) for t in (q, k, v, g, beta))
    G = jnp.cumsum(g, axis=-1)
    incl = jnp.tril(jnp.ones((CHUNK, CHUNK), dtype=bool))
    strict = jnp.tril(jnp.ones((CHUNK, CHUNK), dtype=bool), k=-1)
    diff = G[..., :, None] - G[..., None, :]
    decay = jnp.where(incl, jnp.exp(jnp.where(incl, diff, 0.0)), 0.0)
    kb = k * beta[..., None]
    m_low = jnp.where(strict, jnp.einsum('bhnid,bhnjd->bhnij', kb, k) * decay, 0.0)
    eye = jnp.eye(CHUNK, dtype=jnp.float32)
    rhs = jnp.concatenate([v * beta[..., None], kb * jnp.exp(G)[..., None]], axis=-1)
    sol = lax.linalg.triangular_solve(eye + m_low, rhs, left_side=True, lower=True,
                                      unit_diagonal=True)
    u_blk, w_blk = sol[..., :dv], sol[..., dv:]
    qk = jnp.einsum('bhnid,bhnjd->bhnij', q, k) * decay
    q_dec = q * jnp.exp(G)[..., None]
    k_end = k * jnp.exp(G[..., -1:] - G)[..., None]
    blk_dec = jnp.exp(G[..., -1])
    xs = tuple(jnp.moveaxis(t, 2, 0) for t in (q_dec, qk, u_blk, w_blk, k_end, blk_dec))

    def step(S, inp):
        q_c, qk_c, u_c, w_c, k_c, d_c = inp
        v_new = u_c - jnp.einsum('bhlk,bhkv->bhlv', w_c, S)
        o_c = jnp.einsum('bhlk,bhkv->bhlv', q_c, S) + jnp.einsum('bhlm,bhmv->bhlv', qk_c, v_new)
        S = S * d_c[..., None, None] + jnp.einsum('bhlk,bhlv->bhkv', k_c, v_new)
        return S, o_c

    S0 = jnp.zeros((bsz, H, dk, dv), jnp.float32)
    _, o = lax.scan(step, S0, xs)
    return jnp.transpose(o, (1, 0, 3, 2, 4)).reshape(bsz, T, H, dv)


def gated_deltanet_mixer(u, conv_w, A_log, dt_bias, norm_g):
    bsz, T, _ = u.shape
    qkv, z, a, b = jnp.split(u, (3 * GROUP_WIDTH, 4 * GROUP_WIDTH, 4 * GROUP_WIDTH + GDN_HEADS), axis=-1)
    qkv = jax.nn.silu(causal_dwconv(qkv, conv_w))
    q, k, v = (t.astype(jnp.float32).reshape(bsz, T, GDN_HEADS, GDN_HEAD_DIM)
               for t in jnp.split(qkv, 3, axis=-1))
    q = l2norm(q) * GDN_HEAD_DIM ** -0.5
    k = l2norm(k)
    g = -jnp.exp(A_log.astype(jnp.float32)) * jax.nn.softplus(a.astype(jnp.float32) + dt_bias.astype(jnp.float32))
    beta = jax.nn.sigmoid(b.astype(jnp.float32))
    o = gated_delta_rule_chunked(q, k, v, g, beta)
    gate = jax.nn.silu(z.astype(jnp.float32).reshape(bsz, T, GDN_HEADS, GDN_HEAD_DIM))
    o = rmsnorm(o, norm_g) * gate
    return o.reshape(bsz, T, GROUP_WIDTH).astype(u.dtype)


def ssd_chunked(x, dt, A, Bm, Cm):
    bsz, T, H, P = x.shape
    n = T // CHUNK
    hg = H // SSM_GROUPS
    X = (x.astype(jnp.float32) * dt[..., None]).reshape(bsz, n, CHUNK, SSM_GROUPS, hg, P)
    a = jnp.transpose((dt * A).reshape(bsz, n, CHUNK, SSM_GROUPS, hg), (0, 3, 4, 1, 2))
    Bm = Bm.astype(jnp.float32).reshape(bsz, n, CHUNK, SSM_GROUPS, SSM_STATE)
    Cm = Cm.astype(jnp.float32).reshape(bsz, n, CHUNK, SSM_GROUPS, SSM_STATE)
    Acs = jnp.cumsum(a, axis=-1)
    incl = jnp.tril(jnp.ones((CHUNK, CHUNK), dtype=bool))
    seg = Acs[..., :, None] - Acs[..., None, :]
    Lm = jnp.where(incl, jnp.exp(jnp.where(incl, seg, 0.0)), 0.0)
    CB = jnp.einsum('bnlgs,bnmgs->bgnlm', Cm, Bm)
    y_diag = jnp.einsum('bghnlm,bnmghp->bnlghp', CB[:, :, None] * Lm, X)
    to_end = jnp.exp(Acs[..., -1:] - Acs)
    states = jnp.einsum('bnlgs,bghnl,bnlghp->nbghps', Bm, to_end, X)
    blk_dec = jnp.moveaxis(jnp.exp(Acs[..., -1]), 3, 0)

    def step(h, inp):
        s_c, d_c = inp
        return h * d_c[..., None, None] + s_c, h

    h0 = jnp.zeros((bsz, SSM_GROUPS, hg, P, SSM_STATE), jnp.float32)
    _, h_prev = lax.scan(step, h0, (states, blk_dec))
    y_off = jnp.einsum('bnlgs,nbghps,bghnl->bnlghp', Cm, h_prev, jnp.exp(Acs))
    return (y_diag + y_off).reshape(bsz, T, H, P)


def mamba2_mixer(u, conv_w, conv_b, A_log, dt_bias, D_skip, norm_g):
    bsz, T, _ = u.shape
    z, xBC, dt = jnp.split(u, (GROUP_WIDTH, GROUP_WIDTH + SSM_XBC), axis=-1)
    xBC = jax.nn.silu(causal_dwconv(xBC, conv_w) + conv_b)
    x, Bm, Cm = jnp.split(xBC, (GROUP_WIDTH, GROUP_WIDTH + SSM_GROUPS * SSM_STATE), axis=-1)
    x = x.reshape(bsz, T, SSM_HEADS, SSM_HEAD_DIM)
    Bm = Bm.reshape(bsz, T, SSM_GROUPS, SSM_STATE)
    Cm = Cm.reshape(bsz, T, SSM_GROUPS, SSM_STATE)
    dt = jax.nn.softplus(dt.astype(jnp.float32) + dt_bias.astype(jnp.float32))
    A = -jnp.exp(A_log.astype(jnp.float32))
    y = ssd_chunked(x, dt, A, Bm, Cm) + D_skip.astype(jnp.float32)[:, None] * x.astype(jnp.float32)
    y = y.reshape(bsz, T, GROUP_WIDTH) * jax.nn.silu(z.astype(jnp.float32))
    y = rmsnorm(y.reshape(bsz, T, SSM_GROUPS, -1), norm_g.reshape(SSM_GROUPS, -1))
    return y.reshape(bsz, T, GROUP_WIDTH).astype(u.dtype)


def memory_cross_attention(h, m, wq, wkv, wo):
    bsz, T, _ = h.shape
    q = (h @ wq).reshape(bsz, T, XA_HEADS, XA_HEAD_DIM)
    k, v = jnp.split(m @ wkv, 2, axis=-1)
    k = k.reshape(bsz, m.shape[1], XA_HEADS, XA_HEAD_DIM)
    v = v.reshape(bsz, m.shape[1], XA_HEADS, XA_HEAD_DIM)
    s = jnp.einsum('bthd,bmhd->bhtm', q, k).astype(jnp.float32) * XA_HEAD_DIM ** -0.5
    p = jax.nn.softmax(s, axis=-1).astype(v.dtype)
    o = jnp.einsum('bhtm,bmhd->bthd', p, v).reshape(bsz, T, D_MODEL)
    return o @ wo


def swiglu_ffn(h, w_gu, w_down):
    gate, up = jnp.split(h @ w_gu, 2, axis=-1)
    return (jax.nn.silu(gate) * up) @ w_down


def setup_inputs(seed: int = 0) -> dict:
    key = jax.random.key(seed)
    ks = jax.random.split(key, 24)
    f32 = jnp.float32

    def dense(k, shape, fan_in):
        return jax.random.normal(k, shape, f32) * fan_in ** -0.5

    def gains(k, shape, noise):
        return 1.0 + noise * jax.random.normal(k, shape, f32)

    def log_a(k, n_heads):
        return jnp.log(jax.random.uniform(k, (DEPTH, n_heads), f32, 1.0, 16.0))

    def dt_bias(k, n_heads):
        dt = jnp.exp(jax.random.uniform(k, (DEPTH, n_heads), f32, np.log(1e-3), np.log(1e-1)))
        return dt + jnp.log(-jnp.expm1(-dt))

    return {
        'x': jax.random.normal(ks[0], (BATCH, SEQ, D_MODEL), f32),
        'mem': jax.random.normal(ks[1], (BATCH, MEM_LEN, D_MODEL), f32),
        'norm_g': gains(ks[2], (DEPTH, N_NORMS, D_MODEL), 0.05),
        'w_in': dense(ks[3], (DEPTH, D_MODEL, IN_COLS), D_MODEL),
        'conv_a_w': dense(ks[4], (DEPTH, CONV_A_WIDTH, GROUP_WIDTH), CONV_A_WIDTH),
        'pool_w': dense(ks[5], (DEPTH, len(POOL_WINDOWS), POOL_GROUP, POOL_GROUP), POOL_GROUP),
        'pool_scale': gains(ks[6], (DEPTH, GROUP_WIDTH), 0.1),
        'gdn_conv_w': dense(ks[7], (DEPTH, GDN_CONV, 3 * GROUP_WIDTH), GDN_CONV),
        'gdn_A_log': log_a(ks[8], GDN_HEADS),
        'gdn_dt_bias': dt_bias(ks[9], GDN_HEADS),
        'gdn_norm_g': gains(ks[10], (DEPTH, GDN_HEAD_DIM), 0.05),
        'ssm_conv_w': dense(ks[11], (DEPTH, SSM_CONV, SSM_XBC), SSM_CONV),
        'ssm_conv_b': 0.02 * jax.random.normal(ks[12], (DEPTH, SSM_XBC), f32),
        'ssm_A_log': log_a(ks[13], SSM_HEADS),
        'ssm_dt_bias': dt_bias(ks[14], SSM_HEADS),
        'ssm_D': gains(ks[15], (DEPTH, SSM_HEADS), 0.1),
        'ssm_norm_g': gains(ks[16], (DEPTH, GROUP_WIDTH), 0.05),
        'w_out': dense(ks[17], (DEPTH, MIX_WIDTH, D_MODEL), MIX_WIDTH),
        'xa_wq': dense(ks[18], (DEPTH, D_MODEL, D_MODEL), D_MODEL),
        'xa_wkv': dense(ks[19], (DEPTH, D_MODEL, 2 * D_MODEL), D_MODEL),
        'xa_wo': dense(ks[20], (DEPTH, D_MODEL, D_MODEL), D_MODEL),
        'ffn_w_gu': dense(ks[21], (DEPTH, D_MODEL, 2 * D_FF), D_MODEL),
        'ffn_w_down': dense(ks[22], (DEPTH, D_FF, D_MODEL), D_FF),
    }


def reference(x, mem, norm_g, w_in, conv_a_w, pool_w, pool_scale, gdn_conv_w, gdn_A_log,
              gdn_dt_bias, gdn_norm_g, ssm_conv_w, ssm_conv_b, ssm_A_log, ssm_dt_bias, ssm_D,
              ssm_norm_g, w_out, xa_wq, xa_wkv, xa_wo, ffn_w_gu, ffn_w_down):
    for l in range(DEPTH):
        g = norm_g[l]
        u = rmsnorm(x, g[MIX_PRE]) @ w_in[l]
        u_a, u_b, u_c, u_d = jnp.split(u, IN_SPLITS, axis=-1)
        mixed = jnp.concatenate([
            short_conv_mixer(u_a, conv_a_w[l]),
            multiscale_pool_mixer(u_b, pool_w[l], pool_scale[l]),
            gated_deltanet_mixer(u_c, gdn_conv_w[l], gdn_A_log[l], gdn_dt_bias[l], gdn_norm_g[l]),
            mamba2_mixer(u_d, ssm_conv_w[l], ssm_conv_b[l], ssm_A_log[l], ssm_dt_bias[l],
                         ssm_D[l], ssm_norm_g[l]),
        ], axis=-1).astype(x.dtype)
        x = x + rmsnorm(mixed @ w_out[l], g[MIX_POST])
        xa = memory_cross_attention(rmsnorm(x, g[XA_PRE]), rmsnorm(mem, g[MEM_NORM]),
                                    xa_wq[l], xa_wkv[l], xa_wo[l])
        x = x + rmsnorm(xa, g[XA_POST])
        x = x + rmsnorm(swiglu_ffn(rmsnorm(x, g[FFN_PRE]), ffn_w_gu[l], ffn_w_down[l]), g[FFN_POST])
    return x
```

```python
import contextlib
import numpy as np
import concourse.bass as bass
import concourse.mybir as mybir
from concourse.bass_utils import run_bass_kernel_spmd

F32 = mybir.dt.float32
BF16 = mybir.dt.bfloat16
ALU = mybir.AluOpType
AF = mybir.ActivationFunctionType

D = 2048
KC = 16
T = 1024
HALO = 16
TT = T + HALO
NT = 8
DFF = 5632
FKC = 44
MEM = 256
EPS = 1e-6
B0 = 16512
SBUF_END = 229376

A_B, A_C, A_H = 0, 512, 1024
B_U = 1536
C_QKV, C_Z, C_AB = 2048, 3584, 4096
D_Z, D_XBC, D_DT = 4104, 4616, 5640


class Prog:
    ENG = ('pe', 'act', 'dve', 'pool', 'sp')

    def __init__(self, nc):
        self.nc = nc
        self.ops = {e: [] for e in self.ENG}
        self.lastw = {}
        self.readers = {}
        self.stream_cnt = {}
        self.pool_streams = set()

    def op(self, eng, emit, reads=(), writes=(), stream=None, inc=16):
        if eng == 'pool' and stream is not None:
            self.pool_streams.add(stream)
        deps = []
        for k in reads:
            t = self.lastw.get(k)
            if t is not None:
                deps.append(t)
        for k in writes:
            t = self.lastw.get(k)
            if t is not None:
                deps.append(t)
            deps.extend(self.readers.get(k, ()))
        idx = len(self.ops[eng])
        if stream is not None:
            c = self.stream_cnt.get(stream, 0) + inc
            self.stream_cnt[stream] = c
            tok = ('s', stream, c)
        else:
            tok = ('e', eng, idx)
        waits = [d for d in deps if not (d[0] == 'e' and d[1] == eng and eng == 'pe')]
        self.ops[eng].append(dict(emit=emit, waits=waits, stream=stream, inc=inc))
        for k in writes:
            self.lastw[k] = tok
            self.readers[k] = []
        for k in reads:
            self.readers.setdefault(k, []).append(tok)
        return tok

    def barrier(self):
        toks = []
        for e in self.ENG:
            real = [i for i, o in enumerate(self.ops[e]) if o['emit'] is not None and o['stream'] is None]
            if real:
                toks.append(('e', e, real[-1]))
        for s, c in self.stream_cnt.items():
            if s not in self.pool_streams:
                toks.append(('s', s, c))
        for e in self.ENG:
            if e != 'pool':
                self.ops[e].append(dict(emit=None, waits=list(toks), stream=None, inc=0))
        keep = ("W", "snd", "rcv")
        self.lastw = {k: v for k, v in self.lastw.items() if k.startswith(keep)}
        self.readers = {k: v for k, v in self.readers.items() if k.startswith(keep)}

    def plan(self):
        needed = {e: set() for e in self.ENG}
        plan = {}
        for e in self.ENG:
            waited = {}
            out = []
            for o in self.ops[e]:
                best = {}
                for d in o['waits']:
                    key = (d[0], d[1])
                    if d[2] > best.get(key, -1):
                        best[key] = d[2]
                w = []
                for key, v in best.items():
                    if v > waited.get(key, -1):
                        waited[key] = v
                        w.append((key[0], key[1], v))
                        if key[0] == 'e':
                            needed[key[1]].add(v)
                out.append(w)
            plan[e] = out
        self.rank = {}
        for e in self.ENG:
            self.rank[e] = {i: c + 1 for c, i in enumerate(sorted(needed[e]))}
        self._plan = plan
        self.needed = needed

    def emit_engine(self, e, eng, sems, ssems):
        plan = self._plan[e]
        for i, o in enumerate(self.ops[e]):
            for (kind, src, v) in plan[i]:
                if kind == 'e':
                    eng.wait_ge(sems[src], self.rank[src][v])
                else:
                    eng.wait_ge(ssems[src], v)
            if o['emit'] is None:
                continue
            ins = o['emit'](eng)
            if o['stream'] is not None:
                ins.then_inc(ssems[o['stream']], o['inc'])
            elif i in self.needed[e]:
                ins.then_inc(sems[e], 1)


class Arena:
    def __init__(self, nc):
        self.nc = nc
        self.n = 0

    def at(self, name, shape, dtype, off):
        esz = 4 if dtype == F32 else 2
        size = esz
        for s in shape[1:]:
            size *= s
        assert B0 + off + size <= SBUF_END, (name, off, size)
        self.n += 1
        return self.nc.alloc_sbuf_tensor_at(f"{name}_{self.n}", list(shape), dtype, offset=B0 + off)


class Seq:
    def __init__(self, ar, lo, hi):
        self.ar, self.lo, self.hi, self.cur = ar, lo, hi, lo

    def a(self, name, shape, dtype):
        esz = 4 if dtype == F32 else 2
        size = esz
        for s in shape[1:]:
            size *= s
        size = (size + 63) // 64 * 64
        off = self.cur
        self.cur += size
        assert self.cur <= self.hi, (name, self.cur, self.hi)
        if not hasattr(self, 'off'):
            self.off = {}
        self.off[name] = off
        return self.ar.at(name, shape, dtype, off)


def build(n_layers=2, debug=False):
    nc = bass.Bass("TRN2", target_bir_lowering=False)
    p = Prog(nc)
    ar = Arena(nc)

    def din(name, shape):
        return nc.dram_tensor(name, list(shape), F32, kind="ExternalInput").ap()

    xT_in = din("xT", [D, T])
    xh_in = din("xh", [D, HALO])
    memT_in = din("memT", [D, MEM])
    cf_in = din("cf", [128, 512])
    rowp_in = din("rowp", [128, 2 * 672])
    colp_in = din("colp", [128, 2 * 216])
    misc_in = din("misc", [128, 80])
    w_in_fm = din("w_in_fm", [2, 36, 128, KC, 128])
    w_in_tm = din("w_in_tm", [2, D, 1040])
    pool_w = din("pool_w", [2, 4, 128, 128])
    w_out = din("w_out_t", [2, KC, 128, KC, 128])
    xa_wq = din("xa_wq_t", [2, KC, 128, KC, 128])
    xa_wk = din("xa_wk_t", [2, KC, 128, KC, 128])
    xa_wv = din("xa_wv_t", [2, KC, 128, KC, 128])
    xa_wo = din("xa_wo_t", [2, KC, 128, KC, 128])
    w_gu = din("ffn_w_gu_t", [2, 2 * FKC, 128, KC, 128])
    w_down = din("ffn_w_down_t", [2, KC, 128, FKC, 128])
    out = nc.dram_tensor("out", [D, T], F32, kind="ExternalOutput").ap()
    xs = nc.dram_tensor("xs", [D, T], F32).ap()
    snd = [nc.dram_tensor(f"snd{i}", [128, 512 if i < 4 else 256], F32) for i in range(5)]
    rcv = [nc.dram_tensor(f"rcv{i}", [256, 512 if i < 4 else 256], F32) for i in range(5)]
    dbg = {}
    if debug:
        dbg['mixed'] = nc.dram_tensor("dbg_mixed", [D, T], F32, kind="ExternalOutput").ap()

    P = Seq(ar, 0, 34304)
    CF = P.a("cf", [128, 512], F32)
    IDF, ONESF, UF, SLF = CF[:, 0:128], CF[:, 128:256], CF[:, 256:384], CF[:, 384:512]
    CBF = P.a("cbf", [128, 256], BF16)
    IDB, ONESB = CBF[:, 0:128], CBF[:, 128:256]
    ROWP = P.a("rowp", [128, 1344], F32)
    COLP = P.a("colp", [128, 432], F32)
    MISC = P.a("misc", [128, 80], F32)
    FLAG = MISC[:, 0:1]
    EPSC = P.a("epsc", [128, 4], F32)
    GS = P.a("gs", [128, 224], F32)
    NSLOT = 5
    WS = [P.a(f"wslot{i}", [128, 2048], BF16) for i in range(NSLOT)]
    PEND = P.cur
    ps = [nc.alloc_psum_tensor(f"ps{i}", [128, 512], F32) for i in range(8)]
    st = dict(bank=0, wslot=0, ev=0)

    def bank():
        b = st['bank']
        st['bank'] = (b + 1) % 8
        return b

    def wslot():
        s = st['wslot']
        st['wslot'] = (s + 1) % NSLOT
        return s

    def mm(o, lhsT, rhs, start, stop, r, w):
        p.op('pe', lambda e: e.matmul(o, lhsT=lhsT, rhs=rhs, start=start, stop=stop), reads=r, writes=w)

    def tr(o, i, ident, r, w):
        p.op('pe', lambda e: e.transpose(o, i, ident), reads=r, writes=w)

    def act(o, i, func, r, w, bias=None, scale=None, accum=None):
        kw = {}
        if bias is not None:
            kw['bias'] = bias
        if scale is not None:
            kw['scale'] = scale
        if accum is not None:
            kw['accum_out'] = accum
        p.op('act', lambda e: e.activation(out=o, in_=i, func=func, **kw), reads=r, writes=w)

    def tt(o, a, b, op, r, w, eng='dve'):
        p.op(eng, lambda e: e.tensor_tensor(out=o, in0=a, in1=b, op=op), reads=r, writes=w)

    def ts(o, a, s1, s2, op0, op1, r, w, eng='dve'):
        if op1 is None:
            p.op(eng, lambda e: e.tensor_scalar(out=o, in0=a, scalar1=s1, scalar2=None, op0=op0), reads=r, writes=w)
        else:
            p.op(eng, lambda e: e.tensor_scalar(out=o, in0=a, scalar1=s1, scalar2=s2, op0=op0, op1=op1), reads=r, writes=w)

    def stt(o, a, s, b, op0, op1, r, w, eng='dve'):
        p.op(eng, lambda e: e.scalar_tensor_tensor(out=o, in0=a, scalar=s, in1=b, op0=op0, op1=op1), reads=r, writes=w)

    def cp(o, i, r, w, eng=None):
        if eng is None:
            st['ev'] ^= 1
            eng = 'act' if st['ev'] else 'dve'
        if eng == 'act':
            act(o, i, AF.Copy, r, w)
        else:
            p.op('dve', lambda e: e.tensor_copy(out=o, in_=i), reads=r, writes=w)

    def recip(o, i, r, w):
        p.op('dve', lambda e: e.reciprocal(out=o, in_=i), reads=r, writes=w)

    def memset(o, v, w):
        p.op('dve', lambda e: e.memset(o, v), writes=w)

    def dma(o, i, stream, r, w, q='sp'):
        p.op(q, lambda e: e.dma_start(out=o, in_=i), reads=r, writes=w, stream=stream)

    def exchange(i, r, w):
        p.op('pool', lambda e: e.collective_compute("AllGather", ALU.bypass,
                                                     replica_groups=[[0, 1], [2, 3], [4, 5], [6, 7]],
                                                     ins=[snd[i].ap()], outs=[rcv[i].ap()]),
             reads=r, writes=w, stream=f"cc{i}", inc=1)

    dma(CF[:, :], cf_in, "c0", [], ["CF"])
    dma(ROWP[:, :], rowp_in, "c1", [], ["ROWP"])
    dma(COLP[:, :], colp_in, "c2", [], ["COLP"])
    dma(MISC[:, :], misc_in, "c3", [], ["MISC"])
    cp(CBF[:, :], CF[:, 0:256], ["CF"], ["CBF"], eng='dve')
    memset(EPSC[:, 0:1], EPS, ["EPSC"])
    memset(EPSC[:, 1:2], D * EPS, ["EPSC1"])
    for l in range(2):
        ts(GS[:, l * 112:(l + 1) * 112], COLP[:, l * 216:l * 216 + 112], float(np.sqrt(D)), None, ALU.mult, None, ["COLP"], [f"GS{l}"])

    def gcol(l, row, kc):
        return GS[:, l * 112 + row * 16 + kc: l * 112 + row * 16 + kc + 1]

    def colp(l, off, n=1):
        return COLP[:, l * 216 + off: l * 216 + off + n]

    def rowp(l, off, n):
        return ROWP[:, l * 672 + off: l * 672 + off + n]

    def load_unit(src_ap, kcn, ncols=128):
        s = wslot()
        view = WS[s][:, 0:kcn * ncols].rearrange("p (k n) -> p k n", k=kcn)
        p.op('pool', lambda e: e.dma_start(out=view, in_=src_ap), reads=[], writes=[f"W{s}"], stream=f"w{s}")
        return view, f"W{s}"

    def linear_fm_gen(Wt, tiles, kcn, src, src_keys, blocks, consume, every=8):
        cnt = 0
        for i, nt in enumerate(tiles):
            units = []
            for k0 in range(0, kcn, 16):
                kn = min(16, kcn - k0)
                view, key = load_unit(Wt[nt][:, k0:k0 + kn, :], kn)
                units.append((k0, kn, view, key))
            for bi, (t0, n) in enumerate(blocks):
                b = bank()
                for (k0, kn, view, key) in units:
                    for k in range(kn):
                        mm(ps[b][:, 0:n], view[:, k, :], src(k0 + k, t0, n), (k0 + k) == 0, (k0 + k) == kcn - 1,
                           [key] + (src_keys(bi) if callable(src_keys) else src_keys), [f"ps{b}"])
                        cnt += 1
                        if cnt % every == 0:
                            yield
                consume(i, bi, ps[b][:, 0:n], f"ps{b}")

    def linear_fm(*a, **kw):
        for _ in linear_fm_gen(*a, **kw):
            pass

    def interleaved_gen(gens):
        gens = list(gens)
        while gens:
            nxt_ = []
            for g_ in gens:
                try:
                    next(g_)
                    nxt_.append(g_)
                except StopIteration:
                    pass
            gens = nxt_
            yield

    def run_interleaved(gens):
        for _ in interleaved_gen(gens):
            pass

    def linear_tm_gen(W2d, c0, ncols, src, src_keys, tiles, consume, every=8):
        view, key = load_unit(W2d.rearrange("(k p) n -> p k n", p=128)[:, :, c0:c0 + ncols], KC, ncols)
        cnt = 0
        for t in tiles:
            b = bank()
            for k in range(KC):
                mm(ps[b][:, 0:ncols], src(k, t), view[:, k, :], k == 0, k == KC - 1, [key] + src_keys, [f"ps{b}"])
                cnt += 1
                if cnt % every == 0:
                    yield
            consume(t, ps[b][:, 0:ncols], f"ps{b}")

    def linear_tm(*a, **kw):
        for _ in linear_tm_gen(*a, **kw):
            pass

    def delay_gen(gen, n):
        for _ in range(n):
            yield
        yield from gen

    def sumsq_rstd(srcf, nkc, n, sqb, rstd, tmp, key, div_eps_scale=True):
        b = bank()
        for k in range(nkc):
            sq = sqb[k % len(sqb)]
            act(sq[:, 0:n], srcf(k), AF.Square, [key], [f"sq{id(sq)}"])
            mm(ps[b][:, 0:n], ONESB, sq[:, 0:n], k == 0, k == nkc - 1, ["CBF", f"sq{id(sq)}"], [f"ps{b}"])
        act(tmp[:, 0:n], ps[b][:, 0:n], AF.Ln, [f"ps{b}", "EPSC1"], [f"t{id(tmp)}"], bias=EPSC[:, 1:2], scale=1.0)
        act(rstd[:, 0:n], tmp[:, 0:n], AF.Exp, [f"t{id(tmp)}"], [f"r{id(rstd)}"], scale=-0.5)
        return f"r{id(rstd)}"

    M_XN = PEND
    M_MIX = M_XN + 33280
    M_SCR = M_MIX + 32768
    SCR_END = SBUF_END - B0
    xn = ar.at("xn", [128, KC, TT], BF16, M_XN)
    mixed = ar.at("mixed", [128, KC, T], BF16, M_MIX)
    XNK = [f"xn{i}" for i in range(5)]
    BLK3 = [(0, HALO), (HALO, 512), (HALO + 512, 512)]

    def XNKB(bi):
        return [["xn0"], ["xn1", "xn2"], ["xn3", "xn4"]][bi]

    def xsrc(k, t0, n):
        return xn[:, k, t0:t0 + n]

    def bc(ap, axis, shape):
        return ap.unsqueeze(axis).to_broadcast(list(shape))

    def conv_fm(raw, rkey, wcol0, ntap, bias, cacc, dst, func, dkey):
        ck = f"cacc{id(cacc)}"
        if bias is None:
            act(cacc[:, :], raw[:, HALO:TT], AF.Identity, rkey + ["COLP"], [ck], scale=COLP[:, wcol0 + ntap - 1: wcol0 + ntap])
        else:
            act(cacc[:, :], raw[:, HALO:TT], AF.Identity, rkey + ["COLP"], [ck], scale=COLP[:, wcol0 + ntap - 1: wcol0 + ntap], bias=bias)
        for k in range(ntap - 1):
            sh = HALO - (ntap - 1) + k
            stt(cacc[:, :], raw[:, sh:sh + T], COLP[:, wcol0 + k: wcol0 + k + 1], cacc[:, :], ALU.mult, ALU.add, rkey + [ck, "COLP"], [ck])
        if func is not None:
            act(dst, cacc[:, :], func, [ck], dkey)
        return ck

    def softplus_neg_scaled(dst, src, brow, arow, n_t, nh, tmp, keys_r, key_w):
        tt(tmp, src, bc(brow, 1, [128, n_t, nh]), ALU.add, keys_r + ["ROWP"], [key_w + "_t"])
        act(tmp, tmp, AF.Exp, [key_w + "_t"], [key_w + "_t"])
        act(dst, tmp, AF.Ln, [key_w + "_t"], [key_w], bias=1.0)

    def layer(l, x_src, halo_from_rcv, x_dst):
        Wl = w_in_fm[l]
        Wtm = w_in_tm[l]
        LC = l * 216
        p.barrier()
        S = Seq(ar, M_SCR, SCR_END)
        xstb = [S.a(f"xst{i}", [128, KC, 256], F32) for i in range(2)]
        sqb = [S.a(f"sqb{i}", [128, 512], BF16) for i in range(4)]
        rtmpb = [S.a(f"rtmp{i}", [128, 256], F32) for i in range(2)]
        rstdb = [S.a(f"rstd{i}", [128, 256], F32) for i in range(2)]
        xv = x_src.rearrange("(k p) t -> p k t", p=128)
        NB1 = [(0, HALO)] + [(HALO + 256 * i, 256) for i in range(4)]
        for bi, (t0, n) in enumerate(NB1):
            xst = xstb[bi % 2]
            xk = f"xst{bi % 2}"
            if bi == 0:
                if halo_from_rcv is None:
                    dma(xst[:, :, 0:HALO], xh_in.rearrange("(k p) t -> p k t", p=128), xk, [], [xk])
                else:
                    dma(xst[:, :, 0:HALO], rcv[halo_from_rcv].ap()[0:128, 0:256].rearrange("p (k t) -> p k t", k=KC), xk,
                        [f"rcv{halo_from_rcv}"], [xk])
                    ts(xst[:, :, 0:HALO], xst[:, :, 0:HALO], FLAG, None, ALU.mult, None, [xk, "MISC"], [xk])
            else:
                dma(xst[:, :, :], xv[:, :, (bi - 1) * 256: bi * 256], xk, [], [xk])
            rk = sumsq_rstd(lambda k: xst[:, k, 0:n], KC, n, sqb[2 * (bi % 2): 2 * (bi % 2) + 2], rstdb[bi % 2], rtmpb[bi % 2], xk)
            for k in range(KC):
                stt(xn[:, k, t0:t0 + n], xst[:, k, 0:n], gcol(l, 0, k), rstdb[bi % 2][:, 0:n], ALU.mult, ALU.mult,
                    [xk, rk, f"GS{l}"], [f"xn{bi}"])

        p.barrier()
        S = Seq(ar, M_SCR, SCR_END)
        craw = [S.a(f"craw{i}", [128, TT], F32) for i in range(2)]
        cacc = S.a("cacc", [128, T], F32)
        xc = S.a("xc", [128, 4, T], BF16)
        BT = S.a("BT", [128, 2, T], BF16)
        CT = S.a("CT", [128, 2, T], BF16)
        x_tm = S.a("x_tm", [128, NT, 512], BF16)
        B_tm = S.a("B_tm", [128, NT, 256], BF16)
        smD = S.a("smD", [128, NT, 8], F32)
        sm_t = S.a("sm_t", [128, NT, 8], F32)
        dtv = S.a("dtv", [128, NT, 8], F32)
        av = S.a("av", [128, NT, 8], F32)
        Arow = S.a("Arow", [128, 8], F32)
        eA = S.a("eA", [128, NT, 8], F32)
        dA = S.a("dA", [128, NT, 8], F32)
        statesT = S.a("statesT", [128, NT, 512], F32)
        zsD = S.a("zsD", [128, NT, 512], F32)
        DI = S.a("DI", [128, 8, 128], F32)
        hst = S.a("hst", [128, 512], F32)
        hb = S.a("hb", [128, 512], BF16)
        acst = S.a("acst", [128, 16], F32)
        te = S.a("te", [128, 8], F32)
        dec = S.a("dec", [128, 8], F32)
        Xdec = S.a("Xdec", [128, 512], BF16)
        gtri = S.a("gtri", [128, 8, 128], F32)
        E = S.a("E", [128, 8, 128], F32)
        CBTm = S.a("CBTm", [128, 2, 128], F32)
        WTb = S.a("WTb", [128, 8, 128], BF16)
        t1 = S.a("t1", [128, 512], F32)
        yv = S.a("yv", [128, 512], F32)
        yb = S.a("yb", [128, 512], BF16)
        ssq = S.a("ssq", [128, 4], F32)
        rs = S.a("rs", [128, 4], F32)

        def cons_xbc(i, bi, pa, pk):
            t0, n = BLK3[bi]
            r = craw[i % 2]
            cp(r[:, t0:t0 + n], pa, [pk], [f"craw{i % 2}_{bi}"])
            if bi == 2:
                dst, dk = (xc[:, i, :], "xc") if i < 4 else ((BT[:, i - 4, :], "BT") if i < 6 else (CT[:, i - 6, :], "CT"))
                conv_fm(r, [f"craw{i % 2}_{b_}" for b_ in range(3)], LC + 176 + i * 4, 4,
                        COLP[:, LC + 208 + i: LC + 209 + i], cacc, dst, AF.Silu, [dk + str(i)])
        linear_fm(Wl, list(range(0, 8)), KC, xsrc, XNKB, BLK3, cons_xbc)
        xck = [f"xc{i}" for i in range(4)]
        BTk = ["BT4", "BT5"]
        CTk = ["CT6", "CT7"]
        for t in range(NT):
            b = bank()
            psb = ps[b][:, :].bitcast(BF16)
            for i in range(4):
                tr(psb[:, i * 128:(i + 1) * 128], xc[:, i, t * 128:(t + 1) * 128], IDB, xck + ["CBF"], [f"ps{b}"])
            cp(x_tm[:, t, :], psb[:, 0:512], [f"ps{b}"], [f"x_tm{t}"])
            b = bank()
            psb = ps[b][:, :].bitcast(BF16)
            for g in range(2):
                tr(psb[:, g * 128:(g + 1) * 128], BT[:, g, t * 128:(t + 1) * 128], IDB, BTk + ["CBF"], [f"ps{b}"])
            cp(B_tm[:, t, :], psb[:, 0:256], [f"ps{b}"], [f"B_tm{t}"])
        linear_tm(Wtm, 1024, 8, lambda k, t: xn[:, k, HALO + t * 128: HALO + (t + 1) * 128], XNK, range(NT),
                  lambda t, pa, pk: cp(smD[:, t, :], pa, [pk], ["smD"]))
        softplus_neg_scaled(dtv[:, :, :], smD[:, :, :], rowp(l, 16, 8), None, NT, 8, sm_t[:, :, :], ["smD"], "dtv")
        act(Arow[:, :], rowp(l, 8, 8), AF.Exp, ["ROWP"], ["Arow"])
        ts(Arow[:, :], Arow[:, :], -1.0, None, ALU.mult, None, ["Arow"], ["Arow"])
        tt(av[:, :, :], dtv[:, :, :], bc(Arow[:, :], 1, [128, NT, 8]), ALU.mult, ["dtv", "Arow"], ["av"])
        tt(DI[:, :, :], bc(IDF, 1, [128, 8, 128]), bc(rowp(l, 24, 8), 2, [128, 8, 128]), ALU.mult, ["CF", "ROWP"], ["DI"])
        def zd_thread():
            for hf in range(4):
                yield from linear_tm_gen(Wtm, hf * 128, 128, lambda k, t: xn[:, k, HALO + t * 128: HALO + (t + 1) * 128], XNK, range(NT),
                                         lambda t, pa, pk, hf=hf: act(zsD[:, t, hf * 128:(hf + 1) * 128], pa, AF.Silu, [pk], [f"zsD{t}_{hf}"]))

        def states_thread():
            for t in range(NT):
                b = bank()
                mm(ps[b][:, 0:8], UF, av[:, t, :], True, True, ["CF", "av"], [f"ps{b}"])
                mm(ps[b][:, 8:16], ONESF, av[:, t, :], True, True, ["CF", "av"], [f"ps{b}"])
                yield
                cp(acst[:, :], ps[b][:, 0:16], [f"ps{b}"], ["acst"], eng='dve')
                yield
                act(eA[:, t, :], acst[:, 0:8], AF.Exp, ["acst"], [f"eA{t}"])
                act(dA[:, t, :], acst[:, 8:16], AF.Exp, ["acst"], [f"dA{t}"])
                tt(te[:, :], acst[:, 8:16], acst[:, 0:8], ALU.subtract, ["acst"], ["te"])
                yield
                act(te[:, :], te[:, :], AF.Exp, ["te"], ["te"])
                yield
                tt(dec[:, :], te[:, :], dtv[:, t, :], ALU.mult, ["te", "dtv"], ["dec"])
                yield
                tt(Xdec[:, :].rearrange("p (h q) -> p h q", h=8), x_tm[:, t, :].rearrange("p (h q) -> p h q", h=8),
                   bc(dec[:, :], 2, [128, 8, 64]), ALU.mult, ["dec", f"x_tm{t}"], ["Xdec"])
                yield
                b = bank()
                for g in range(2):
                    mm(ps[b][:, g * 256:(g + 1) * 256], B_tm[:, t, g * 128:(g + 1) * 128], Xdec[:, g * 256:(g + 1) * 256], True, True,
                       [f"B_tm{t}", "Xdec"], [f"ps{b}"])
                yield
                cp(statesT[:, t, :], ps[b][:, :], [f"ps{b}"], [f"st{t}"])
        run_interleaved([states_thread(), zd_thread()])

        def h_step(t):
            tt(hst[:, :].rearrange("p (h q) -> p h q", h=8), hst[:, :].rearrange("p (h q) -> p h q", h=8),
               bc(dA[:, t, :], 2, [128, 8, 64]), ALU.mult, ["hst", f"dA{t}"], ["hst"])
            tt(hst[:, :], hst[:, :], statesT[:, t, :], ALU.add, ["hst", f"st{t}"], ["hst"])
        memset(hst[:, :], 0.0, ["hst"])
        for t in range(NT):
            h_step(t)
        ex = 2 * l
        dma(snd[ex].ap()[:, :], hst[:, :], f"snd{ex}", ["hst"], [f"snd{ex}"])
        exchange(ex, [f"snd{ex}"], [f"rcv{ex}"])

        SA = Seq(ar, S.cur, SCR_END)
        ra = [SA.a(f"ra{i}", [128, TT], F32) for i in range(3)]
        p.barrier()
        R1 = Seq(ar, S.off["craw0"], S.off["craw0"] + 8320)
        R2 = Seq(ar, S.off["xc"], S.off["xc"] + 8192)
        R3 = Seq(ar, S.off["B_tm"], S.off["B_tm"] + 4096)
        TS = [dict(gtri=gtri, E=E, CBTm=CBTm, WTb=WTb, t1=t1, yv=yv, yb=yb, hb=hb, ssq=ssq, rs=rs),
              dict(gtri=R1.a("gtri1", [128, 8, 128], F32), E=R1.a("E1", [128, 8, 128], F32),
                   CBTm=R2.a("CBTm1", [128, 2, 128], F32), WTb=R2.a("WTb1", [128, 8, 128], BF16), t1=R2.a("t11", [128, 512], F32),
                   yv=R2.a("yv1", [128, 512], F32), yb=R2.a("yb1", [128, 512], BF16), hb=R3.a("hb1", [128, 512], BF16),
                   ssq=R3.a("ssq1", [128, 4], F32), rs=R3.a("rs1", [128, 4], F32))]

        def mixer_a():
            for j in range(4):
                def cons_a(i, bi, pa, pk, j=j):
                    t0, n = BLK3[bi]
                    cp(ra[i][:, t0:t0 + n], pa, [pk], [f"ra{i}_{bi}"])
                yield from linear_fm_gen(Wl, [20 + 3 * j, 21 + 3 * j, 22 + 3 * j], KC, xsrc, XNKB, BLK3, cons_a)
                rk = [f"ra{i}_{b_}" for i in range(3) for b_ in range(3)]
                tt(ra[0][:, :], ra[0][:, :], ra[1][:, :], ALU.mult, rk, ["ra0_0", "ra0_1", "ra0_2"])
                yield
                ck = conv_fm(ra[0], rk, LC + 112 + j * 3, 3, None, cacc, None, None, None)
                tt(mixed[:, j, :], cacc[:, :], ra[2][:, HALO:TT], ALU.mult, [ck] + rk, [f"mixed{j}"])
                yield

        def ssd_tile(t):
            f_ = t % 2
            Bf = TS[f_]
            gtri, E, CBTm, WTb, t1, yv, yb, hb, ssq, rs = (Bf[k_] for k_ in ("gtri", "E", "CBTm", "WTb", "t1", "yv", "yb", "hb", "ssq", "rs"))
            K = lambda n_: f"{n_}#{f_}"
            tsl = slice(t * 128, (t + 1) * 128)
            tt(gtri[:, :, :], bc(UF, 1, [128, 8, 128]), bc(av[:, t, :], 2, [128, 8, 128]), ALU.mult, ["CF", "av"], [K("gtri")])
            b = bank()
            for g in range(2):
                mm(ps[b][:, g * 128:(g + 1) * 128], BT[:, g, tsl], CT[:, g, tsl], True, True, BTk + CTk, [f"ps{b}"])
            yield
            tt(CBTm[:, :, :], ps[b][:, 0:256].rearrange("p (g l) -> p g l", g=2), bc(UF, 1, [128, 2, 128]), ALU.mult,
               [f"ps{b}", "CF"], [K("CBTm")])
            bq_ = []
            for q in range(2):
                b = bank()
                bq_.append(b)
                mm(ps[b][:, :], SLF, gtri[:, 4 * q:4 * q + 4, :].rearrange("p h l -> p (h l)"), True, True, ["CF", K("gtri")], [f"ps{b}"])
            yield
            for q in range(2):
                act(E[:, 4 * q:4 * q + 4, :].rearrange("p h l -> p (h l)"), ps[bq_[q]][:, :], AF.Exp, [f"ps{bq_[q]}"], [K(f"E{q}")])
            yield
            for g in range(2):
                tt(E[:, 4 * g:4 * g + 4, :], E[:, 4 * g:4 * g + 4, :], bc(CBTm[:, g, :], 1, [128, 4, 128]), ALU.mult,
                   [K(f"E{g}"), K("CBTm")], [K(f"E{g}")])
            yield
            tt(E[:, :, :], E[:, :, :], bc(dtv[:, t, :], 2, [128, 8, 128]), ALU.mult, [K("E0"), K("E1"), "dtv"], [K("E0"), K("E1")])
            yield
            tt(WTb[:, :, :], E[:, :, :], DI[:, :, :], ALU.add, [K("E0"), K("E1"), "DI"], [K("WTb")])
            yield
            by = bank()
            for h in range(8):
                mm(ps[by][:, h * 64:(h + 1) * 64], WTb[:, h, :], x_tm[:, t, h * 64:(h + 1) * 64], True, True, [K("WTb"), f"x_tm{t}"], [f"ps{by}"])
            if t == 0:
                dma(hst[:, :], rcv[ex].ap()[0:128, :], "hst", [f"rcv{ex}"], ["hst"])
                ts(hst[:, :], hst[:, :], FLAG, None, ALU.mult, None, ["hst", "MISC"], ["hst"])
            cp(hb[:, :], hst[:, :], ["hst"], [K("hb")], eng='act')
            h_step(t)
            yield
            bo = bank()
            for g in range(2):
                mm(ps[bo][:, g * 256:(g + 1) * 256], CT[:, g, tsl], hb[:, g * 256:(g + 1) * 256], True, True, CTk + [K("hb")], [f"ps{bo}"])
            yield
            tt(t1[:, :].rearrange("p (h q) -> p h q", h=8), ps[bo][:, :].rearrange("p (h q) -> p h q", h=8),
               bc(eA[:, t, :], 2, [128, 8, 64]), ALU.mult, [f"ps{bo}", f"eA{t}"], [K("t1")])
            yield
            tt(yv[:, :], ps[by][:, :], t1[:, :], ALU.add, [f"ps{by}", K("t1")], [K("yv")])
            yield
            tt(yv[:, :], yv[:, :], zsD[:, t, :], ALU.mult, [K("yv")] + [f"zsD{t}_{hf}" for hf in range(4)], [K("yv")])
            memset(ssq[:, :], 0.0, [K("ssq")])
            yield
            for g in range(2):
                act(t1[:, 0:256], yv[:, g * 256:(g + 1) * 256], AF.Square, [K("yv"), K("ssq")], [K("t1"), K("ssq")], accum=ssq[:, g:g + 1])
            yield
            act(rs[:, 0:2], ssq[:, 0:2], AF.Sqrt, [K("ssq"), "EPSC"], [K("rs")], bias=EPSC[:, 0:1], scale=1.0 / 256)
            yield
            recip(rs[:, 0:2], rs[:, 0:2], [K("rs")], [K("rs")])
            yield
            for g in range(2):
                stt(yb[:, g * 256:(g + 1) * 256], yv[:, g * 256:(g + 1) * 256], rs[:, g:g + 1], rowp(l, 160 + g * 256, 256),
                    ALU.mult, ALU.mult, [K("yv"), K("rs"), "ROWP"], [K("yb")])
            yield
            b = bank()
            psb = ps[b][:, :].bitcast(BF16)
            for c in range(4):
                tr(psb[:, c * 128:(c + 1) * 128], yb[:, c * 128:(c + 1) * 128], IDB, [K("yb"), "CBF"], [f"ps{b}"])
            yield
            cp(mixed[:, 12:16, tsl], psb[:, 0:512].rearrange("p (c n) -> p c n", c=4), [f"ps{b}"], [f"mixedD{t}"])

        def ssd_pairs():
            for t0_ in range(0, NT, 2):
                yield from interleaved_gen([ssd_tile(t0_), ssd_tile(t0_ + 1)])
        run_interleaved([ssd_pairs(), mixer_a()])

        p.barrier()
        G2 = Seq(ar, M_SCR, M_SCR + 50688)
        G1 = Seq(ar, M_SCR + 50688, SCR_END)
        qT = G2.a("qT", [128, 4, T], BF16)
        vb = G2.a("vb", [128, NT, 512], BF16)
        TTb = G2.a("TTb", [128, NT, 512], BF16)
        wT = G2.a("wT", [128, NT, 512], BF16)
        qkTm = G2.a("qkTm", [128, NT, 512], BF16)
        k_end = G2.a("k_end", [128, NT, 512], BF16)
        eG = G2.a("eG", [128, NT, 4], F32)
        dG = G2.a("dG", [128, NT, 4], F32)
        smC = G2.a("smC", [128, NT, 8], F32)
        smt = G2.a("smt", [128, NT, 4], F32)
        gv = G2.a("gv", [128, NT, 4], F32)
        beta = G2.a("beta", [128, NT, 4], F32)
        gArow = G2.a("gArow", [128, 4], F32)
        gt = G2.a("gt", [128, 8], F32)
        ke = G2.a("ke", [128, 4], F32)
        bg = G2.a("bg", [128, 4], F32)
        kT = G1.a("kT", [128, 4, T], BF16)
        k_tm = G1.a("k_tm", [128, NT, 512], BF16)
        G1_TMP = G1.cur
        craw = [G1.a(f"gcraw{i}", [128, TT], F32) for i in range(2)]
        caccs = [G1.a(f"gcacc{i}", [128, T], F32) for i in range(2)]
        qfs = [G1.a(f"qf{i}", [128, T], F32) for i in range(2)]
        vT = G1.a("vT", [128, 4, T], BF16)
        sqg = [G1.a(f"sqg{i}", [128, 512], BF16) for i in range(2)]
        rtgs = [G1.a(f"rtg{i}", [128, 512], F32) for i in range(2)]
        rsgs = [G1.a(f"rsg{i}", [128, 512], F32) for i in range(2)]

        pend = []

        def flush_pend():
            while pend:
                pend.pop(0)()

        def cons_qkv(i, bi, pa, pk):
            t0, n = BLK3[bi]
            r = craw[i % 2]
            cacc = caccs[i % 2]
            qf = qfs[i % 2]
            qk_ = f"qf{i % 2}"
            cp(r[:, t0:t0 + n], pa, [pk], [f"gcraw{i % 2}_{bi}"])
            if bi != 2:
                return
            flush_pend()
            rk = [f"gcraw{i % 2}_{b_}" for b_ in range(3)]
            h = i % 4
            if i >= 8:
                conv_fm(r, rk, LC + 128 + i * 4, 4, None, cacc, vT[:, h, :], AF.Silu, [f"vT{h}"])
                return
            conv_fm(r, rk, LC + 128 + i * 4, 4, None, cacc, qf[:, :], AF.Silu, [qk_])
            for hb_ in range(2):
                sl = slice(hb_ * 512, (hb_ + 1) * 512)
                act(sqg[hb_][:, :], qf[:, sl], AF.Square, [qk_], [f"sqg{hb_}"])

            def l2n(i=i, h=h, qf=qf, qk_=qk_):
                for hb_ in range(2):
                    sl = slice(hb_ * 512, (hb_ + 1) * 512)
                    sq = sqg[hb_]
                    rtg, rsg = rtgs[hb_], rsgs[hb_]
                    b = bank()
                    mm(ps[b][:, :], ONESB, sq[:, :], True, True, ["CBF", f"sqg{hb_}"], [f"ps{b}"])
                    act(rtg[:, :], ps[b][:, :], AF.Ln, [f"ps{b}", "EPSC"], [f"rtg{hb_}"], bias=EPSC[:, 0:1], scale=1.0)
                    act(rsg[:, :], rtg[:, :], AF.Exp, [f"rtg{hb_}"], [f"rsg{hb_}"], scale=-0.5)
                    if i < 4:
                        stt(qT[:, h, sl], qf[:, sl], float(128 ** -0.5), rsg[:, :], ALU.mult, ALU.mult, [qk_, f"rsg{hb_}"], [f"qT{h}"])
                    else:
                        tt(kT[:, h, sl], qf[:, sl], rsg[:, :], ALU.mult, [qk_, f"rsg{hb_}"], [f"kT{h}"])
            pend.append(l2n)
        linear_fm(Wl, list(range(8, 20)), KC, xsrc, XNKB, BLK3, cons_qkv)
        flush_pend()
        qTk = [f"qT{h}" for h in range(4)]
        kTk = [f"kT{h}" for h in range(4)]
        vTk = [f"vT{h}" for h in range(4)]
        for t in range(NT):
            for (src_, sk, dst_, dk) in ((kT, kTk, k_tm, "k_tm"), (vT, vTk, vb, "vb")):
                b = bank()
                psb = ps[b][:, :].bitcast(BF16)
                for h in range(4):
                    tr(psb[:, h * 128:(h + 1) * 128], src_[:, h, t * 128:(t + 1) * 128], IDB, sk + ["CBF"], [f"ps{b}"])
                cp(dst_[:, t, :], psb[:, 0:512], [f"ps{b}"], [f"{dk}{t}"])
        linear_tm(Wtm, 1032, 8, lambda k, t: xn[:, k, HALO + t * 128: HALO + (t + 1) * 128], XNK, range(NT),
                  lambda t, pa, pk: cp(smC[:, t, :], pa, [pk], ["smC"]))
        softplus_neg_scaled(gv[:, :, :], smC[:, :, 0:4], rowp(l, 4, 4), None, NT, 4, smt[:, :, :], ["smC"], "gv")
        act(gArow[:, :], rowp(l, 0, 4), AF.Exp, ["ROWP"], ["gArow"])
        ts(gArow[:, :], gArow[:, :], -1.0, None, ALU.mult, None, ["gArow"], ["gArow"])
        tt(gv[:, :, :], gv[:, :, :], bc(gArow[:, :], 1, [128, NT, 4]), ALU.mult, ["gv", "gArow"], ["gv"])
        act(beta[:, :, :], smC[:, :, 4:8], AF.Sigmoid, ["smC"], ["beta"])

        def v4(ap):
            return ap.rearrange("p (h n) -> p h n", h=4)
        p.barrier()
        G1t = Seq(ar, G1_TMP, SCR_END)
        NFL = 2
        TB = []
        for f_ in range(NFL):
            TB.append(dict(
                gtU=G1t.a(f"gtU{f_}", [128, 4, 128], F32), gtS=G1t.a(f"gtS{f_}", [128, 4, 128], F32),
                Eg=G1t.a(f"Eg{f_}", [128, 4, 128], F32), ETg=G1t.a(f"ETg{f_}", [128, 4, 128], F32),
                Pm=[G1t.a(f"Pm{f_}_{i}", [128, 4, 128], F32) for i in range(2)],
                PTm=[G1t.a(f"PTm{f_}_{i}", [128, 4, 128], F32) for i in range(2)],
                RT=G1t.a(f"RT{f_}", [128, 4, 128], F32), kbe=G1t.a(f"kbe{f_}", [128, 512], BF16),
                gt=G1t.a(f"gt{f_}", [128, 8], F32), ke=G1t.a(f"ke{f_}", [128, 4], F32), bg=G1t.a(f"bg{f_}", [128, 4], F32)))

        def gdn_tile(t):
            f_ = t % NFL
            Bf = TB[f_]
            gtU, gtS, Eg, ETg, Pm, PTm, RT, kbe, gt, ke, bg = (Bf[k_] for k_ in ("gtU", "gtS", "Eg", "ETg", "Pm", "PTm", "RT", "kbe", "gt", "ke", "bg"))
            K = lambda n_: f"{n_}#{f_}"
            tsl = slice(t * 128, (t + 1) * 128)
            b = bank()
            mm(ps[b][:, 0:4], UF, gv[:, t, :], True, True, ["CF", "gv"], [f"ps{b}"])
            mm(ps[b][:, 4:8], ONESF, gv[:, t, :], True, True, ["CF", "gv"], [f"ps{b}"])
            cp(gt[:, :], ps[b][:, 0:8], [f"ps{b}"], [K("gt")], eng='dve')
            tt(gtU[:, :, :], bc(UF, 1, [128, 4, 128]), bc(gv[:, t, :], 2, [128, 4, 128]), ALU.mult, ["CF", "gv"], [K("gtU")])
            tt(gtS[:, :, :], bc(SLF, 1, [128, 4, 128]), bc(gv[:, t, :], 2, [128, 4, 128]), ALU.mult, ["CF", "gv"], [K("gtS")])
            yield
            act(eG[:, t, :], gt[:, 0:4], AF.Exp, [K("gt")], [f"eG{t}"])
            act(dG[:, t, :], gt[:, 4:8], AF.Exp, [K("gt")], [f"dG{t}"])
            tt(ke[:, :], gt[:, 4:8], gt[:, 0:4], ALU.subtract, [K("gt")], [K("ke")])
            act(ke[:, :], ke[:, :], AF.Exp, [K("ke")], [K("ke")])
            b1 = bank()
            mm(ps[b1][:, :], SLF, gtU[:, :, :].rearrange("p h l -> p (h l)"), True, True, ["CF", K("gtU")], [f"ps{b1}"])
            b2 = bank()
            mm(ps[b2][:, :], UF, gtS[:, :, :].rearrange("p h l -> p (h l)"), True, True, ["CF", K("gtS")], [f"ps{b2}"])
            bk = bank()
            for h in range(4):
                mm(ps[bk][:, h * 128:(h + 1) * 128], kT[:, h, tsl], kT[:, h, tsl], True, True, kTk, [f"ps{bk}"])
            bq = bank()
            for h in range(4):
                mm(ps[bq][:, h * 128:(h + 1) * 128], kT[:, h, tsl], qT[:, h, tsl], True, True, kTk + qTk, [f"ps{bq}"])
            yield
            tt(bg[:, :], beta[:, t, :], eG[:, t, :], ALU.mult, ["beta", f"eG{t}"], [K("bg")])
            act(ETg[:, :, :].rearrange("p h l -> p (h l)"), ps[b1][:, :], AF.Exp, [f"ps{b1}"], [K("ETg")])
            act(Eg[:, :, :].rearrange("p h l -> p (h l)"), ps[b2][:, :], AF.Exp, [f"ps{b2}"], [K("Eg")])
            yield
            tt(Eg[:, :, :], Eg[:, :, :], bc(SLF, 1, [128, 4, 128]), ALU.mult, [K("Eg"), "CF"], [K("Eg")])
            tt(Eg[:, :, :], Eg[:, :, :], bc(beta[:, t, :], 2, [128, 4, 128]), ALU.mult, [K("Eg"), "beta"], [K("Eg")])
            tt(Pm[0][:, :, :], v4(ps[bk][:, :]), Eg[:, :, :], ALU.mult, [f"ps{bk}", K("Eg")], [K("Pm0")])
            yield
            bt_ = bank()
            for h in range(4):
                tr(ps[bt_][:, h * 128:(h + 1) * 128], Pm[0][:, h, :], IDF, [K("Pm0"), "CF"], [f"ps{bt_}"])
            tt(ETg[:, :, :], ETg[:, :, :], bc(UF, 1, [128, 4, 128]), ALU.mult, [K("ETg"), "CF"], [K("ETg")])
            tt(v4(qkTm[:, t, :]), v4(ps[bq][:, :]), ETg[:, :, :], ALU.mult, [f"ps{bq}", K("ETg")], [f"qkTm{t}"])
            yield
            cp(PTm[0][:, :, :], v4(ps[bt_][:, :]), [f"ps{bt_}"], [K("PTm0")], eng='act')
            tt(RT[:, :, :], bc(IDF, 1, [128, 4, 128]), v4(ps[bt_][:, :]), ALU.subtract, ["CF", f"ps{bt_}"], [K("RT")])
            yield
            cur = 0
            for kstep in range(6):
                nxt = 1 - cur
                ba = bank()
                for h in range(4):
                    mm(ps[ba][:, h * 128:(h + 1) * 128], PTm[cur][:, h, :], Pm[cur][:, h, :], True, True,
                       [K(f"Pm{cur}"), K(f"PTm{cur}")], [f"ps{ba}"])
                if kstep < 5:
                    bb = bank()
                    for h in range(4):
                        mm(ps[bb][:, h * 128:(h + 1) * 128], Pm[cur][:, h, :], PTm[cur][:, h, :], True, True,
                           [K(f"Pm{cur}"), K(f"PTm{cur}")], [f"ps{bb}"])
                yield
                cp(Pm[nxt][:, :, :], v4(ps[ba][:, :]), [f"ps{ba}"], [K(f"Pm{nxt}")], eng='act')
                if kstep < 5:
                    cp(PTm[nxt][:, :, :], v4(ps[bb][:, :]), [f"ps{bb}"], [K(f"PTm{nxt}")], eng='dve')
                yield
                bc_ = bank()
                for h in range(4):
                    mm(ps[bc_][:, h * 128:(h + 1) * 128], Pm[nxt][:, h, :], RT[:, h, :], True, True, [K(f"Pm{nxt}"), K("RT")], [f"ps{bc_}"])
                yield
                tt(RT[:, :, :], RT[:, :, :], v4(ps[bc_][:, :]), ALU.add, [K("RT"), f"ps{bc_}"], [K("RT")])
                cur = nxt
            cp(v4(TTb[:, t, :]), RT[:, :, :], [K("RT")], [f"TTb{t}"], eng='act')
            tt(v4(vb[:, t, :]), v4(vb[:, t, :]), bc(beta[:, t, :], 2, [128, 4, 128]), ALU.mult, [f"vb{t}", "beta"], [f"vb{t}"])
            tt(v4(kbe[:, :]), v4(k_tm[:, t, :]), bc(bg[:, :], 2, [128, 4, 128]), ALU.mult, [f"k_tm{t}", K("bg")], [K("kbe")])
            tt(v4(k_end[:, t, :]), v4(k_tm[:, t, :]), bc(ke[:, :], 2, [128, 4, 128]), ALU.mult, [f"k_tm{t}", K("ke")], [f"k_end{t}"])
            yield
            bw = bank()
            for h in range(4):
                mm(ps[bw][:, h * 128:(h + 1) * 128], kbe[:, h * 128:(h + 1) * 128], TTb[:, t, h * 128:(h + 1) * 128], True, True,
                   [K("kbe"), f"TTb{t}"], [f"ps{bw}"])
            yield
            act(wT[:, t, :], ps[bw][:, :], AF.Copy, [f"ps{bw}"], [f"wT{t}"], scale=-1.0)

        for t0_ in range(0, NT, NFL):
            run_interleaved([gdn_tile(t0_ + i_) for i_ in range(NFL)])

        p.barrier()
        G1b = Seq(ar, M_SCR + 50688, SCR_END)
        gz = G1b.a("gz", [128, NT, 512], F32)
        Sst = G1b.a("Sst", [128, 512], F32)
        Sb = G1b.a("Sb", [128, 512], BF16)
        Sn = G1b.a("Sn", [128, 512], BF16)
        vnew = G1b.a("vnew", [128, 512], BF16)
        ovs = [G1b.a(f"ov{i}", [128, 512], F32) for i in range(2)]
        otmp = G1b.a("otmp", [128, 512], F32)
        ob = G1b.a("ob", [128, 512], BF16)
        ssq4 = G1b.a("ssq4", [128, 4], F32)
        rs4 = G1b.a("rs4", [128, 4], F32)
        junk2 = G1b.a("junk2", [128, 128], F32)
        def gz_thread():
            for hf in range(4):
                def cons_z(t, pa, pk, hf=hf):
                    act(gz[:, t, hf * 128:(hf + 1) * 128], pa, AF.Silu, [pk], [f"gz{t}_{hf}"])
                    tt(gz[:, t, hf * 128:(hf + 1) * 128], gz[:, t, hf * 128:(hf + 1) * 128], rowp(l, 32, 128), ALU.mult,
                       [f"gz{t}_{hf}", "ROWP"], [f"gz{t}_{hf}"])
                yield from linear_tm_gen(Wtm, 512 + hf * 128, 128, lambda k, t: xn[:, k, HALO + t * 128: HALO + (t + 1) * 128], XNK,
                                         range(NT), cons_z)

        def s_refresh():
            cp(Sb[:, :], Sst[:, :], ["Sst"], ["Sb"], eng='act')

        def scan_gen(full, from_rcv=None):
            if from_rcv is not None:
                dma(Sst[:, :], rcv[from_rcv].ap()[0:128, :], "Sst", [f"rcv{from_rcv}"], ["Sst"])
                ts(Sst[:, :], Sst[:, :], FLAG, None, ALU.mult, None, ["Sst", "MISC"], ["Sst"])
                s_refresh()
                yield
            for t in range(NT):
                tsl = slice(t * 128, (t + 1) * 128)
                bv = bank()
                for h in range(4):
                    hs = slice(h * 128, (h + 1) * 128)
                    mm(ps[bv][:, hs], TTb[:, t, hs], vb[:, t, hs], True, False, [f"TTb{t}", f"vb{t}"], [f"ps{bv}"])
                    mm(ps[bv][:, hs], wT[:, t, hs], Sb[:, hs], False, True, [f"wT{t}", "Sb"], [f"ps{bv}"])
                if full:
                    bo = bank()
                    for h in range(4):
                        hs = slice(h * 128, (h + 1) * 128)
                        mm(ps[bo][:, hs], qT[:, h, tsl], Sb[:, hs], True, True, qTk + ["Sb"], [f"ps{bo}"])
                yield
                cp(vnew[:, :], ps[bv][:, :], [f"ps{bv}"], ["vnew"], eng='act')
                yield
                bs = bank()
                for h in range(4):
                    hs = slice(h * 128, (h + 1) * 128)
                    mm(ps[bs][:, hs], k_end[:, t, hs], vnew[:, hs], True, True, [f"k_end{t}", "vnew"], [f"ps{bs}"])
                if full:
                    bo2 = bank()
                    for h in range(4):
                        hs = slice(h * 128, (h + 1) * 128)
                        mm(ps[bo2][:, hs], qkTm[:, t, hs], vnew[:, hs], True, True, [f"qkTm{t}", "vnew"], [f"ps{bo2}"])
                yield
                for h in range(4):
                    hs = slice(h * 128, (h + 1) * 128)
                    stt(Sst[:, hs], Sst[:, hs], dG[:, t, h:h + 1], ps[bs][:, hs], ALU.mult, ALU.add, ["Sst", f"dG{t}", f"ps{bs}"], ["Sst"])
                yield
                s_refresh()
                if full:
                    ovt = ovs[t % 2]
                    tt(v4(otmp[:, :]), v4(ps[bo][:, :]), bc(eG[:, t, :], 2, [128, 4, 128]), ALU.mult, [f"ps{bo}", f"eG{t}"], ["otmp"])
                    yield
                    tt(ovt[:, :], ps[bo2][:, :], otmp[:, :], ALU.add, [f"ps{bo2}", "otmp"], [f"ov{t % 2}"])
                    fin_state['ready'] = t + 1
                yield

        fin_state = {'ready': 0}

        def fin_thread():
            for t in range(NT):
                while fin_state['ready'] <= t:
                    yield
                tsl = slice(t * 128, (t + 1) * 128)
                ovt = ovs[t % 2]
                ok_ = f"ov{t % 2}"
                memset(ssq4[:, :], 0.0, ["ssq4"])
                for h in range(4):
                    act(junk2[:, :], ovt[:, h * 128:(h + 1) * 128], AF.Square, [ok_, "ssq4"], ["junk2", "ssq4"], accum=ssq4[:, h:h + 1])
                yield
                act(rs4[:, :], ssq4[:, :], AF.Sqrt, ["ssq4", "EPSC"], ["rs4"], bias=EPSC[:, 0:1], scale=1.0 / 128)
                yield
                recip(rs4[:, :], rs4[:, :], ["rs4"], ["rs4"])
                yield
                for h in range(4):
                    hs = slice(h * 128, (h + 1) * 128)
                    stt(ob[:, hs], ovt[:, hs], rs4[:, h:h + 1], gz[:, t, hs], ALU.mult, ALU.mult,
                        [ok_, "rs4"] + [f"gz{t}_{hf}" for hf in range(4)], ["ob"])
                yield
                b = bank()
                psb = ps[b][:, :].bitcast(BF16)
                for c in range(4):
                    tr(psb[:, c * 128:(c + 1) * 128], ob[:, c * 128:(c + 1) * 128], IDB, ["ob", "CBF"], [f"ps{b}"])
                yield
                cp(mixed[:, 8:12, tsl], psb[:, 0:512].rearrange("p (c n) -> p c n", c=4), [f"ps{b}"], [f"mixedC{t}"])
                yield

        memset(Sst[:, :], 0.0, ["Sst"])
        s_refresh()
        run_interleaved([scan_gen(False), gz_thread()])
        ex = 2 * l + 1
        dma(snd[ex].ap()[:, :], Sst[:, :], f"snd{ex}", ["Sst"], [f"snd{ex}"])
        exchange(ex, [f"snd{ex}"], [f"rcv{ex}"])

        SB = Seq(ar, G1b.cur, SCR_END)
        rb = SB.a("rb", [128, TT], F32)
        lv = [SB.a(f"lv{i}", [128, TT], F32) for i in range(2)]
        pl = SB.a("pl", [128, T], F32)
        plb = SB.a("plb", [128, T], BF16)
        pwb = SB.a("pwb", [128, 128], BF16)
        pwf = SB.a("pwf", [128, 128], F32)

        def mixer_b():
            for gi in range(4):
                def cons_b(i, bi, pa, pk):
                    t0, n = BLK3[bi]
                    cp(rb[:, t0:t0 + n], pa, [pk], [f"rb_{bi}"])
                yield from linear_fm_gen(Wl, [32 + gi], KC, xsrc, XNKB, BLK3, cons_b)
                rk = [f"rb_{b_}" for b_ in range(3)]
                w = 2 ** (gi + 1)
                srcb, sk, lo = rb, rk, 0
                for lev in range(gi + 1):
                    sh = 2 ** lev
                    dstb = lv[lev % 2]
                    nlo = lo + sh
                    tt(dstb[:, nlo:TT], srcb[:, nlo:TT], srcb[:, nlo - sh:TT - sh], ALU.add, sk, [f"lv{lev % 2}"])
                    srcb, sk, lo = dstb, [f"lv{lev % 2}"], nlo
                    yield
                stt(pl[:, :], srcb[:, HALO:TT], 1.0 / w, rb[:, HALO:TT], ALU.mult, ALU.subtract, sk + rk, ["pl"])
                tt(lv[(gi + 1) % 2][:, 0:HALO], srcb[:, HALO:2 * HALO], MISC[:, 16 + gi * 16: 32 + gi * 16], ALU.mult, sk + ["MISC"],
                   [f"lv{(gi + 1) % 2}"])
                yield
                tt(pl[:, 0:HALO], lv[(gi + 1) % 2][:, 0:HALO], rb[:, HALO:2 * HALO], ALU.subtract, [f"lv{(gi + 1) % 2}", "pl"] + rk, ["pl"])
                dma(pwf[:, :], pool_w[l, gi], "pwf", [], ["pwf"])
                yield
                cp(plb[:, :], pl[:, :], ["pl"], ["plb"], eng='act')
                cp(pwb[:, :], pwf[:, :], ["pwf"], ["pwb"], eng='act')
                yield
                for hb_ in range(2):
                    b = bank()
                    mm(ps[b][:, :], pwb[:, :], plb[:, hb_ * 512:(hb_ + 1) * 512], True, True, ["pwb", "plb"], [f"ps{b}"])
                    act(mixed[:, 4 + gi, hb_ * 512:(hb_ + 1) * 512], ps[b][:, :], AF.Identity, [f"ps{b}", "COLP"], [f"mixedB{gi}_{hb_}"],
                        scale=COLP[:, LC + 124 + gi: LC + 125 + gi])
                yield

        run_interleaved([delay_gen(scan_gen(True, from_rcv=ex), 10), fin_thread(), mixer_b()])

        p.barrier()
        R = Seq(ar, M_SCR, SCR_END)
        x = R.a("x", [128, KC, T], F32)
        y = R.a("y", [128, KC, 512], F32)
        sqb = [R.a(f"rsqb{i}", [128, 512], BF16) for i in range(3)]
        rtmp = R.a("rrtmp", [128, 512], F32)
        rstd = R.a("rrstd", [128, 512], F32)
        mnT = R.a("mnT", [128, KC, MEM], BF16)
        eT = [ar.at(f"eT{i}", [128, 2, 512], BF16, R.off["mnT"] + 2048 * i) for i in range(2)]
        rden = ar.at("rden", [128, 512], F32, R.off["mnT"] + 4096)
        vTf = [R.a(f"vTf{i}", [128, MEM], BF16) for i in range(2)]
        xn2 = ar.at("xn2", [128, KC, 512], BF16, M_XN)
        q = ar.at("q", [128, KC, 512], BF16, M_XN + 16384)
        memst = ar.at("memst", [128, KC, MEM], F32, M_XN + 16384)
        o = ar.at("o", [128, KC, 512], BF16, M_MIX)
        kTm = ar.at("kTm", [128, KC, MEM], BF16, M_MIX + 16384)
        vm = ar.at("vm", [128, 2, D], BF16, M_MIX + 24576)
        hff = ar.at("hff", [128, FKC, 512], BF16, M_XN + 16384)
        sg = rden
        xv = x_src.rearrange("(k p) t -> p k t", p=128)
        xdv = x_dst.rearrange("(k p) t -> p k t", p=128)

        def post_norm_add(row, tb):
            rk = sumsq_rstd(lambda k: y[:, k, :], KC, 512, sqb, rstd, rtmp, "y")
            for k in range(KC):
                stt(y[:, k, :], y[:, k, :], gcol(l, row, k), rstd[:, :], ALU.mult, ALU.mult, ["y", rk, f"GS{l}"], ["y"])
                tt(x[:, k, tb * 512:(tb + 1) * 512], x[:, k, tb * 512:(tb + 1) * 512], y[:, k, :], ALU.add, [f"x{tb}", "y"], [f"x{tb}"])

        def pre_norm(row, tb, dst, dkey):
            rk = sumsq_rstd(lambda k: x[:, k, tb * 512:(tb + 1) * 512], KC, 512, sqb, rstd, rtmp, f"x{tb}")
            for k in range(KC):
                stt(dst[:, k, :], x[:, k, tb * 512:(tb + 1) * 512], gcol(l, row, k), rstd[:, :], ALU.mult, ALU.mult,
                    [f"x{tb}", rk, f"GS{l}"], [dkey])

        def cons_y(i, bi, pa, pk):
            cp(y[:, i, :], pa, [pk], ["y"])

        for tb in range(2):
            dma(x[:, :, tb * 512:(tb + 1) * 512], xv[:, :, tb * 512:(tb + 1) * 512], f"xld{tb}", [], [f"x{tb}"])
        dma(memst[:, :, :], memT_in.rearrange("(k p) t -> p k t", p=128), "memst", [], ["memst"])
        rk = sumsq_rstd(lambda k: memst[:, k, :], KC, MEM, sqb, rstd, rtmp, "memst")
        for k in range(KC):
            stt(mnT[:, k, :], memst[:, k, :], gcol(l, 4, k), rstd[:, 0:MEM], ALU.mult, ALU.mult, ["memst", rk, f"GS{l}"], ["mnT"])
        def w_out_tb(tb):
            linear_fm(w_out[l], list(range(KC)), KC, lambda k, t0, n, tb=tb: mixed[:, k, tb * 512:(tb + 1) * 512], ["mixedR"],
                      [(0, 512)], cons_y)
        w_out_tb(0)
        post_norm_add(1, 0)
        w_out_tb(1)

        linear_fm(xa_wk[l], list(range(KC)), KC, lambda k, t0, n: mnT[:, k, :], ["mnT"], [(0, MEM)],
                  lambda i, bi, pa, pk: cp(kTm[:, i, :], pa, [pk], ["kTm", "mixedR"]))
        post_norm_add(1, 1)
        pre_norm(2, 0, xn2, "xn2")
        pendv = []

        def cons_v(i, bi, pa, pk):
            buf = vTf[i % 2]
            cp(buf[:, :], pa, [pk], [f"vTf{i % 2}"])
            while pendv:
                pendv.pop(0)()

            def trs(i=i, buf=buf):
                b = bank()
                psb = ps[b][:, :].bitcast(BF16)
                for mt in range(2):
                    tr(psb[:, mt * 128:(mt + 1) * 128], buf[:, mt * 128:(mt + 1) * 128], IDB, [f"vTf{i % 2}", "CBF"], [f"ps{b}"])
                cp(vm[:, :, i * 128:(i + 1) * 128], psb[:, 0:256].rearrange("p (m n) -> p m n", m=2), [f"ps{b}"], ["vm", "mixedR"])
            pendv.append(trs)
        linear_fm(xa_wv[l], list(range(KC)), KC, lambda k, t0, n: mnT[:, k, :], ["mnT"], [(0, MEM)], cons_v)
        while pendv:
            pendv.pop(0)()

        p.barrier()
        for tb in range(2):
            linear_fm(xa_wq[l], list(range(KC)), KC, lambda k, t0, n: xn2[:, k, :], ["xn2"], [(0, 512)],
                      lambda i, bi, pa, pk: cp(q[:, i, :], pa, [pk], ["q"]))
            if tb == 0:
                pre_norm(2, 1, xn2, "xn2")
            else:
                pre_norm(5, 0, xn2, "xn2")
            def scores(h):
                e_ = eT[h % 2]
                ek = f"eT{h % 2}"
                for mt in range(2):
                    b = bank()
                    for c in range(4):
                        mm(ps[b][:, :], kTm[:, h * 4 + c, mt * 128:(mt + 1) * 128], q[:, h * 4 + c, :], c == 0, c == 3,
                           ["kTm", "q"], [f"ps{b}"])
                    act(e_[:, mt, :], ps[b][:, :], AF.Exp, [f"ps{b}"], [ek], scale=float(512 ** -0.5))
            scores(0)
            for h in range(4):
                e_ = eT[h % 2]
                ek = f"eT{h % 2}"
                if h + 1 < 4:
                    scores(h + 1)
                bd = bank()
                for mt in range(2):
                    mm(ps[bd][:, :], ONESB, e_[:, mt, :], mt == 0, mt == 1, ["CBF", ek], [f"ps{bd}"])
                act(rden[:, :], ps[bd][:, :], AF.Ln, [f"ps{bd}"], ["rden"])
                act(rden[:, :], rden[:, :], AF.Exp, ["rden"], ["rden"], scale=-1.0)
                for c in range(4):
                    b = bank()
                    for mt in range(2):
                        mm(ps[b][:, :], vm[:, mt, h * 512 + c * 128: h * 512 + (c + 1) * 128], e_[:, mt, :], mt == 0, mt == 1,
                           ["vm", ek], [f"ps{b}"])
                    tt(o[:, h * 4 + c, :], ps[b][:, :], rden[:, :], ALU.mult, [f"ps{b}", "rden"], ["o"])
            linear_fm(xa_wo[l], list(range(KC)), KC, lambda k, t0, n: o[:, k, :], ["o"], [(0, 512)], cons_y)
            post_norm_add(3, tb)

        for tb in range(2):
            for j in range(FKC):
                vg, kg = load_unit(w_gu[l][j], KC)
                vu, ku = load_unit(w_gu[l][FKC + j], KC)
                bg_ = bank()
                for k in range(KC):
                    mm(ps[bg_][:, :], vg[:, k, :], xn2[:, k, :], k == 0, k == KC - 1, [kg, "xn2"], [f"ps{bg_}"])
                bu_ = bank()
                for k in range(KC):
                    mm(ps[bu_][:, :], vu[:, k, :], xn2[:, k, :], k == 0, k == KC - 1, [ku, "xn2"], [f"ps{bu_}"])
                act(sg[:, :], ps[bg_][:, :], AF.Silu, [f"ps{bg_}"], ["sg", "rden"])
                tt(hff[:, j, :], sg[:, :], ps[bu_][:, :], ALU.mult, ["sg", f"ps{bu_}"], ["hff", "q", "o", "kTm", "vm"])
            if tb == 0:
                pre_norm(5, 1, xn2, "xn2")
            linear_fm(w_down[l], list(range(KC)), FKC, lambda k, t0, n: hff[:, k, :], ["hff"], [(0, 512)], cons_y)
            post_norm_add(6, tb)
            dma(xdv[:, :, tb * 512:(tb + 1) * 512], x[:, :, tb * 512:(tb + 1) * 512], f"xst{tb}", [f"x{tb}"], [f"xdst{tb}"])
        if l + 1 < n_layers:
            dma(snd[4].ap()[:, 0:256].rearrange("p (k t) -> p k t", k=KC), x[:, :, T - HALO:T], "snd4", ["x1"], ["snd4"])
            exchange(4, ["snd4"], ["rcv4"])
        return [f"xst{tb}" for tb in range(2)]

    final_streams = None
    for l in range(n_layers):
        last = (l == n_layers - 1)
        final_streams = layer(l, xT_in if l == 0 else xs, None if l == 0 else 4, out if last else xs)

    p.plan()
    with contextlib.ExitStack() as es:
        sems = {e: es.enter_context(nc.semaphore(f"s_{e}")) for e in p.ENG}
        ssems = {s: es.enter_context(nc.semaphore(f"d_{s}")) for s in p.stream_cnt}
        block = es.enter_context(nc.Block())

        @block.tensor
        def _(e):
            p.emit_engine('pe', e, sems, ssems)

        @block.scalar
        def _(e):
            p.emit_engine('act', e, sems, ssems)

        @block.vector
        def _(e):
            p.emit_engine('dve', e, sems, ssems)

        @block.gpsimd
        def _(e):
            p.emit_engine('pool', e, sems, ssems)

        @block.sync
        def _(e):
            p.emit_engine('sp', e, sems, ssems)
            for s in p.stream_cnt:
                if s.startswith("xst") or s.startswith("xld"):
                    e.wait_ge(ssems[s], p.stream_cnt[s])
    return nc, p


_CACHE = {}


def _host_prep(inputs):
    f = lambda a: np.ascontiguousarray(np.asarray(a, dtype=np.float32))
    x = f(inputs['x'])
    mem = f(inputs['mem'])
    cf = np.zeros((128, 512), np.float32)
    cf[:, 0:128] = np.eye(128)
    cf[:, 128:256] = 1.0
    cf[:, 256:384] = np.triu(np.ones((128, 128), np.float32))
    cf[:, 384:512] = np.tril(np.ones((128, 128), np.float32), -1)
    rowp = np.zeros((1, 2 * 672), np.float32)
    colp = np.zeros((128, 2 * 216), np.float32)
    for l in range(2):
        r = rowp[0, l * 672:(l + 1) * 672]
        r[0:4] = inputs['gdn_A_log'][l]
        r[4:8] = inputs['gdn_dt_bias'][l]
        r[8:16] = inputs['ssm_A_log'][l]
        r[16:24] = inputs['ssm_dt_bias'][l]
        r[24:32] = inputs['ssm_D'][l]
        r[32:160] = inputs['gdn_norm_g'][l]
        r[160:672] = inputs['ssm_norm_g'][l]
        c = colp[:, l * 216:(l + 1) * 216]
        c[:, 0:112] = np.asarray(inputs['norm_g'][l]).reshape(7, 16, 128).transpose(2, 0, 1).reshape(128, 112)
        c[:, 112:124] = np.asarray(inputs['conv_a_w'][l]).reshape(3, 4, 128).transpose(2, 1, 0).reshape(128, 12)
        c[:, 124:128] = np.asarray(inputs['pool_scale'][l]).reshape(4, 128).T
        c[:, 128:176] = np.asarray(inputs['gdn_conv_w'][l]).reshape(4, 12, 128).transpose(2, 1, 0).reshape(128, 48)
        c[:, 176:208] = np.asarray(inputs['ssm_conv_w'][l]).reshape(4, 8, 128).transpose(2, 1, 0).reshape(128, 32)
        c[:, 208:216] = np.asarray(inputs['ssm_conv_b'][l]).reshape(8, 128).T
    rowp = np.ascontiguousarray(np.broadcast_to(rowp, (128, 2 * 672)))
    shared = dict(cf=cf, rowp=rowp, colp=colp)

    def tile_w(w):
        w = f(w)
        L, K, N = w.shape
        return np.ascontiguousarray(w.reshape(L, K // 128, 128, N // 128, 128).transpose(0, 3, 2, 1, 4))
    w_in_ = f(inputs['w_in'])
    a_cols = []
    for j in range(4):
        for base in (A_C, A_H, A_B):
            a_cols.append(np.arange(base + 128 * j, base + 128 * (j + 1)))
    fm_cols = np.concatenate([np.arange(D_XBC, D_XBC + 1024), np.arange(C_QKV, C_QKV + 1536)] + a_cols + [np.arange(B_U, B_U + 512)])
    tm_cols = np.concatenate([np.arange(D_Z, D_Z + 512), np.arange(C_Z, C_Z + 512), np.arange(D_DT, D_DT + 8), np.arange(C_AB, C_AB + 8)])
    shared['w_in_fm'] = tile_w(w_in_[:, :, fm_cols])
    shared['w_in_tm'] = np.ascontiguousarray(w_in_[:, :, tm_cols])
    shared['pool_w'] = f(inputs['pool_w'])
    shared['w_out_t'] = tile_w(inputs['w_out'])
    shared['xa_wq_t'] = tile_w(inputs['xa_wq'])
    wkv = f(inputs['xa_wkv'])
    shared['xa_wk_t'] = tile_w(wkv[:, :, :D])
    shared['xa_wv_t'] = tile_w(wkv[:, :, D:])
    shared['xa_wo_t'] = tile_w(inputs['xa_wo'])
    shared['ffn_w_gu_t'] = tile_w(inputs['ffn_w_gu'])
    shared['ffn_w_down_t'] = tile_w(inputs['ffn_w_down'])
    in_maps = []
    for core in range(8):
        b, s = core // 2, core % 2
        m = dict(shared)
        m['xT'] = np.ascontiguousarray(x[b, s * T:(s + 1) * T, :].T)
        if s == 0:
            m['xh'] = np.zeros((D, HALO), np.float32)
        else:
            m['xh'] = np.ascontiguousarray(x[b, T - HALO:T, :].T)
        m['memT'] = np.ascontiguousarray(mem[b].T)
        misc = np.zeros((128, 80), np.float32)
        misc[:, 0] = float(s)
        for gi in range(4):
            w = 2 ** (gi + 1)
            for t in range(HALO):
                misc[:, 16 + gi * 16 + t] = 1.0 / (min(t + 1, w) if s == 0 else w)
        m['misc'] = misc
        in_maps.append(m)
    return in_maps


def kernel(**inputs):
    if 'nc' not in _CACHE:
        _CACHE['nc'] = build()[0]
    nc = _CACHE['nc']
    in_maps = _host_prep(inputs)
    res = run_bass_kernel_spmd(nc, in_maps, core_ids=list(range(8)))
    outp = np.zeros((4, 2 * T, D), np.float32)
    for core in range(8):
        b, s = core // 2, core % 2
        outp[b, s * T:(s + 1) * T, :] = res.results[core]['out'].T
    return outp
```

```python
import contextlib
import numpy as np
import concourse.bass as bass
import concourse.mybir as mybir
from concourse.bass_utils import run_bass_kernel_spmd

F32 = mybir.dt.float32
BF16 = mybir.dt.bfloat16
ALU = mybir.AluOpType
AF = mybir.ActivationFunctionType

D = 2048
KC = 16
T = 1024
HALO = 16
TT = T + HALO
NT = 8
DFF = 5632
FKC = 44
MEM = 256
EPS = 1e-6
B0 = 16512
SBUF_END = 229376

A_B, A_C, A_H = 0, 512, 1024
B_U = 1536
C_QKV, C_Z, C_AB = 2048, 3584, 4096
D_Z, D_XBC, D_DT = 4104, 4616, 5640


class Prog:
    ENG = ('pe', 'act', 'dve', 'pool', 'sp')

    def __init__(self, nc):
        self.nc = nc
        self.ops = {e: [] for e in self.ENG}
        self.lastw = {}
        self.readers = {}
        self.stream_cnt = {}
        self.pool_streams = set()

    def op(self, eng, emit, reads=(), writes=(), stream=None, inc=16):
        if eng == 'pool' and stream is not None:
            self.pool_streams.add(stream)
        deps = []
        for k in reads:
            t = self.lastw.get(k)
            if t is not None:
                deps.append(t)
        for k in writes:
            t = self.lastw.get(k)
            if t is not None:
                deps.append(t)
            deps.extend(self.readers.get(k, ()))
        idx = len(self.ops[eng])
        if stream is not None:
            c = self.stream_cnt.get(stream, 0) + inc
            self.stream_cnt[stream] = c
            tok = ('s', stream, c)
        else:
            tok = ('e', eng, idx)
        waits = [d for d in deps if not (d[0] == 'e' and d[1] == eng and eng == 'pe')]
        self.ops[eng].append(dict(emit=emit, waits=waits, stream=stream, inc=inc))
        for k in writes:
            self.lastw[k] = tok
            self.readers[k] = []
        for k in reads:
            self.readers.setdefault(k, []).append(tok)
        return tok

    def barrier(self):
        toks = []
        for e in self.ENG:
            real = [i for i, o in enumerate(self.ops[e]) if o['emit'] is not None and o['stream'] is None]
            if real:
                toks.append(('e', e, real[-1]))
        for s, c in self.stream_cnt.items():
            if s not in self.pool_streams:
                toks.append(('s', s, c))
        for e in self.ENG:
            if e != 'pool':
                self.ops[e].append(dict(emit=None, waits=list(toks), stream=None, inc=0))
        keep = ("W", "snd", "rcv")
        self.lastw = {k: v for k, v in self.lastw.items() if k.startswith(keep)}
        self.readers = {k: v for k, v in self.readers.items() if k.startswith(keep)}

    def plan(self):
        needed = {e: set() for e in self.ENG}
        plan = {}
        for e in self.ENG:
            waited = {}
            out = []
            for o in self.ops[e]:
                best = {}
                for d in o['waits']:
                    key = (d[0], d[1])
                    if d[2] > best.get(key, -1):
                        best[key] = d[2]
                w = []
                for key, v in best.items():
                    if v > waited.get(key, -1):
                        waited[key] = v
                        w.append((key[0], key[1], v))
                        if key[0] == 'e':
                            needed[key[1]].add(v)
                out.append(w)
            plan[e] = out
        self.rank = {}
        for e in self.ENG:
            self.rank[e] = {i: c + 1 for c, i in enumerate(sorted(needed[e]))}
        self._plan = plan
        self.needed = needed

    def emit_engine(self, e, eng, sems, ssems):
        plan = self._plan[e]
        for i, o in enumerate(self.ops[e]):
            for (kind, src, v) in plan[i]:
                if kind == 'e':
                    eng.wait_ge(sems[src], self.rank[src][v])
                else:
                    eng.wait_ge(ssems[src], v)
            if o['emit'] is None:
                continue
            ins = o['emit'](eng)
            if o['stream'] is not None:
                ins.then_inc(ssems[o['stream']], o['inc'])
            elif i in self.needed[e]:
                ins.then_inc(sems[e], 1)


class Arena:
    def __init__(self, nc):
        self.nc = nc
        self.n = 0

    def at(self, name, shape, dtype, off):
        esz = 4 if dtype == F32 else 2
        size = esz
        for s in shape[1:]:
            size *= s
        assert B0 + off + size <= SBUF_END, (name, off, size)
        self.n += 1
        return self.nc.alloc_sbuf_tensor_at(f"{name}_{self.n}", list(shape), dtype, offset=B0 + off)


class Seq:
    def __init__(self, ar, lo, hi):
        self.ar, self.lo, self.hi, self.cur = ar, lo, hi, lo

    def a(self, name, shape, dtype):
        esz = 4 if dtype == F32 else 2
        size = esz
        for s in shape[1:]:
            size *= s
        size = (size + 63) // 64 * 64
        off = self.cur
        self.cur += size
        assert self.cur <= self.hi, (name, self.cur, self.hi)
        if not hasattr(self, 'off'):
            self.off = {}
        self.off[name] = off
        return self.ar.at(name, shape, dtype, off)


def build(n_layers=2, debug=False):
    nc = bass.Bass("TRN2", target_bir_lowering=False)
    p = Prog(nc)
    ar = Arena(nc)

    def din(name, shape):
        return nc.dram_tensor(name, list(shape), F32, kind="ExternalInput").ap()

    xT_in = din("xT", [D, T])
    xh_in = din("xh", [D, HALO])
    memT_in = din("memT", [D, MEM])
    cf_in = din("cf", [128, 512])
    rowp_in = din("rowp", [128, 2 * 672])
    colp_in = din("colp", [128, 2 * 216])
    misc_in = din("misc", [128, 80])
    w_in_fm = din("w_in_fm", [2, 36, 128, KC, 128])
    w_in_tm = din("w_in_tm", [2, D, 1040])
    pool_w = din("pool_w", [2, 4, 128, 128])
    w_out = din("w_out_t", [2, KC, 128, KC, 128])
    xa_wq = din("xa_wq_t", [2, KC, 128, KC, 128])
    xa_wk = din("xa_wk_t", [2, KC, 128, KC, 128])
    xa_wv = din("xa_wv_t", [2, KC, 128, KC, 128])
    xa_wo = din("xa_wo_t", [2, KC, 128, KC, 128])
    w_gu = din("ffn_w_gu_t", [2, 2 * FKC, 128, KC, 128])
    w_down = din("ffn_w_down_t", [2, KC, 128, FKC, 128])
    out = nc.dram_tensor("out", [D, T], F32, kind="ExternalOutput").ap()
    xs = nc.dram_tensor("xs", [D, T], F32).ap()
    snd = [nc.dram_tensor(f"snd{i}", [128, 512 if i < 4 else 256], F32) for i in range(5)]
    rcv = [nc.dram_tensor(f"rcv{i}", [256, 512 if i < 4 else 256], F32) for i in range(5)]
    dbg = {}
    if debug:
        dbg['mixed'] = nc.dram_tensor("dbg_mixed", [D, T], F32, kind="ExternalOutput").ap()

    P = Seq(ar, 0, 34304)
    CF = P.a("cf", [128, 512], F32)
    IDF, ONESF, UF, SLF = CF[:, 0:128], CF[:, 128:256], CF[:, 256:384], CF[:, 384:512]
    CBF = P.a("cbf", [128, 256], BF16)
    IDB, ONESB = CBF[:, 0:128], CBF[:, 128:256]
    ROWP = P.a("rowp", [128, 1344], F32)
    COLP = P.a("colp", [128, 432], F32)
    MISC = P.a("misc", [128, 80], F32)
    FLAG = MISC[:, 0:1]
    EPSC = P.a("epsc", [128, 4], F32)
    GS = P.a("gs", [128, 224], F32)
    NSLOT = 5
    WS = [P.a(f"wslot{i}", [128, 2048], BF16) for i in range(NSLOT)]
    PEND = P.cur
    ps = [nc.alloc_psum_tensor(f"ps{i}", [128, 512], F32) for i in range(8)]
    st = dict(bank=0, wslot=0, ev=0, pool=None, pidx={})

    def bank():
        b = st['bank']
        st['bank'] = (b + 1) % 8
        pool = st.get('pool')
        if pool is not None:
            i = st['pidx'].get(id(pool), 0)
            st['pidx'][id(pool)] = i + 1
            return pool[i % len(pool)]
        return b

    def wslot():
        s = st['wslot']
        st['wslot'] = (s + 1) % NSLOT
        return s

    def mm(o, lhsT, rhs, start, stop, r, w):
        p.op('pe', lambda e: e.matmul(o, lhsT=lhsT, rhs=rhs, start=start, stop=stop), reads=r, writes=w)

    def tr(o, i, ident, r, w):
        p.op('pe', lambda e: e.transpose(o, i, ident), reads=r, writes=w)

    def act(o, i, func, r, w, bias=None, scale=None, accum=None):
        kw = {}
        if bias is not None:
            kw['bias'] = bias
        if scale is not None:
            kw['scale'] = scale
        if accum is not None:
            kw['accum_out'] = accum
        p.op('act', lambda e: e.activation(out=o, in_=i, func=func, **kw), reads=r, writes=w)

    def tt(o, a, b, op, r, w, eng='dve'):
        p.op(eng, lambda e: e.tensor_tensor(out=o, in0=a, in1=b, op=op), reads=r, writes=w)

    def ts(o, a, s1, s2, op0, op1, r, w, eng='dve'):
        if op1 is None:
            p.op(eng, lambda e: e.tensor_scalar(out=o, in0=a, scalar1=s1, scalar2=None, op0=op0), reads=r, writes=w)
        else:
            p.op(eng, lambda e: e.tensor_scalar(out=o, in0=a, scalar1=s1, scalar2=s2, op0=op0, op1=op1), reads=r, writes=w)

    def stt(o, a, s, b, op0, op1, r, w, eng='dve'):
        p.op(eng, lambda e: e.scalar_tensor_tensor(out=o, in0=a, scalar=s, in1=b, op0=op0, op1=op1), reads=r, writes=w)

    def cp(o, i, r, w, eng=None):
        if eng is None:
            st['ev'] ^= 1
            eng = 'act' if st['ev'] else 'dve'
        if eng == 'act':
            act(o, i, AF.Copy, r, w)
        else:
            p.op('dve', lambda e: e.tensor_copy(out=o, in_=i), reads=r, writes=w)

    def recip(o, i, r, w):
        p.op('dve', lambda e: e.reciprocal(out=o, in_=i), reads=r, writes=w)

    def memset(o, v, w):
        p.op('dve', lambda e: e.memset(o, v), writes=w)

    def dma(o, i, stream, r, w, q='sp'):
        p.op(q, lambda e: e.dma_start(out=o, in_=i), reads=r, writes=w, stream=stream)

    def exchange(i, r, w):
        p.op('pool', lambda e: e.collective_compute("AllGather", ALU.bypass,
                                                     replica_groups=[[0, 1], [2, 3], [4, 5], [6, 7]],
                                                     ins=[snd[i].ap()], outs=[rcv[i].ap()]),
             reads=r, writes=w, stream=f"cc{i}", inc=1)

    dma(CF[:, :], cf_in, "c0", [], ["CF"])
    dma(ROWP[:, :], rowp_in, "c1", [], ["ROWP"])
    dma(COLP[:, :], colp_in, "c2", [], ["COLP"])
    dma(MISC[:, :], misc_in, "c3", [], ["MISC"])
    cp(CBF[:, :], CF[:, 0:256], ["CF"], ["CBF"], eng='dve')
    memset(EPSC[:, 0:1], EPS, ["EPSC"])
    memset(EPSC[:, 1:2], D * EPS, ["EPSC1"])
    for l in range(2):
        ts(GS[:, l * 112:(l + 1) * 112], COLP[:, l * 216:l * 216 + 112], float(np.sqrt(D)), None, ALU.mult, None, ["COLP"], [f"GS{l}"])

    def gcol(l, row, kc):
        return GS[:, l * 112 + row * 16 + kc: l * 112 + row * 16 + kc + 1]

    def colp(l, off, n=1):
        return COLP[:, l * 216 + off: l * 216 + off + n]

    def rowp(l, off, n):
        return ROWP[:, l * 672 + off: l * 672 + off + n]

    def load_unit(src_ap, kcn, ncols=128):
        s = wslot()
        view = WS[s][:, 0:kcn * ncols].rearrange("p (k n) -> p k n", k=kcn)
        p.op('pool', lambda e: e.dma_start(out=view, in_=src_ap), reads=[], writes=[f"W{s}"], stream=f"w{s}")
        return view, f"W{s}"

    def linear_fm_gen(Wt, tiles, kcn, src, src_keys, blocks, consume, every=8):
        cnt = 0
        for i, nt in enumerate(tiles):
            units = []
            for k0 in range(0, kcn, 16):
                kn = min(16, kcn - k0)
                view, key = load_unit(Wt[nt][:, k0:k0 + kn, :], kn)
                units.append((k0, kn, view, key))
            for bi, (t0, n) in enumerate(blocks):
                b = bank()
                for (k0, kn, view, key) in units:
                    for k in range(kn):
                        mm(ps[b][:, 0:n], view[:, k, :], src(k0 + k, t0, n), (k0 + k) == 0, (k0 + k) == kcn - 1,
                           [key] + (src_keys(bi) if callable(src_keys) else src_keys), [f"ps{b}"])
                        cnt += 1
                        if cnt % every == 0:
                            yield
                consume(i, bi, ps[b][:, 0:n], f"ps{b}")

    def linear_fm(*a, **kw):
        for _ in linear_fm_gen(*a, **kw):
            pass

    def interleaved_gen(gens, pools=None):
        gens = [(g_, (pools[j_] if pools else None)) for j_, g_ in enumerate(gens)]
        while gens:
            nxt_ = []
            for g_, pl_ in gens:
                st['pool'] = pl_
                try:
                    next(g_)
                    nxt_.append((g_, pl_))
                except StopIteration:
                    pass
                st['pool'] = None
            gens = nxt_
            yield

    def run_interleaved(gens, pools=None):
        for _ in interleaved_gen(gens, pools):
            pass

    def linear_tm_gen(W2d, c0, ncols, src, src_keys, tiles, consume, every=8):
        view, key = load_unit(W2d.rearrange("(k p) n -> p k n", p=128)[:, :, c0:c0 + ncols], KC, ncols)
        cnt = 0
        for t in tiles:
            b = bank()
            for k in range(KC):
                mm(ps[b][:, 0:ncols], src(k, t), view[:, k, :], k == 0, k == KC - 1, [key] + src_keys, [f"ps{b}"])
                cnt += 1
                if cnt % every == 0:
                    yield
            consume(t, ps[b][:, 0:ncols], f"ps{b}")

    def linear_tm(*a, **kw):
        for _ in linear_tm_gen(*a, **kw):
            pass

    def slow_gen(gen, k):
        while True:
            for _ in range(k - 1):
                yield
            try:
                next(gen)
            except StopIteration:
                return
            yield

    def delay_gen(gen, n):
        for _ in range(n):
            yield
        yield from gen

    def sumsq_rstd(srcf, nkc, n, sqb, rstd, tmp, key, div_eps_scale=True):
        b = bank()
        for k in range(nkc):
            sq = sqb[k % len(sqb)]
            act(sq[:, 0:n], srcf(k), AF.Square, [key], [f"sq{id(sq)}"])
            mm(ps[b][:, 0:n], ONESB, sq[:, 0:n], k == 0, k == nkc - 1, ["CBF", f"sq{id(sq)}"], [f"ps{b}"])
        act(tmp[:, 0:n], ps[b][:, 0:n], AF.Ln, [f"ps{b}", "EPSC1"], [f"t{id(tmp)}"], bias=EPSC[:, 1:2], scale=1.0)
        act(rstd[:, 0:n], tmp[:, 0:n], AF.Exp, [f"t{id(tmp)}"], [f"r{id(rstd)}"], scale=-0.5)
        return f"r{id(rstd)}"

    M_XN = PEND
    M_MIX = M_XN + 33280
    M_SCR = M_MIX + 32768
    SCR_END = SBUF_END - B0
    xn = ar.at("xn", [128, KC, TT], BF16, M_XN)
    mixed = ar.at("mixed", [128, KC, T], BF16, M_MIX)
    XNK = [f"xn{i}" for i in range(5)]
    BLK3 = [(0, HALO), (HALO, 512), (HALO + 512, 512)]

    def XNKB(bi):
        return [["xn0"], ["xn1", "xn2"], ["xn3", "xn4"]][bi]

    def xsrc(k, t0, n):
        return xn[:, k, t0:t0 + n]

    def bc(ap, axis, shape):
        return ap.unsqueeze(axis).to_broadcast(list(shape))

    def conv_fm(raw, rkey, wcol0, ntap, bias, cacc, dst, func, dkey):
        ck = f"cacc{id(cacc)}"
        if bias is None:
            act(cacc[:, :], raw[:, HALO:TT], AF.Identity, rkey + ["COLP"], [ck], scale=COLP[:, wcol0 + ntap - 1: wcol0 + ntap])
        else:
            act(cacc[:, :], raw[:, HALO:TT], AF.Identity, rkey + ["COLP"], [ck], scale=COLP[:, wcol0 + ntap - 1: wcol0 + ntap], bias=bias)
        for k in range(ntap - 1):
            sh = HALO - (ntap - 1) + k
            stt(cacc[:, :], raw[:, sh:sh + T], COLP[:, wcol0 + k: wcol0 + k + 1], cacc[:, :], ALU.mult, ALU.add, rkey + [ck, "COLP"], [ck])
        if func is not None:
            act(dst, cacc[:, :], func, [ck], dkey)
        return ck

    def softplus_neg_scaled(dst, src, brow, arow, n_t, nh, tmp, keys_r, key_w):
        tt(tmp, src, bc(brow, 1, [128, n_t, nh]), ALU.add, keys_r + ["ROWP"], [key_w + "_t"])
        act(tmp, tmp, AF.Exp, [key_w + "_t"], [key_w + "_t"])
        act(dst, tmp, AF.Ln, [key_w + "_t"], [key_w], bias=1.0)

    def layer(l, x_src, halo_from_rcv, x_dst):
        Wl = w_in_fm[l]
        Wtm = w_in_tm[l]
        LC = l * 216
        p.barrier()
        S = Seq(ar, M_SCR, SCR_END)
        xstb = [S.a(f"xst{i}", [128, KC, 256], F32) for i in range(2)]
        sqb = [S.a(f"sqb{i}", [128, 512], BF16) for i in range(4)]
        rtmpb = [S.a(f"rtmp{i}", [128, 256], F32) for i in range(2)]
        rstdb = [S.a(f"rstd{i}", [128, 256], F32) for i in range(2)]
        xv = x_src.rearrange("(k p) t -> p k t", p=128)
        NB1 = [(0, HALO)] + [(HALO + 256 * i, 256) for i in range(4)]
        for bi, (t0, n) in enumerate(NB1):
            xst = xstb[bi % 2]
            xk = f"xst{bi % 2}"
            if bi == 0:
                if halo_from_rcv is None:
                    dma(xst[:, :, 0:HALO], xh_in.rearrange("(k p) t -> p k t", p=128), xk, [], [xk])
                else:
                    dma(xst[:, :, 0:HALO], rcv[halo_from_rcv].ap()[0:128, 0:256].rearrange("p (k t) -> p k t", k=KC), xk,
                        [f"rcv{halo_from_rcv}"], [xk])
                    ts(xst[:, :, 0:HALO], xst[:, :, 0:HALO], FLAG, None, ALU.mult, None, [xk, "MISC"], [xk])
            else:
                dma(xst[:, :, :], xv[:, :, (bi - 1) * 256: bi * 256], xk, [], [xk])
            rk = sumsq_rstd(lambda k: xst[:, k, 0:n], KC, n, sqb[2 * (bi % 2): 2 * (bi % 2) + 2], rstdb[bi % 2], rtmpb[bi % 2], xk)
            for k in range(KC):
                stt(xn[:, k, t0:t0 + n], xst[:, k, 0:n], gcol(l, 0, k), rstdb[bi % 2][:, 0:n], ALU.mult, ALU.mult,
                    [xk, rk, f"GS{l}"], [f"xn{bi}"])

        p.barrier()
        S = Seq(ar, M_SCR, SCR_END)
        craw = [S.a(f"craw{i}", [128, TT], F32) for i in range(2)]
        cacc = S.a("cacc", [128, T], F32)
        xc = S.a("xc", [128, 4, T], BF16)
        BT = S.a("BT", [128, 2, T], BF16)
        CT = S.a("CT", [128, 2, T], BF16)
        x_tm = S.a("x_tm", [128, NT, 512], BF16)
        B_tm = S.a("B_tm", [128, NT, 256], BF16)
        smD = S.a("smD", [128, NT, 8], F32)
        sm_t = S.a("sm_t", [128, NT, 8], F32)
        dtv = S.a("dtv", [128, NT, 8], F32)
        av = S.a("av", [128, NT, 8], F32)
        Arow = S.a("Arow", [128, 8], F32)
        eA = S.a("eA", [128, NT, 8], F32)
        dA = S.a("dA", [128, NT, 8], F32)
        statesT = S.a("statesT", [128, NT, 512], F32)
        zsD = S.a("zsD", [128, NT, 512], F32)
        DI = S.a("DI", [128, 8, 128], F32)
        hst = S.a("hst", [128, 512], F32)
        hb = S.a("hb", [128, 512], BF16)
        acst = S.a("acst", [128, 16], F32)
        te = S.a("te", [128, 8], F32)
        dec = S.a("dec", [128, 8], F32)
        Xdec = S.a("Xdec", [128, 512], BF16)
        gtri = S.a("gtri", [128, 8, 128], F32)
        E = S.a("E", [128, 8, 128], F32)
        CBTm = S.a("CBTm", [128, 2, 128], F32)
        WTb = S.a("WTb", [128, 8, 128], BF16)
        t1 = S.a("t1", [128, 512], F32)
        yv = S.a("yv", [128, 512], F32)
        yb = S.a("yb", [128, 512], BF16)
        ssq = S.a("ssq", [128, 4], F32)
        rs = S.a("rs", [128, 4], F32)

        def cons_xbc(i, bi, pa, pk):
            t0, n = BLK3[bi]
            r = craw[i % 2]
            cp(r[:, t0:t0 + n], pa, [pk], [f"craw{i % 2}_{bi}"])
            if bi == 2:
                dst, dk = (xc[:, i, :], "xc") if i < 4 else ((BT[:, i - 4, :], "BT") if i < 6 else (CT[:, i - 6, :], "CT"))
                conv_fm(r, [f"craw{i % 2}_{b_}" for b_ in range(3)], LC + 176 + i * 4, 4,
                        COLP[:, LC + 208 + i: LC + 209 + i], cacc, dst, AF.Silu, [dk + str(i)])
        linear_fm(Wl, list(range(0, 8)), KC, xsrc, XNKB, BLK3, cons_xbc)
        xck = [f"xc{i}" for i in range(4)]
        BTk = ["BT4", "BT5"]
        CTk = ["CT6", "CT7"]
        for t in range(NT):
            b = bank()
            psb = ps[b][:, :].bitcast(BF16)
            for i in range(4):
                tr(psb[:, i * 128:(i + 1) * 128], xc[:, i, t * 128:(t + 1) * 128], IDB, xck + ["CBF"], [f"ps{b}"])
            cp(x_tm[:, t, :], psb[:, 0:512], [f"ps{b}"], [f"x_tm{t}"])
            b = bank()
            psb = ps[b][:, :].bitcast(BF16)
            for g in range(2):
                tr(psb[:, g * 128:(g + 1) * 128], BT[:, g, t * 128:(t + 1) * 128], IDB, BTk + ["CBF"], [f"ps{b}"])
            cp(B_tm[:, t, :], psb[:, 0:256], [f"ps{b}"], [f"B_tm{t}"])
        linear_tm(Wtm, 1024, 8, lambda k, t: xn[:, k, HALO + t * 128: HALO + (t + 1) * 128], XNK, range(NT),
                  lambda t, pa, pk: cp(smD[:, t, :], pa, [pk], ["smD"]))
        softplus_neg_scaled(dtv[:, :, :], smD[:, :, :], rowp(l, 16, 8), None, NT, 8, sm_t[:, :, :], ["smD"], "dtv")
        act(Arow[:, :], rowp(l, 8, 8), AF.Exp, ["ROWP"], ["Arow"])
        ts(Arow[:, :], Arow[:, :], -1.0, None, ALU.mult, None, ["Arow"], ["Arow"])
        tt(av[:, :, :], dtv[:, :, :], bc(Arow[:, :], 1, [128, NT, 8]), ALU.mult, ["dtv", "Arow"], ["av"])
        tt(DI[:, :, :], bc(IDF, 1, [128, 8, 128]), bc(rowp(l, 24, 8), 2, [128, 8, 128]), ALU.mult, ["CF", "ROWP"], ["DI"])
        def zd_thread():
            for hf in range(4):
                yield from linear_tm_gen(Wtm, hf * 128, 128, lambda k, t: xn[:, k, HALO + t * 128: HALO + (t + 1) * 128], XNK, range(NT),
                                         lambda t, pa, pk, hf=hf: act(zsD[:, t, hf * 128:(hf + 1) * 128], pa, AF.Silu, [pk], [f"zsD{t}_{hf}"]))

        def states_thread():
            for t in range(NT):
                b = bank()
                mm(ps[b][:, 0:8], UF, av[:, t, :], True, True, ["CF", "av"], [f"ps{b}"])
                mm(ps[b][:, 8:16], ONESF, av[:, t, :], True, True, ["CF", "av"], [f"ps{b}"])
                yield
                cp(acst[:, :], ps[b][:, 0:16], [f"ps{b}"], ["acst"], eng='dve')
                yield
                act(eA[:, t, :], acst[:, 0:8], AF.Exp, ["acst"], [f"eA{t}"])
                act(dA[:, t, :], acst[:, 8:16], AF.Exp, ["acst"], [f"dA{t}"])
                tt(te[:, :], acst[:, 8:16], acst[:, 0:8], ALU.subtract, ["acst"], ["te"])
                yield
                act(te[:, :], te[:, :], AF.Exp, ["te"], ["te"])
                yield
                tt(dec[:, :], te[:, :], dtv[:, t, :], ALU.mult, ["te", "dtv"], ["dec"])
                yield
                tt(Xdec[:, :].rearrange("p (h q) -> p h q", h=8), x_tm[:, t, :].rearrange("p (h q) -> p h q", h=8),
                   bc(dec[:, :], 2, [128, 8, 64]), ALU.mult, ["dec", f"x_tm{t}"], ["Xdec"])
                yield
                b = bank()
                for g in range(2):
                    mm(ps[b][:, g * 256:(g + 1) * 256], B_tm[:, t, g * 128:(g + 1) * 128], Xdec[:, g * 256:(g + 1) * 256], True, True,
                       [f"B_tm{t}", "Xdec"], [f"ps{b}"])
                yield
                cp(statesT[:, t, :], ps[b][:, :], [f"ps{b}"], [f"st{t}"])
        run_interleaved([states_thread(), zd_thread()])

        def h_step(t):
            tt(hst[:, :].rearrange("p (h q) -> p h q", h=8), hst[:, :].rearrange("p (h q) -> p h q", h=8),
               bc(dA[:, t, :], 2, [128, 8, 64]), ALU.mult, ["hst", f"dA{t}"], ["hst"])
            tt(hst[:, :], hst[:, :], statesT[:, t, :], ALU.add, ["hst", f"st{t}"], ["hst"])
        memset(hst[:, :], 0.0, ["hst"])
        for t in range(NT):
            h_step(t)
        ex = 2 * l
        dma(snd[ex].ap()[:, :], hst[:, :], f"snd{ex}", ["hst"], [f"snd{ex}"])
        exchange(ex, [f"snd{ex}"], [f"rcv{ex}"])

        SA = Seq(ar, S.cur, SCR_END)
        ra = [SA.a(f"ra{i}", [128, TT], F32) for i in range(3)]
        p.barrier()
        R1 = Seq(ar, S.off["craw0"], S.off["craw0"] + 8320)
        R2 = Seq(ar, S.off["xc"], S.off["xc"] + 8192)
        R3 = Seq(ar, S.off["B_tm"], S.off["B_tm"] + 4096)
        TS = [dict(gtri=gtri, E=E, CBTm=CBTm, WTb=WTb, t1=t1, yv=yv, yb=yb, hb=hb, ssq=ssq, rs=rs),
              dict(gtri=R1.a("gtri1", [128, 8, 128], F32), E=R1.a("E1", [128, 8, 128], F32),
                   CBTm=R2.a("CBTm1", [128, 2, 128], F32), WTb=R2.a("WTb1", [128, 8, 128], BF16), t1=R2.a("t11", [128, 512], F32),
                   yv=R2.a("yv1", [128, 512], F32), yb=R2.a("yb1", [128, 512], BF16), hb=R3.a("hb1", [128, 512], BF16),
                   ssq=R3.a("ssq1", [128, 4], F32), rs=R3.a("rs1", [128, 4], F32))]

        def mixer_a():
            for j in range(4):
                def cons_a(i, bi, pa, pk, j=j):
                    t0, n = BLK3[bi]
                    cp(ra[i][:, t0:t0 + n], pa, [pk], [f"ra{i}_{bi}"])
                yield from linear_fm_gen(Wl, [20 + 3 * j, 21 + 3 * j, 22 + 3 * j], KC, xsrc, XNKB, BLK3, cons_a)
                rk = [f"ra{i}_{b_}" for i in range(3) for b_ in range(3)]
                tt(ra[0][:, :], ra[0][:, :], ra[1][:, :], ALU.mult, rk, ["ra0_0", "ra0_1", "ra0_2"])
                yield
                ck = conv_fm(ra[0], rk, LC + 112 + j * 3, 3, None, cacc, None, None, None)
                tt(mixed[:, j, :], cacc[:, :], ra[2][:, HALO:TT], ALU.mult, [ck] + rk, [f"mixed{j}"])
                yield

        def ssd_tile(t):
            f_ = t % 2
            Bf = TS[f_]
            gtri, E, CBTm, WTb, t1, yv, yb, hb, ssq, rs = (Bf[k_] for k_ in ("gtri", "E", "CBTm", "WTb", "t1", "yv", "yb", "hb", "ssq", "rs"))
            K = lambda n_: f"{n_}#{f_}"
            tsl = slice(t * 128, (t + 1) * 128)
            tt(gtri[:, :, :], bc(UF, 1, [128, 8, 128]), bc(av[:, t, :], 2, [128, 8, 128]), ALU.mult, ["CF", "av"], [K("gtri")])
            b = bank()
            for g in range(2):
                mm(ps[b][:, g * 128:(g + 1) * 128], BT[:, g, tsl], CT[:, g, tsl], True, True, BTk + CTk, [f"ps{b}"])
            yield
            tt(CBTm[:, :, :], ps[b][:, 0:256].rearrange("p (g l) -> p g l", g=2), bc(UF, 1, [128, 2, 128]), ALU.mult,
               [f"ps{b}", "CF"], [K("CBTm")])
            bq_ = []
            for q in range(2):
                b = bank()
                bq_.append(b)
                mm(ps[b][:, :], SLF, gtri[:, 4 * q:4 * q + 4, :].rearrange("p h l -> p (h l)"), True, True, ["CF", K("gtri")], [f"ps{b}"])
            yield
            for q in range(2):
                act(E[:, 4 * q:4 * q + 4, :].rearrange("p h l -> p (h l)"), ps[bq_[q]][:, :], AF.Exp, [f"ps{bq_[q]}"], [K(f"E{q}")])
            yield
            for g in range(2):
                tt(E[:, 4 * g:4 * g + 4, :], E[:, 4 * g:4 * g + 4, :], bc(CBTm[:, g, :], 1, [128, 4, 128]), ALU.mult,
                   [K(f"E{g}"), K("CBTm")], [K(f"E{g}")])
            yield
            tt(E[:, :, :], E[:, :, :], bc(dtv[:, t, :], 2, [128, 8, 128]), ALU.mult, [K("E0"), K("E1"), "dtv"], [K("E0"), K("E1")])
            yield
            tt(WTb[:, :, :], E[:, :, :], DI[:, :, :], ALU.add, [K("E0"), K("E1"), "DI"], [K("WTb")])
            yield
            by = bank()
            for h in range(8):
                mm(ps[by][:, h * 64:(h + 1) * 64], WTb[:, h, :], x_tm[:, t, h * 64:(h + 1) * 64], True, True, [K("WTb"), f"x_tm{t}"], [f"ps{by}"])
            if t == 0:
                dma(hst[:, :], rcv[ex].ap()[0:128, :], "hst", [f"rcv{ex}"], ["hst"])
                ts(hst[:, :], hst[:, :], FLAG, None, ALU.mult, None, ["hst", "MISC"], ["hst"])
            cp(hb[:, :], hst[:, :], ["hst"], [K("hb")], eng='act')
            h_step(t)
            yield
            bo = bank()
            for g in range(2):
                mm(ps[bo][:, g * 256:(g + 1) * 256], CT[:, g, tsl], hb[:, g * 256:(g + 1) * 256], True, True, CTk + [K("hb")], [f"ps{bo}"])
            yield
            tt(t1[:, :].rearrange("p (h q) -> p h q", h=8), ps[bo][:, :].rearrange("p (h q) -> p h q", h=8),
               bc(eA[:, t, :], 2, [128, 8, 64]), ALU.mult, [f"ps{bo}", f"eA{t}"], [K("t1")])
            yield
            tt(yv[:, :], ps[by][:, :], t1[:, :], ALU.add, [f"ps{by}", K("t1")], [K("yv")])
            yield
            tt(yv[:, :], yv[:, :], zsD[:, t, :], ALU.mult, [K("yv")] + [f"zsD{t}_{hf}" for hf in range(4)], [K("yv")])
            memset(ssq[:, :], 0.0, [K("ssq")])
            yield
            for g in range(2):
                act(t1[:, 0:256], yv[:, g * 256:(g + 1) * 256], AF.Square, [K("yv"), K("ssq")], [K("t1"), K("ssq")], accum=ssq[:, g:g + 1])
            yield
            act(rs[:, 0:2], ssq[:, 0:2], AF.Sqrt, [K("ssq"), "EPSC"], [K("rs")], bias=EPSC[:, 0:1], scale=1.0 / 256)
            yield
            recip(rs[:, 0:2], rs[:, 0:2], [K("rs")], [K("rs")])
            yield
            for g in range(2):
                stt(yb[:, g * 256:(g + 1) * 256], yv[:, g * 256:(g + 1) * 256], rs[:, g:g + 1], rowp(l, 160 + g * 256, 256),
                    ALU.mult, ALU.mult, [K("yv"), K("rs"), "ROWP"], [K("yb")])
            yield
            b = bank()
            psb = ps[b][:, :].bitcast(BF16)
            for c in range(4):
                tr(psb[:, c * 128:(c + 1) * 128], yb[:, c * 128:(c + 1) * 128], IDB, [K("yb"), "CBF"], [f"ps{b}"])
            yield
            cp(mixed[:, 12:16, tsl], psb[:, 0:512].rearrange("p (c n) -> p c n", c=4), [f"ps{b}"], [f"mixedD{t}"])

        def ssd_pairs():
            for t0_ in range(0, NT, 2):
                yield from interleaved_gen([ssd_tile(t0_), ssd_tile(t0_ + 1)])
        run_interleaved([ssd_pairs(), mixer_a()])

        p.barrier()
        G2 = Seq(ar, M_SCR, M_SCR + 50688)
        G1 = Seq(ar, M_SCR + 50688, SCR_END)
        qT = G2.a("qT", [128, 4, T], BF16)
        vb = G2.a("vb", [128, NT, 512], BF16)
        TTb = G2.a("TTb", [128, NT, 512], BF16)
        wT = G2.a("wT", [128, NT, 512], BF16)
        qkTm = G2.a("qkTm", [128, NT, 512], BF16)
        k_end = G2.a("k_end", [128, NT, 512], BF16)
        eG = G2.a("eG", [128, NT, 4], F32)
        dG = G2.a("dG", [128, NT, 4], F32)
        smC = G2.a("smC", [128, NT, 8], F32)
        smt = G2.a("smt", [128, NT, 4], F32)
        gv = G2.a("gv", [128, NT, 4], F32)
        beta = G2.a("beta", [128, NT, 4], F32)
        gArow = G2.a("gArow", [128, 4], F32)
        gt = G2.a("gt", [128, 8], F32)
        ke = G2.a("ke", [128, 4], F32)
        bg = G2.a("bg", [128, 4], F32)
        kT = G1.a("kT", [128, 4, T], BF16)
        k_tm = G1.a("k_tm", [128, NT, 512], BF16)
        G1_TMP = G1.cur
        craw = [G1.a(f"gcraw{i}", [128, TT], F32) for i in range(2)]
        caccs = [G1.a(f"gcacc{i}", [128, T], F32) for i in range(2)]
        qfs = [G1.a(f"qf{i}", [128, T], F32) for i in range(2)]
        vT = G1.a("vT", [128, 4, T], BF16)
        sqg = [G1.a(f"sqg{i}", [128, 512], BF16) for i in range(2)]
        rtgs = [G1.a(f"rtg{i}", [128, 512], F32) for i in range(2)]
        rsgs = [G1.a(f"rsg{i}", [128, 512], F32) for i in range(2)]

        pend = []

        def flush_pend():
            while pend:
                pend.pop(0)()

        def cons_qkv(i, bi, pa, pk):
            t0, n = BLK3[bi]
            r = craw[i % 2]
            cacc = caccs[i % 2]
            qf = qfs[i % 2]
            qk_ = f"qf{i % 2}"
            cp(r[:, t0:t0 + n], pa, [pk], [f"gcraw{i % 2}_{bi}"])
            if bi != 2:
                return
            flush_pend()
            rk = [f"gcraw{i % 2}_{b_}" for b_ in range(3)]
            h = i % 4
            if i >= 8:
                conv_fm(r, rk, LC + 128 + i * 4, 4, None, cacc, vT[:, h, :], AF.Silu, [f"vT{h}"])
                return
            conv_fm(r, rk, LC + 128 + i * 4, 4, None, cacc, qf[:, :], AF.Silu, [qk_])
            for hb_ in range(2):
                sl = slice(hb_ * 512, (hb_ + 1) * 512)
                act(sqg[hb_][:, :], qf[:, sl], AF.Square, [qk_], [f"sqg{hb_}"])

            def l2n(i=i, h=h, qf=qf, qk_=qk_):
                for hb_ in range(2):
                    sl = slice(hb_ * 512, (hb_ + 1) * 512)
                    sq = sqg[hb_]
                    rtg, rsg = rtgs[hb_], rsgs[hb_]
                    b = bank()
                    mm(ps[b][:, :], ONESB, sq[:, :], True, True, ["CBF", f"sqg{hb_}"], [f"ps{b}"])
                    act(rtg[:, :], ps[b][:, :], AF.Ln, [f"ps{b}", "EPSC"], [f"rtg{hb_}"], bias=EPSC[:, 0:1], scale=1.0)
                    act(rsg[:, :], rtg[:, :], AF.Exp, [f"rtg{hb_}"], [f"rsg{hb_}"], scale=-0.5)
                    if i < 4:
                        stt(qT[:, h, sl], qf[:, sl], float(128 ** -0.5), rsg[:, :], ALU.mult, ALU.mult, [qk_, f"rsg{hb_}"], [f"qT{h}"])
                    else:
                        tt(kT[:, h, sl], qf[:, sl], rsg[:, :], ALU.mult, [qk_, f"rsg{hb_}"], [f"kT{h}"])
            pend.append(l2n)
        linear_fm(Wl, list(range(8, 20)), KC, xsrc, XNKB, BLK3, cons_qkv)
        flush_pend()
        qTk = [f"qT{h}" for h in range(4)]
        kTk = [f"kT{h}" for h in range(4)]
        vTk = [f"vT{h}" for h in range(4)]
        for t in range(NT):
            for (src_, sk, dst_, dk) in ((kT, kTk, k_tm, "k_tm"), (vT, vTk, vb, "vb")):
                b = bank()
                psb = ps[b][:, :].bitcast(BF16)
                for h in range(4):
                    tr(psb[:, h * 128:(h + 1) * 128], src_[:, h, t * 128:(t + 1) * 128], IDB, sk + ["CBF"], [f"ps{b}"])
                cp(dst_[:, t, :], psb[:, 0:512], [f"ps{b}"], [f"{dk}{t}"])
        linear_tm(Wtm, 1032, 8, lambda k, t: xn[:, k, HALO + t * 128: HALO + (t + 1) * 128], XNK, range(NT),
                  lambda t, pa, pk: cp(smC[:, t, :], pa, [pk], ["smC"]))
        softplus_neg_scaled(gv[:, :, :], smC[:, :, 0:4], rowp(l, 4, 4), None, NT, 4, smt[:, :, :], ["smC"], "gv")
        act(gArow[:, :], rowp(l, 0, 4), AF.Exp, ["ROWP"], ["gArow"])
        ts(gArow[:, :], gArow[:, :], -1.0, None, ALU.mult, None, ["gArow"], ["gArow"])
        tt(gv[:, :, :], gv[:, :, :], bc(gArow[:, :], 1, [128, NT, 4]), ALU.mult, ["gv", "gArow"], ["gv"])
        act(beta[:, :, :], smC[:, :, 4:8], AF.Sigmoid, ["smC"], ["beta"])

        def v4(ap):
            return ap.rearrange("p (h n) -> p h n", h=4)
        p.barrier()
        G1t = Seq(ar, G1_TMP, SCR_END)
        NFL = 2
        TB = []
        for f_ in range(NFL):
            TB.append(dict(
                gtU=G1t.a(f"gtU{f_}", [128, 4, 128], F32), gtS=G1t.a(f"gtS{f_}", [128, 4, 128], F32),
                Eg=G1t.a(f"Eg{f_}", [128, 4, 128], F32), ETg=G1t.a(f"ETg{f_}", [128, 4, 128], F32),
                Pm=[G1t.a(f"Pm{f_}_{i}", [128, 4, 128], F32) for i in range(2)],
                PTm=[G1t.a(f"PTm{f_}_{i}", [128, 4, 128], F32) for i in range(2)],
                RT=G1t.a(f"RT{f_}", [128, 4, 128], F32), kbe=G1t.a(f"kbe{f_}", [128, 512], BF16),
                gt=G1t.a(f"gt{f_}", [128, 8], F32), ke=G1t.a(f"ke{f_}", [128, 4], F32), bg=G1t.a(f"bg{f_}", [128, 4], F32)))

        def gdn_tile(t):
            f_ = t % NFL
            Bf = TB[f_]
            gtU, gtS, Eg, ETg, Pm, PTm, RT, kbe, gt, ke, bg = (Bf[k_] for k_ in ("gtU", "gtS", "Eg", "ETg", "Pm", "PTm", "RT", "kbe", "gt", "ke", "bg"))
            K = lambda n_: f"{n_}#{f_}"
            tsl = slice(t * 128, (t + 1) * 128)
            b = bank()
            mm(ps[b][:, 0:4], UF, gv[:, t, :], True, True, ["CF", "gv"], [f"ps{b}"])
            mm(ps[b][:, 4:8], ONESF, gv[:, t, :], True, True, ["CF", "gv"], [f"ps{b}"])
            cp(gt[:, :], ps[b][:, 0:8], [f"ps{b}"], [K("gt")], eng='dve')
            tt(gtU[:, :, :], bc(UF, 1, [128, 4, 128]), bc(gv[:, t, :], 2, [128, 4, 128]), ALU.mult, ["CF", "gv"], [K("gtU")])
            tt(gtS[:, :, :], bc(SLF, 1, [128, 4, 128]), bc(gv[:, t, :], 2, [128, 4, 128]), ALU.mult, ["CF", "gv"], [K("gtS")])
            yield
            act(eG[:, t, :], gt[:, 0:4], AF.Exp, [K("gt")], [f"eG{t}"])
            act(dG[:, t, :], gt[:, 4:8], AF.Exp, [K("gt")], [f"dG{t}"])
            tt(ke[:, :], gt[:, 4:8], gt[:, 0:4], ALU.subtract, [K("gt")], [K("ke")])
            act(ke[:, :], ke[:, :], AF.Exp, [K("ke")], [K("ke")])
            b1 = bank()
            mm(ps[b1][:, :], SLF, gtU[:, :, :].rearrange("p h l -> p (h l)"), True, True, ["CF", K("gtU")], [f"ps{b1}"])
            b2 = bank()
            mm(ps[b2][:, :], UF, gtS[:, :, :].rearrange("p h l -> p (h l)"), True, True, ["CF", K("gtS")], [f"ps{b2}"])
            bk = bank()
            for h in range(4):
                mm(ps[bk][:, h * 128:(h + 1) * 128], kT[:, h, tsl], kT[:, h, tsl], True, True, kTk, [f"ps{bk}"])
            bq = bank()
            for h in range(4):
                mm(ps[bq][:, h * 128:(h + 1) * 128], kT[:, h, tsl], qT[:, h, tsl], True, True, kTk + qTk, [f"ps{bq}"])
            yield
            tt(bg[:, :], beta[:, t, :], eG[:, t, :], ALU.mult, ["beta", f"eG{t}"], [K("bg")])
            act(ETg[:, :, :].rearrange("p h l -> p (h l)"), ps[b1][:, :], AF.Exp, [f"ps{b1}"], [K("ETg")])
            act(Eg[:, :, :].rearrange("p h l -> p (h l)"), ps[b2][:, :], AF.Exp, [f"ps{b2}"], [K("Eg")])
            yield
            tt(Eg[:, :, :], Eg[:, :, :], bc(SLF, 1, [128, 4, 128]), ALU.mult, [K("Eg"), "CF"], [K("Eg")])
            tt(Eg[:, :, :], Eg[:, :, :], bc(beta[:, t, :], 2, [128, 4, 128]), ALU.mult, [K("Eg"), "beta"], [K("Eg")])
            tt(Pm[0][:, :, :], v4(ps[bk][:, :]), Eg[:, :, :], ALU.mult, [f"ps{bk}", K("Eg")], [K("Pm0")])
            yield
            bt_ = bank()
            for h in range(4):
                tr(ps[bt_][:, h * 128:(h + 1) * 128], Pm[0][:, h, :], IDF, [K("Pm0"), "CF"], [f"ps{bt_}"])
            tt(ETg[:, :, :], ETg[:, :, :], bc(UF, 1, [128, 4, 128]), ALU.mult, [K("ETg"), "CF"], [K("ETg")])
            tt(v4(qkTm[:, t, :]), v4(ps[bq][:, :]), ETg[:, :, :], ALU.mult, [f"ps{bq}", K("ETg")], [f"qkTm{t}"])
            yield
            cp(PTm[0][:, :, :], v4(ps[bt_][:, :]), [f"ps{bt_}"], [K("PTm0")], eng='act')
            tt(RT[:, :, :], bc(IDF, 1, [128, 4, 128]), v4(ps[bt_][:, :]), ALU.subtract, ["CF", f"ps{bt_}"], [K("RT")])
            yield
            cur = 0
            for kstep in range(6):
                nxt = 1 - cur
                ba = bank()
                for h in range(4):
                    mm(ps[ba][:, h * 128:(h + 1) * 128], PTm[cur][:, h, :], Pm[cur][:, h, :], True, True,
                       [K(f"Pm{cur}"), K(f"PTm{cur}")], [f"ps{ba}"])
                if kstep < 5:
                    bb = bank()
                    for h in range(4):
                        mm(ps[bb][:, h * 128:(h + 1) * 128], Pm[cur][:, h, :], PTm[cur][:, h, :], True, True,
                           [K(f"Pm{cur}"), K(f"PTm{cur}")], [f"ps{bb}"])
                yield
                cp(Pm[nxt][:, :, :], v4(ps[ba][:, :]), [f"ps{ba}"], [K(f"Pm{nxt}")], eng='act')
                if kstep < 5:
                    cp(PTm[nxt][:, :, :], v4(ps[bb][:, :]), [f"ps{bb}"], [K(f"PTm{nxt}")], eng='dve')
                yield
                bc_ = bank()
                for h in range(4):
                    mm(ps[bc_][:, h * 128:(h + 1) * 128], Pm[nxt][:, h, :], RT[:, h, :], True, True, [K(f"Pm{nxt}"), K("RT")], [f"ps{bc_}"])
                yield
                tt(RT[:, :, :], RT[:, :, :], v4(ps[bc_][:, :]), ALU.add, [K("RT"), f"ps{bc_}"], [K("RT")])
                cur = nxt
            cp(v4(TTb[:, t, :]), RT[:, :, :], [K("RT")], [f"TTb{t}"], eng='act')
            tt(v4(vb[:, t, :]), v4(vb[:, t, :]), bc(beta[:, t, :], 2, [128, 4, 128]), ALU.mult, [f"vb{t}", "beta"], [f"vb{t}"])
            tt(v4(kbe[:, :]), v4(k_tm[:, t, :]), bc(bg[:, :], 2, [128, 4, 128]), ALU.mult, [f"k_tm{t}", K("bg")], [K("kbe")])
            tt(v4(k_end[:, t, :]), v4(k_tm[:, t, :]), bc(ke[:, :], 2, [128, 4, 128]), ALU.mult, [f"k_tm{t}", K("ke")], [f"k_end{t}"])
            yield
            bw = bank()
            for h in range(4):
                mm(ps[bw][:, h * 128:(h + 1) * 128], kbe[:, h * 128:(h + 1) * 128], TTb[:, t, h * 128:(h + 1) * 128], True, True,
                   [K("kbe"), f"TTb{t}"], [f"ps{bw}"])
            yield
            act(wT[:, t, :], ps[bw][:, :], AF.Copy, [f"ps{bw}"], [f"wT{t}"], scale=-1.0)

        for t0_ in range(0, NT, NFL):
            run_interleaved([gdn_tile(t0_ + i_) for i_ in range(NFL)])

        p.barrier()
        G1b = Seq(ar, M_SCR + 50688, SCR_END)
        gz = G1b.a("gz", [128, NT, 512], F32)
        Sst = G1b.a("Sst", [128, 512], F32)
        Sb = G1b.a("Sb", [128, 512], BF16)
        Sn = G1b.a("Sn", [128, 512], BF16)
        vnew = G1b.a("vnew", [128, 512], BF16)
        ovs = [G1b.a(f"ov{i}", [128, 512], F32) for i in range(2)]
        otmp = G1b.a("otmp", [128, 512], F32)
        ob = G1b.a("ob", [128, 512], BF16)
        ssq4 = G1b.a("ssq4", [128, 4], F32)
        rs4 = G1b.a("rs4", [128, 4], F32)
        junk2 = G1b.a("junk2", [128, 128], F32)
        def gz_thread():
            for hf in range(4):
                def cons_z(t, pa, pk, hf=hf):
                    act(gz[:, t, hf * 128:(hf + 1) * 128], pa, AF.Silu, [pk], [f"gz{t}_{hf}"])
                    tt(gz[:, t, hf * 128:(hf + 1) * 128], gz[:, t, hf * 128:(hf + 1) * 128], rowp(l, 32, 128), ALU.mult,
                       [f"gz{t}_{hf}", "ROWP"], [f"gz{t}_{hf}"])
                yield from linear_tm_gen(Wtm, 512 + hf * 128, 128, lambda k, t: xn[:, k, HALO + t * 128: HALO + (t + 1) * 128], XNK,
                                         range(NT), cons_z)

        def s_refresh():
            cp(Sb[:, :], Sst[:, :], ["Sst"], ["Sb"], eng='act')

        def scan_gen(full, from_rcv=None):
            if from_rcv is not None:
                dma(Sst[:, :], rcv[from_rcv].ap()[0:128, :], "Sst", [f"rcv{from_rcv}"], ["Sst"])
                ts(Sst[:, :], Sst[:, :], FLAG, None, ALU.mult, None, ["Sst", "MISC"], ["Sst"])
                s_refresh()
                yield
            for t in range(NT):
                tsl = slice(t * 128, (t + 1) * 128)
                bv = bank()
                for h in range(4):
                    hs = slice(h * 128, (h + 1) * 128)
                    mm(ps[bv][:, hs], TTb[:, t, hs], vb[:, t, hs], True, False, [f"TTb{t}", f"vb{t}"], [f"ps{bv}"])
                    mm(ps[bv][:, hs], wT[:, t, hs], Sb[:, hs], False, True, [f"wT{t}", "Sb"], [f"ps{bv}"])
                if full:
                    bo = bank()
                    for h in range(4):
                        hs = slice(h * 128, (h + 1) * 128)
                        mm(ps[bo][:, hs], qT[:, h, tsl], Sb[:, hs], True, True, qTk + ["Sb"], [f"ps{bo}"])
                yield
                cp(vnew[:, :], ps[bv][:, :], [f"ps{bv}"], ["vnew"], eng='act')
                yield
                bs = bank()
                for h in range(4):
                    hs = slice(h * 128, (h + 1) * 128)
                    mm(ps[bs][:, hs], k_end[:, t, hs], vnew[:, hs], True, True, [f"k_end{t}", "vnew"], [f"ps{bs}"])
                if full:
                    bo2 = bank()
                    for h in range(4):
                        hs = slice(h * 128, (h + 1) * 128)
                        mm(ps[bo2][:, hs], qkTm[:, t, hs], vnew[:, hs], True, True, [f"qkTm{t}", "vnew"], [f"ps{bo2}"])
                yield
                for h in range(4):
                    hs = slice(h * 128, (h + 1) * 128)
                    stt(Sst[:, hs], Sst[:, hs], dG[:, t, h:h + 1], ps[bs][:, hs], ALU.mult, ALU.add, ["Sst", f"dG{t}", f"ps{bs}"], ["Sst"])
                yield
                s_refresh()
                if full:
                    ovt = ovs[t % 2]
                    tt(v4(otmp[:, :]), v4(ps[bo][:, :]), bc(eG[:, t, :], 2, [128, 4, 128]), ALU.mult, [f"ps{bo}", f"eG{t}"], ["otmp"])
                    yield
                    while fin_state['done'] < t - 1:
                        yield
                    tt(ovt[:, :], ps[bo2][:, :], otmp[:, :], ALU.add, [f"ps{bo2}", "otmp"], [f"ov{t % 2}"])
                    fin_state['ready'] = t + 1
                yield

        fin_state = {'ready': 0, 'done': 0}

        def fin_thread():
            for t in range(NT):
                while fin_state['ready'] <= t:
                    yield
                tsl = slice(t * 128, (t + 1) * 128)
                ovt = ovs[t % 2]
                ok_ = f"ov{t % 2}"
                memset(ssq4[:, :], 0.0, ["ssq4"])
                for h in range(4):
                    act(junk2[:, :], ovt[:, h * 128:(h + 1) * 128], AF.Square, [ok_, "ssq4"], ["junk2", "ssq4"], accum=ssq4[:, h:h + 1])
                yield
                act(rs4[:, :], ssq4[:, :], AF.Sqrt, ["ssq4", "EPSC"], ["rs4"], bias=EPSC[:, 0:1], scale=1.0 / 128)
                yield
                recip(rs4[:, :], rs4[:, :], ["rs4"], ["rs4"])
                yield
                for h in range(4):
                    hs = slice(h * 128, (h + 1) * 128)
                    stt(ob[:, hs], ovt[:, hs], rs4[:, h:h + 1], gz[:, t, hs], ALU.mult, ALU.mult,
                        [ok_, "rs4"] + [f"gz{t}_{hf}" for hf in range(4)], ["ob"])
                fin_state['done'] = t + 1
                yield
                b = bank()
                psb = ps[b][:, :].bitcast(BF16)
                for c in range(4):
                    tr(psb[:, c * 128:(c + 1) * 128], ob[:, c * 128:(c + 1) * 128], IDB, ["ob", "CBF"], [f"ps{b}"])
                yield
                cp(mixed[:, 8:12, tsl], psb[:, 0:512].rearrange("p (c n) -> p c n", c=4), [f"ps{b}"], [f"mixedC{t}"])
                yield

        memset(Sst[:, :], 0.0, ["Sst"])
        s_refresh()
        run_interleaved([scan_gen(False), gz_thread()])
        ex = 2 * l + 1
        dma(snd[ex].ap()[:, :], Sst[:, :], f"snd{ex}", ["Sst"], [f"snd{ex}"])
        exchange(ex, [f"snd{ex}"], [f"rcv{ex}"])

        SB = Seq(ar, G1b.cur, SCR_END)
        rb = SB.a("rb", [128, TT], F32)
        lv = [SB.a(f"lv{i}", [128, TT], F32) for i in range(2)]
        pl = SB.a("pl", [128, T], F32)
        plb = SB.a("plb", [128, T], BF16)
        pwb = SB.a("pwb", [128, 128], BF16)
        pwf = SB.a("pwf", [128, 128], F32)

        def mixer_b():
            for gi in range(4):
                def cons_b(i, bi, pa, pk):
                    t0, n = BLK3[bi]
                    cp(rb[:, t0:t0 + n], pa, [pk], [f"rb_{bi}"])
                yield from linear_fm_gen(Wl, [32 + gi], KC, xsrc, XNKB, BLK3, cons_b)
                rk = [f"rb_{b_}" for b_ in range(3)]
                w = 2 ** (gi + 1)
                srcb, sk, lo = rb, rk, 0
                for lev in range(gi + 1):
                    sh = 2 ** lev
                    dstb = lv[lev % 2]
                    nlo = lo + sh
                    tt(dstb[:, nlo:TT], srcb[:, nlo:TT], srcb[:, nlo - sh:TT - sh], ALU.add, sk, [f"lv{lev % 2}"])
                    srcb, sk, lo = dstb, [f"lv{lev % 2}"], nlo
                    yield
                stt(pl[:, :], srcb[:, HALO:TT], 1.0 / w, rb[:, HALO:TT], ALU.mult, ALU.subtract, sk + rk, ["pl"])
                tt(lv[(gi + 1) % 2][:, 0:HALO], srcb[:, HALO:2 * HALO], MISC[:, 16 + gi * 16: 32 + gi * 16], ALU.mult, sk + ["MISC"],
                   [f"lv{(gi + 1) % 2}"])
                yield
                tt(pl[:, 0:HALO], lv[(gi + 1) % 2][:, 0:HALO], rb[:, HALO:2 * HALO], ALU.subtract, [f"lv{(gi + 1) % 2}", "pl"] + rk, ["pl"])
                dma(pwf[:, :], pool_w[l, gi], "pwf", [], ["pwf"])
                yield
                cp(plb[:, :], pl[:, :], ["pl"], ["plb"], eng='act')
                cp(pwb[:, :], pwf[:, :], ["pwf"], ["pwb"], eng='act')
                yield
                for hb_ in range(2):
                    b = bank()
                    mm(ps[b][:, :], pwb[:, :], plb[:, hb_ * 512:(hb_ + 1) * 512], True, True, ["pwb", "plb"], [f"ps{b}"])
                    act(mixed[:, 4 + gi, hb_ * 512:(hb_ + 1) * 512], ps[b][:, :], AF.Identity, [f"ps{b}", "COLP"], [f"mixedB{gi}_{hb_}"],
                        scale=COLP[:, LC + 124 + gi: LC + 125 + gi])
                yield

        run_interleaved([delay_gen(scan_gen(True, from_rcv=ex), 24), fin_thread(), slow_gen(mixer_b(), 2)],
                        pools=[[0, 1, 2, 3], [4], [5, 6, 7]])

        p.barrier()
        R = Seq(ar, M_SCR, SCR_END)
        x = R.a("x", [128, KC, T], F32)
        y = R.a("y", [128, KC, 512], F32)
        sqb = [R.a(f"rsqb{i}", [128, 512], BF16) for i in range(3)]
        rtmp = R.a("rrtmp", [128, 512], F32)
        rstd = R.a("rrstd", [128, 512], F32)
        mnT = R.a("mnT", [128, KC, MEM], BF16)
        eT = [ar.at(f"eT{i}", [128, 2, 512], BF16, R.off["mnT"] + 2048 * i) for i in range(2)]
        rden = ar.at("rden", [128, 512], F32, R.off["mnT"] + 4096)
        vTf = [R.a(f"vTf{i}", [128, MEM], BF16) for i in range(2)]
        xn2 = ar.at("xn2", [128, KC, 512], BF16, M_XN)
        q = ar.at("q", [128, KC, 512], BF16, M_XN + 16384)
        memst = ar.at("memst", [128, KC, MEM], F32, M_XN + 16384)
        o = ar.at("o", [128, KC, 512], BF16, M_MIX)
        kTm = ar.at("kTm", [128, KC, MEM], BF16, M_MIX + 16384)
        vm = ar.at("vm", [128, 2, D], BF16, M_MIX + 24576)
        hff = ar.at("hff", [128, FKC, 512], BF16, M_XN + 16384)
        sg = rden
        xv = x_src.rearrange("(k p) t -> p k t", p=128)
        xdv = x_dst.rearrange("(k p) t -> p k t", p=128)

        def post_norm_add(row, tb):
            rk = sumsq_rstd(lambda k: y[:, k, :], KC, 512, sqb, rstd, rtmp, "y")
            for k in range(KC):
                stt(y[:, k, :], y[:, k, :], gcol(l, row, k), rstd[:, :], ALU.mult, ALU.mult, ["y", rk, f"GS{l}"], ["y"])
                tt(x[:, k, tb * 512:(tb + 1) * 512], x[:, k, tb * 512:(tb + 1) * 512], y[:, k, :], ALU.add, [f"x{tb}", "y"], [f"x{tb}"])

        def pre_norm(row, tb, dst, dkey):
            rk = sumsq_rstd(lambda k: x[:, k, tb * 512:(tb + 1) * 512], KC, 512, sqb, rstd, rtmp, f"x{tb}")
            for k in range(KC):
                stt(dst[:, k, :], x[:, k, tb * 512:(tb + 1) * 512], gcol(l, row, k), rstd[:, :], ALU.mult, ALU.mult,
                    [f"x{tb}", rk, f"GS{l}"], [dkey])

        def cons_y(i, bi, pa, pk):
            cp(y[:, i, :], pa, [pk], ["y"])

        for tb in range(2):
            dma(x[:, :, tb * 512:(tb + 1) * 512], xv[:, :, tb * 512:(tb + 1) * 512], f"xld{tb}", [], [f"x{tb}"])
        dma(memst[:, :, :], memT_in.rearrange("(k p) t -> p k t", p=128), "memst", [], ["memst"])
        rk = sumsq_rstd(lambda k: memst[:, k, :], KC, MEM, sqb, rstd, rtmp, "memst")
        for k in range(KC):
            stt(mnT[:, k, :], memst[:, k, :], gcol(l, 4, k), rstd[:, 0:MEM], ALU.mult, ALU.mult, ["memst", rk, f"GS{l}"], ["mnT"])
        def w_out_tb(tb):
            linear_fm(w_out[l], list(range(KC)), KC, lambda k, t0, n, tb=tb: mixed[:, k, tb * 512:(tb + 1) * 512], ["mixedR"],
                      [(0, 512)], cons_y)
        w_out_tb(0)
        post_norm_add(1, 0)
        w_out_tb(1)

        linear_fm(xa_wk[l], list(range(KC)), KC, lambda k, t0, n: mnT[:, k, :], ["mnT"], [(0, MEM)],
                  lambda i, bi, pa, pk: cp(kTm[:, i, :], pa, [pk], ["kTm", "mixedR"]))
        post_norm_add(1, 1)
        pre_norm(2, 0, xn2, "xn2")
        pendv = []

        def cons_v(i, bi, pa, pk):
            buf = vTf[i % 2]
            cp(buf[:, :], pa, [pk], [f"vTf{i % 2}"])
            while pendv:
                pendv.pop(0)()

            def trs(i=i, buf=buf):
                b = bank()
                psb = ps[b][:, :].bitcast(BF16)
                for mt in range(2):
                    tr(psb[:, mt * 128:(mt + 1) * 128], buf[:, mt * 128:(mt + 1) * 128], IDB, [f"vTf{i % 2}", "CBF"], [f"ps{b}"])
                cp(vm[:, :, i * 128:(i + 1) * 128], psb[:, 0:256].rearrange("p (m n) -> p m n", m=2), [f"ps{b}"], ["vm", "mixedR"])
            pendv.append(trs)
        linear_fm(xa_wv[l], list(range(KC)), KC, lambda k, t0, n: mnT[:, k, :], ["mnT"], [(0, MEM)], cons_v)
        while pendv:
            pendv.pop(0)()

        p.barrier()
        for tb in range(2):
            linear_fm(xa_wq[l], list(range(KC)), KC, lambda k, t0, n: xn2[:, k, :], ["xn2"], [(0, 512)],
                      lambda i, bi, pa, pk: cp(q[:, i, :], pa, [pk], ["q"]))
            if tb == 0:
                pre_norm(2, 1, xn2, "xn2")
            else:
                pre_norm(5, 0, xn2, "xn2")
            def scores(h):
                e_ = eT[h % 2]
                ek = f"eT{h % 2}"
                for mt in range(2):
                    b = bank()
                    for c in range(4):
                        mm(ps[b][:, :], kTm[:, h * 4 + c, mt * 128:(mt + 1) * 128], q[:, h * 4 + c, :], c == 0, c == 3,
                           ["kTm", "q"], [f"ps{b}"])
                    act(e_[:, mt, :], ps[b][:, :], AF.Exp, [f"ps{b}"], [ek], scale=float(512 ** -0.5))
            scores(0)
            for h in range(4):
                e_ = eT[h % 2]
                ek = f"eT{h % 2}"
                if h + 1 < 4:
                    scores(h + 1)
                bd = bank()
                for mt in range(2):
                    mm(ps[bd][:, :], ONESB, e_[:, mt, :], mt == 0, mt == 1, ["CBF", ek], [f"ps{bd}"])
                act(rden[:, :], ps[bd][:, :], AF.Ln, [f"ps{bd}"], ["rden"])
                act(rden[:, :], rden[:, :], AF.Exp, ["rden"], ["rden"], scale=-1.0)
                for c in range(4):
                    b = bank()
                    for mt in range(2):
                        mm(ps[b][:, :], vm[:, mt, h * 512 + c * 128: h * 512 + (c + 1) * 128], e_[:, mt, :], mt == 0, mt == 1,
                           ["vm", ek], [f"ps{b}"])
                    tt(o[:, h * 4 + c, :], ps[b][:, :], rden[:, :], ALU.mult, [f"ps{b}", "rden"], ["o"])
            linear_fm(xa_wo[l], list(range(KC)), KC, lambda k, t0, n: o[:, k, :], ["o"], [(0, 512)], cons_y)
            post_norm_add(3, tb)

        for tb in range(2):
            for j in range(FKC):
                vg, kg = load_unit(w_gu[l][j], KC)
                vu, ku = load_unit(w_gu[l][FKC + j], KC)
                bg_ = bank()
                for k in range(KC):
                    mm(ps[bg_][:, :], vg[:, k, :], xn2[:, k, :], k == 0, k == KC - 1, [kg, "xn2"], [f"ps{bg_}"])
                bu_ = bank()
                for k in range(KC):
                    mm(ps[bu_][:, :], vu[:, k, :], xn2[:, k, :], k == 0, k == KC - 1, [ku, "xn2"], [f"ps{bu_}"])
                act(sg[:, :], ps[bg_][:, :], AF.Silu, [f"ps{bg_}"], ["sg", "rden"])
                tt(hff[:, j, :], sg[:, :], ps[bu_][:, :], ALU.mult, ["sg", f"ps{bu_}"], ["hff", "q", "o", "kTm", "vm"])
            if tb == 0:
                pre_norm(5, 1, xn2, "xn2")
            linear_fm(w_down[l], list(range(KC)), FKC, lambda k, t0, n: hff[:, k, :], ["hff"], [(0, 512)], cons_y)
            post_norm_add(6, tb)
            dma(xdv[:, :, tb * 512:(tb + 1) * 512], x[:, :, tb * 512:(tb + 1) * 512], f"xst{tb}", [f"x{tb}"], [f"xdst{tb}"])
        if l + 1 < n_layers:
            dma(snd[4].ap()[:, 0:256].rearrange("p (k t) -> p k t", k=KC), x[:, :, T - HALO:T], "snd4", ["x1"], ["snd4"])
            exchange(4, ["snd4"], ["rcv4"])
        return [f"xst{tb}" for tb in range(2)]

    final_streams = None
    for l in range(n_layers):
        last = (l == n_layers - 1)
        final_streams = layer(l, xT_in if l == 0 else xs, None if l == 0 else 4, out if last else xs)

    p.plan()
    with contextlib.ExitStack() as es:
        sems = {e: es.enter_context(nc.semaphore(f"s_{e}")) for e in p.ENG}
        ssems = {s: es.enter_context(nc.semaphore(f"d_{s}")) for s in p.stream_cnt}
        block = es.enter_context(nc.Block())

        @block.tensor
        def _(e):
            p.emit_engine('pe', e, sems, ssems)

        @block.scalar
        def _(e):
            p.emit_engine('act', e, sems, ssems)

        @block.vector
        def _(e):
            p.emit_engine('dve', e, sems, ssems)

        @block.gpsimd
        def _(e):
            p.emit_engine('pool', e, sems, ssems)

        @block.sync
        def _(e):
            p.emit_engine('sp', e, sems, ssems)
            for s in p.stream_cnt:
                if s.startswith("xst") or s.startswith("xld"):
                    e.wait_ge(ssems[s], p.stream_cnt[s])
    return nc, p


_CACHE = {}


def _host_prep(inputs):
    f = lambda a: np.ascontiguousarray(np.asarray(a, dtype=np.float32))
    x = f(inputs['x'])
    mem = f(inputs['mem'])
    cf = np.zeros((128, 512), np.float32)
    cf[:, 0:128] = np.eye(128)
    cf[:, 128:256] = 1.0
    cf[:, 256:384] = np.triu(np.ones((128, 128), np.float32))
    cf[:, 384:512] = np.tril(np.ones((128, 128), np.float32), -1)
    rowp = np.zeros((1, 2 * 672), np.float32)
    colp = np.zeros((128, 2 * 216), np.float32)
    for l in range(2):
        r = rowp[0, l * 672:(l + 1) * 672]
        r[0:4] = inputs['gdn_A_log'][l]
        r[4:8] = inputs['gdn_dt_bias'][l]
        r[8:16] = inputs['ssm_A_log'][l]
        r[16:24] = inputs['ssm_dt_bias'][l]
        r[24:32] = inputs['ssm_D'][l]
        r[32:160] = inputs['gdn_norm_g'][l]
        r[160:672] = inputs['ssm_norm_g'][l]
        c = colp[:, l * 216:(l + 1) * 216]
        c[:, 0:112] = np.asarray(inputs['norm_g'][l]).reshape(7, 16, 128).transpose(2, 0, 1).reshape(128, 112)
        c[:, 112:124] = np.asarray(inputs['conv_a_w'][l]).reshape(3, 4, 128).transpose(2, 1, 0).reshape(128, 12)
        c[:, 124:128] = np.asarray(inputs['pool_scale'][l]).reshape(4, 128).T
        c[:, 128:176] = np.asarray(inputs['gdn_conv_w'][l]).reshape(4, 12, 128).transpose(2, 1, 0).reshape(128, 48)
        c[:, 176:208] = np.asarray(inputs['ssm_conv_w'][l]).reshape(4, 8, 128).transpose(2, 1, 0).reshape(128, 32)
        c[:, 208:216] = np.asarray(inputs['ssm_conv_b'][l]).reshape(8, 128).T
    rowp = np.ascontiguousarray(np.broadcast_to(rowp, (128, 2 * 672)))
    shared = dict(cf=cf, rowp=rowp, colp=colp)

    def tile_w(w):
        w = f(w)
        L, K, N = w.shape
        return np.ascontiguousarray(w.reshape(L, K // 128, 128, N // 128, 128).transpose(0, 3, 2, 1, 4))
    w_in_ = f(inputs['w_in'])
    a_cols = []
    for j in range(4):
        for base in (A_C, A_H, A_B):
            a_cols.append(np.arange(base + 128 * j, base + 128 * (j + 1)))
    fm_cols = np.concatenate([np.arange(D_XBC, D_XBC + 1024), np.arange(C_QKV, C_QKV + 1536)] + a_cols + [np.arange(B_U, B_U + 512)])
    tm_cols = np.concatenate([np.arange(D_Z, D_Z + 512), np.arange(C_Z, C_Z + 512), np.arange(D_DT, D_DT + 8), np.arange(C_AB, C_AB + 8)])
    shared['w_in_fm'] = tile_w(w_in_[:, :, fm_cols])
    shared['w_in_tm'] = np.ascontiguousarray(w_in_[:, :, tm_cols])
    shared['pool_w'] = f(inputs['pool_w'])
    shared['w_out_t'] = tile_w(inputs['w_out'])
    shared['xa_wq_t'] = tile_w(inputs['xa_wq'])
    wkv = f(inputs['xa_wkv'])
    shared['xa_wk_t'] = tile_w(wkv[:, :, :D])
    shared['xa_wv_t'] = tile_w(wkv[:, :, D:])
    shared['xa_wo_t'] = tile_w(inputs['xa_wo'])
    shared['ffn_w_gu_t'] = tile_w(inputs['ffn_w_gu'])
    shared['ffn_w_down_t'] = tile_w(inputs['ffn_w_down'])
    in_maps = []
    for core in range(8):
        b, s = core // 2, core % 2
        m = dict(shared)
        m['xT'] = np.ascontiguousarray(x[b, s * T:(s + 1) * T, :].T)
        if s == 0:
            m['xh'] = np.zeros((D, HALO), np.float32)
        else:
            m['xh'] = np.ascontiguousarray(x[b, T - HALO:T, :].T)
        m['memT'] = np.ascontiguousarray(mem[b].T)
        misc = np.zeros((128, 80), np.float32)
        misc[:, 0] = float(s)
        for gi in range(4):
            w = 2 ** (gi + 1)
            for t in range(HALO):
                misc[:, 16 + gi * 16 + t] = 1.0 / (min(t + 1, w) if s == 0 else w)
        m['misc'] = misc
        in_maps.append(m)
    return in_maps


def kernel(**inputs):
    if 'nc' not in _CACHE:
        _CACHE['nc'] = build()[0]
    nc = _CACHE['nc']
    in_maps = _host_prep(inputs)
    res = run_bass_kernel_spmd(nc, in_maps, core_ids=list(range(8)))
    outp = np.zeros((4, 2 * T, D), np.float32)
    for core in range(8):
        b, s = core // 2, core % 2
        outp[b, s * T:(s + 1) * T, :] = res.results[core]['out'].T
    return outp
```

```python
import contextlib
import numpy as np
import concourse.bass as bass
import concourse.mybir as mybir
from concourse.bass_utils import run_bass_kernel_spmd

F32 = mybir.dt.float32
BF16 = mybir.dt.bfloat16
ALU = mybir.AluOpType
AF = mybir.ActivationFunctionType

D = 2048
KC = 16
T = 1024
HALO = 16
TT = T + HALO
NT = 8
DFF = 5632
FKC = 44
MEM = 256
EPS = 1e-6
B0 = 16512
SBUF_END = 229376

A_B, A_C, A_H = 0, 512, 1024
B_U = 1536
C_QKV, C_Z, C_AB = 2048, 3584, 4096
D_Z, D_XBC, D_DT = 4104, 4616, 5640


class Prog:
    ENG = ('pe', 'act', 'dve', 'pool', 'sp')

    def __init__(self, nc):
        self.nc = nc
        self.ops = {e: [] for e in self.ENG}
        self.lastw = {}
        self.readers = {}
        self.stream_cnt = {}
        self.pool_streams = set()

    def op(self, eng, emit, reads=(), writes=(), stream=None, inc=16):
        if eng == 'pool' and stream is not None:
            self.pool_streams.add(stream)
        deps = []
        for k in reads:
            t = self.lastw.get(k)
            if t is not None:
                deps.append(t)
        for k in writes:
            t = self.lastw.get(k)
            if t is not None:
                deps.append(t)
            deps.extend(self.readers.get(k, ()))
        idx = len(self.ops[eng])
        if stream is not None:
            c = self.stream_cnt.get(stream, 0) + inc
            self.stream_cnt[stream] = c
            tok = ('s', stream, c)
        else:
            tok = ('e', eng, idx)
        waits = [d for d in deps if not (d[0] == 'e' and d[1] == eng and eng == 'pe')]
        self.ops[eng].append(dict(emit=emit, waits=waits, stream=stream, inc=inc))
        for k in writes:
            self.lastw[k] = tok
            self.readers[k] = []
        for k in reads:
            self.readers.setdefault(k, []).append(tok)
        return tok

    def barrier(self):
        toks = []
        for e in self.ENG:
            real = [i for i, o in enumerate(self.ops[e]) if o['emit'] is not None and o['stream'] is None]
            if real:
                toks.append(('e', e, real[-1]))
        for s, c in self.stream_cnt.items():
            if s not in self.pool_streams:
                toks.append(('s', s, c))
        for e in self.ENG:
            if e != 'pool':
                self.ops[e].append(dict(emit=None, waits=list(toks), stream=None, inc=0))
        keep = ("W", "snd", "rcv")
        self.lastw = {k: v for k, v in self.lastw.items() if k.startswith(keep)}
        self.readers = {k: v for k, v in self.readers.items() if k.startswith(keep)}

    def plan(self):
        needed = {e: set() for e in self.ENG}
        plan = {}
        for e in self.ENG:
            waited = {}
            out = []
            for o in self.ops[e]:
                best = {}
                for d in o['waits']:
                    key = (d[0], d[1])
                    if d[2] > best.get(key, -1):
                        best[key] = d[2]
                w = []
                for key, v in best.items():
                    if v > waited.get(key, -1):
                        waited[key] = v
                        w.append((key[0], key[1], v))
                        if key[0] == 'e':
                            needed[key[1]].add(v)
                out.append(w)
            plan[e] = out
        self.rank = {}
        for e in self.ENG:
            self.rank[e] = {i: c + 1 for c, i in enumerate(sorted(needed[e]))}
        self._plan = plan
        self.needed = needed

    def emit_engine(self, e, eng, sems, ssems):
        plan = self._plan[e]
        for i, o in enumerate(self.ops[e]):
            for (kind, src, v) in plan[i]:
                if kind == 'e':
                    eng.wait_ge(sems[src], self.rank[src][v])
                else:
                    eng.wait_ge(ssems[src], v)
            if o['emit'] is None:
                continue
            ins = o['emit'](eng)
            if o['stream'] is not None:
                ins.then_inc(ssems[o['stream']], o['inc'])
            elif i in self.needed[e]:
                ins.then_inc(sems[e], 1)


class Arena:
    def __init__(self, nc):
        self.nc = nc
        self.n = 0

    def at(self, name, shape, dtype, off):
        esz = 4 if dtype == F32 else 2
        size = esz
        for s in shape[1:]:
            size *= s
        assert B0 + off + size <= SBUF_END, (name, off, size)
        self.n += 1
        return self.nc.alloc_sbuf_tensor_at(f"{name}_{self.n}", list(shape), dtype, offset=B0 + off)


class Seq:
    def __init__(self, ar, lo, hi):
        self.ar, self.lo, self.hi, self.cur = ar, lo, hi, lo

    def a(self, name, shape, dtype):
        esz = 4 if dtype == F32 else 2
        size = esz
        for s in shape[1:]:
            size *= s
        size = (size + 63) // 64 * 64
        off = self.cur
        self.cur += size
        assert self.cur <= self.hi, (name, self.cur, self.hi)
        if not hasattr(self, 'off'):
            self.off = {}
        self.off[name] = off
        return self.ar.at(name, shape, dtype, off)


def build(n_layers=2, debug=False):
    nc = bass.Bass("TRN2", target_bir_lowering=False)
    p = Prog(nc)
    ar = Arena(nc)

    def din(name, shape):
        return nc.dram_tensor(name, list(shape), F32, kind="ExternalInput").ap()

    xT_in = din("xT", [D, T])
    xh_in = din("xh", [D, HALO])
    memT_in = din("memT", [D, MEM])
    cf_in = din("cf", [128, 512])
    rowp_in = din("rowp", [128, 2 * 672])
    colp_in = din("colp", [128, 2 * 216])
    misc_in = din("misc", [128, 80])
    w_in_fm = din("w_in_fm", [2, 36, 128, KC, 128])
    w_in_tm = din("w_in_tm", [2, D, 1040])
    pool_w = din("pool_w", [2, 4, 128, 128])
    w_out = din("w_out_t", [2, KC, 128, KC, 128])
    xa_wq = din("xa_wq_t", [2, KC, 128, KC, 128])
    xa_wk = din("xa_wk_t", [2, KC, 128, KC, 128])
    xa_wv = din("xa_wv_t", [2, KC, 128, KC, 128])
    xa_wo = din("xa_wo_t", [2, KC, 128, KC, 128])
    w_gu = din("ffn_w_gu_t", [2, 2 * FKC, 128, KC, 128])
    w_down = din("ffn_w_down_t", [2, KC, 128, FKC, 128])
    out = nc.dram_tensor("out", [D, T], F32, kind="ExternalOutput").ap()
    xs = nc.dram_tensor("xs", [D, T], F32).ap()
    snd = [nc.dram_tensor(f"snd{i}", [128, 512 if i < 4 else 256], F32) for i in range(5)]
    rcv = [nc.dram_tensor(f"rcv{i}", [256, 512 if i < 4 else 256], F32) for i in range(5)]
    dbg = {}
    if debug:
        dbg['mixed'] = nc.dram_tensor("dbg_mixed", [D, T], F32, kind="ExternalOutput").ap()

    P = Seq(ar, 0, 34304)
    CF = P.a("cf", [128, 512], F32)
    IDF, ONESF, UF, SLF = CF[:, 0:128], CF[:, 128:256], CF[:, 256:384], CF[:, 384:512]
    CBF = P.a("cbf", [128, 256], BF16)
    IDB, ONESB = CBF[:, 0:128], CBF[:, 128:256]
    ROWP = P.a("rowp", [128, 1344], F32)
    COLP = P.a("colp", [128, 432], F32)
    MISC = P.a("misc", [128, 80], F32)
    FLAG = MISC[:, 0:1]
    EPSC = P.a("epsc", [128, 4], F32)
    GS = P.a("gs", [128, 224], F32)
    NSLOT = 5
    WS = [P.a(f"wslot{i}", [128, 2048], BF16) for i in range(NSLOT)]
    PEND = P.cur
    ps = [nc.alloc_psum_tensor(f"ps{i}", [128, 512], F32) for i in range(8)]
    st = dict(bank=0, wslot=0, ev=0, pool=None, pidx={})

    def bank():
        b = st['bank']
        st['bank'] = (b + 1) % 8
        pool = st.get('pool')
        if pool is not None:
            i = st['pidx'].get(id(pool), 0)
            st['pidx'][id(pool)] = i + 1
            return pool[i % len(pool)]
        return b

    def wslot():
        s = st['wslot']
        st['wslot'] = (s + 1) % NSLOT
        return s

    def mm(o, lhsT, rhs, start, stop, r, w):
        p.op('pe', lambda e: e.matmul(o, lhsT=lhsT, rhs=rhs, start=start, stop=stop), reads=r, writes=w)

    def tr(o, i, ident, r, w):
        p.op('pe', lambda e: e.transpose(o, i, ident), reads=r, writes=w)

    def act(o, i, func, r, w, bias=None, scale=None, accum=None):
        kw = {}
        if bias is not None:
            kw['bias'] = bias
        if scale is not None:
            kw['scale'] = scale
        if accum is not None:
            kw['accum_out'] = accum
        p.op('act', lambda e: e.activation(out=o, in_=i, func=func, **kw), reads=r, writes=w)

    def tt(o, a, b, op, r, w, eng='dve'):
        p.op(eng, lambda e: e.tensor_tensor(out=o, in0=a, in1=b, op=op), reads=r, writes=w)

    def ts(o, a, s1, s2, op0, op1, r, w, eng='dve'):
        if op1 is None:
            p.op(eng, lambda e: e.tensor_scalar(out=o, in0=a, scalar1=s1, scalar2=None, op0=op0), reads=r, writes=w)
        else:
            p.op(eng, lambda e: e.tensor_scalar(out=o, in0=a, scalar1=s1, scalar2=s2, op0=op0, op1=op1), reads=r, writes=w)

    def stt(o, a, s, b, op0, op1, r, w, eng='dve'):
        p.op(eng, lambda e: e.scalar_tensor_tensor(out=o, in0=a, scalar=s, in1=b, op0=op0, op1=op1), reads=r, writes=w)

    def cp(o, i, r, w, eng=None):
        if eng is None:
            st['ev'] ^= 1
            eng = 'act' if st['ev'] else 'dve'
        if eng == 'act':
            act(o, i, AF.Copy, r, w)
        else:
            p.op('dve', lambda e: e.tensor_copy(out=o, in_=i), reads=r, writes=w)

    def recip(o, i, r, w):
        p.op('dve', lambda e: e.reciprocal(out=o, in_=i), reads=r, writes=w)

    def memset(o, v, w):
        p.op('dve', lambda e: e.memset(o, v), writes=w)

    def dma(o, i, stream, r, w, q='sp'):
        p.op(q, lambda e: e.dma_start(out=o, in_=i), reads=r, writes=w, stream=stream)

    def exchange(i, r, w):
        p.op('pool', lambda e: e.collective_compute("AllGather", ALU.bypass,
                                                     replica_groups=[[0, 1], [2, 3], [4, 5], [6, 7]],
                                                     ins=[snd[i].ap()], outs=[rcv[i].ap()]),
             reads=r, writes=w, stream=f"cc{i}", inc=1)

    dma(CF[:, :], cf_in, "c0", [], ["CF"])
    dma(ROWP[:, :], rowp_in, "c1", [], ["ROWP"])
    dma(COLP[:, :], colp_in, "c2", [], ["COLP"])
    dma(MISC[:, :], misc_in, "c3", [], ["MISC"])
    cp(CBF[:, :], CF[:, 0:256], ["CF"], ["CBF"], eng='dve')
    memset(EPSC[:, 0:1], EPS, ["EPSC"])
    memset(EPSC[:, 1:2], D * EPS, ["EPSC1"])
    for l in range(2):
        ts(GS[:, l * 112:(l + 1) * 112], COLP[:, l * 216:l * 216 + 112], float(np.sqrt(D)), None, ALU.mult, None, ["COLP"], [f"GS{l}"])

    def gcol(l, row, kc):
        return GS[:, l * 112 + row * 16 + kc: l * 112 + row * 16 + kc + 1]

    def colp(l, off, n=1):
        return COLP[:, l * 216 + off: l * 216 + off + n]

    def rowp(l, off, n):
        return ROWP[:, l * 672 + off: l * 672 + off + n]

    def load_unit(src_ap, kcn, ncols=128):
        s = wslot()
        view = WS[s][:, 0:kcn * ncols].rearrange("p (k n) -> p k n", k=kcn)
        p.op('pool', lambda e: e.dma_start(out=view, in_=src_ap), reads=[], writes=[f"W{s}"], stream=f"w{s}")
        return view, f"W{s}"

    def linear_fm_gen(Wt, tiles, kcn, src, src_keys, blocks, consume, every=8):
        cnt = 0
        for i, nt in enumerate(tiles):
            units = []
            for k0 in range(0, kcn, 16):
                kn = min(16, kcn - k0)
                view, key = load_unit(Wt[nt][:, k0:k0 + kn, :], kn)
                units.append((k0, kn, view, key))
            for bi, (t0, n) in enumerate(blocks):
                b = bank()
                for (k0, kn, view, key) in units:
                    for k in range(kn):
                        mm(ps[b][:, 0:n], view[:, k, :], src(k0 + k, t0, n), (k0 + k) == 0, (k0 + k) == kcn - 1,
                           [key] + (src_keys(bi) if callable(src_keys) else src_keys), [f"ps{b}"])
                        cnt += 1
                        if cnt % every == 0:
                            yield
                consume(i, bi, ps[b][:, 0:n], f"ps{b}")

    def linear_fm(*a, **kw):
        for _ in linear_fm_gen(*a, **kw):
            pass

    def interleaved_gen(gens, pools=None):
        gens = [(g_, (pools[j_] if pools else None)) for j_, g_ in enumerate(gens)]
        while gens:
            nxt_ = []
            for g_, pl_ in gens:
                st['pool'] = pl_
                try:
                    next(g_)
                    nxt_.append((g_, pl_))
                except StopIteration:
                    pass
                st['pool'] = None
            gens = nxt_
            yield

    def run_interleaved(gens, pools=None):
        for _ in interleaved_gen(gens, pools):
            pass

    def linear_tm_gen(W2d, c0, ncols, src, src_keys, tiles, consume, every=8):
        view, key = load_unit(W2d.rearrange("(k p) n -> p k n", p=128)[:, :, c0:c0 + ncols], KC, ncols)
        cnt = 0
        for t in tiles:
            b = bank()
            for k in range(KC):
                mm(ps[b][:, 0:ncols], src(k, t), view[:, k, :], k == 0, k == KC - 1, [key] + src_keys, [f"ps{b}"])
                cnt += 1
                if cnt % every == 0:
                    yield
            consume(t, ps[b][:, 0:ncols], f"ps{b}")

    def linear_tm(*a, **kw):
        for _ in linear_tm_gen(*a, **kw):
            pass

    def slow_gen(gen, k):
        while True:
            for _ in range(k - 1):
                yield
            try:
                next(gen)
            except StopIteration:
                return
            yield

    def delay_gen(gen, n):
        for _ in range(n):
            yield
        yield from gen

    def sumsq_rstd(srcf, nkc, n, sqb, rstd, tmp, key, div_eps_scale=True):
        b = bank()
        for k in range(nkc):
            sq = sqb[k % len(sqb)]
            act(sq[:, 0:n], srcf(k), AF.Square, [key], [f"sq{id(sq)}"])
            mm(ps[b][:, 0:n], ONESB, sq[:, 0:n], k == 0, k == nkc - 1, ["CBF", f"sq{id(sq)}"], [f"ps{b}"])
        act(tmp[:, 0:n], ps[b][:, 0:n], AF.Ln, [f"ps{b}", "EPSC1"], [f"t{id(tmp)}"], bias=EPSC[:, 1:2], scale=1.0)
        act(rstd[:, 0:n], tmp[:, 0:n], AF.Exp, [f"t{id(tmp)}"], [f"r{id(rstd)}"], scale=-0.5)
        return f"r{id(rstd)}"

    M_XN = PEND
    M_MIX = M_XN + 33280
    M_SCR = M_MIX + 32768
    SCR_END = SBUF_END - B0
    xn = ar.at("xn", [128, KC, TT], BF16, M_XN)
    mixed = ar.at("mixed", [128, KC, T], BF16, M_MIX)
    XNK = [f"xn{i}" for i in range(5)]
    BLK3 = [(0, HALO), (HALO, 512), (HALO + 512, 512)]

    def XNKB(bi):
        return [["xn0"], ["xn1", "xn2"], ["xn3", "xn4"]][bi]

    def xsrc(k, t0, n):
        return xn[:, k, t0:t0 + n]

    def bc(ap, axis, shape):
        return ap.unsqueeze(axis).to_broadcast(list(shape))

    def conv_fm(raw, rkey, wcol0, ntap, bias, cacc, dst, func, dkey):
        ck = f"cacc{id(cacc)}"
        if bias is None:
            act(cacc[:, :], raw[:, HALO:TT], AF.Identity, rkey + ["COLP"], [ck], scale=COLP[:, wcol0 + ntap - 1: wcol0 + ntap])
        else:
            act(cacc[:, :], raw[:, HALO:TT], AF.Identity, rkey + ["COLP"], [ck], scale=COLP[:, wcol0 + ntap - 1: wcol0 + ntap], bias=bias)
        for k in range(ntap - 1):
            sh = HALO - (ntap - 1) + k
            stt(cacc[:, :], raw[:, sh:sh + T], COLP[:, wcol0 + k: wcol0 + k + 1], cacc[:, :], ALU.mult, ALU.add, rkey + [ck, "COLP"], [ck])
        if func is not None:
            act(dst, cacc[:, :], func, [ck], dkey)
        return ck

    def softplus_neg_scaled(dst, src, brow, arow, n_t, nh, tmp, keys_r, key_w):
        tt(tmp, src, bc(brow, 1, [128, n_t, nh]), ALU.add, keys_r + ["ROWP"], [key_w + "_t"])
        act(tmp, tmp, AF.Exp, [key_w + "_t"], [key_w + "_t"])
        act(dst, tmp, AF.Ln, [key_w + "_t"], [key_w], bias=1.0)

    def layer(l, x_src, halo_from_rcv, x_dst):
        Wl = w_in_fm[l]
        Wtm = w_in_tm[l]
        LC = l * 216
        p.barrier()
        S = Seq(ar, M_SCR, SCR_END)
        xstb = [S.a(f"xst{i}", [128, KC, 256], F32) for i in range(2)]
        sqb = [S.a(f"sqb{i}", [128, 512], BF16) for i in range(4)]
        rtmpb = [S.a(f"rtmp{i}", [128, 256], F32) for i in range(2)]
        rstdb = [S.a(f"rstd{i}", [128, 256], F32) for i in range(2)]
        xv = x_src.rearrange("(k p) t -> p k t", p=128)
        NB1 = [(0, HALO)] + [(HALO + 256 * i, 256) for i in range(4)]
        for bi, (t0, n) in enumerate(NB1):
            xst = xstb[bi % 2]
            xk = f"xst{bi % 2}"
            if bi == 0:
                if halo_from_rcv is None:
                    dma(xst[:, :, 0:HALO], xh_in.rearrange("(k p) t -> p k t", p=128), xk, [], [xk])
                else:
                    dma(xst[:, :, 0:HALO], rcv[halo_from_rcv].ap()[0:128, 0:256].rearrange("p (k t) -> p k t", k=KC), xk,
                        [f"rcv{halo_from_rcv}"], [xk])
                    ts(xst[:, :, 0:HALO], xst[:, :, 0:HALO], FLAG, None, ALU.mult, None, [xk, "MISC"], [xk])
            else:
                dma(xst[:, :, :], xv[:, :, (bi - 1) * 256: bi * 256], xk, [], [xk])
            rk = sumsq_rstd(lambda k: xst[:, k, 0:n], KC, n, sqb[2 * (bi % 2): 2 * (bi % 2) + 2], rstdb[bi % 2], rtmpb[bi % 2], xk)
            for k in range(KC):
                stt(xn[:, k, t0:t0 + n], xst[:, k, 0:n], gcol(l, 0, k), rstdb[bi % 2][:, 0:n], ALU.mult, ALU.mult,
                    [xk, rk, f"GS{l}"], [f"xn{bi}"])

        p.barrier()
        S = Seq(ar, M_SCR, SCR_END)
        craw = [S.a(f"craw{i}", [128, TT], F32) for i in range(2)]
        cacc = S.a("cacc", [128, T], F32)
        xc = S.a("xc", [128, 4, T], BF16)
        BT = S.a("BT", [128, 2, T], BF16)
        CT = S.a("CT", [128, 2, T], BF16)
        x_tm = S.a("x_tm", [128, NT, 512], BF16)
        B_tm = S.a("B_tm", [128, NT, 256], BF16)
        smD = S.a("smD", [128, NT, 8], F32)
        sm_t = S.a("sm_t", [128, NT, 8], F32)
        dtv = S.a("dtv", [128, NT, 8], F32)
        av = S.a("av", [128, NT, 8], F32)
        Arow = S.a("Arow", [128, 8], F32)
        eA = S.a("eA", [128, NT, 8], F32)
        dA = S.a("dA", [128, NT, 8], F32)
        statesT = S.a("statesT", [128, NT, 512], F32)
        zsD = S.a("zsD", [128, NT, 512], F32)
        DI = S.a("DI", [128, 8, 128], F32)
        hst = S.a("hst", [128, 512], F32)
        hb = S.a("hb", [128, 512], BF16)
        acst = S.a("acst", [128, 16], F32)
        te = S.a("te", [128, 8], F32)
        dec = S.a("dec", [128, 8], F32)
        Xdec = S.a("Xdec", [128, 512], BF16)
        gtri = S.a("gtri", [128, 8, 128], F32)
        E = S.a("E", [128, 8, 128], F32)
        CBTm = S.a("CBTm", [128, 2, 128], F32)
        WTb = S.a("WTb", [128, 8, 128], BF16)
        t1 = S.a("t1", [128, 512], F32)
        yv = S.a("yv", [128, 512], F32)
        yb = S.a("yb", [128, 512], BF16)
        ssq = S.a("ssq", [128, 4], F32)
        rs = S.a("rs", [128, 4], F32)

        def cons_xbc(i, bi, pa, pk):
            t0, n = BLK3[bi]
            r = craw[i % 2]
            cp(r[:, t0:t0 + n], pa, [pk], [f"craw{i % 2}_{bi}"])
            if bi == 2:
                dst, dk = (xc[:, i, :], "xc") if i < 4 else ((BT[:, i - 4, :], "BT") if i < 6 else (CT[:, i - 6, :], "CT"))
                conv_fm(r, [f"craw{i % 2}_{b_}" for b_ in range(3)], LC + 176 + i * 4, 4,
                        COLP[:, LC + 208 + i: LC + 209 + i], cacc, dst, AF.Silu, [dk + str(i)])
        linear_fm(Wl, list(range(0, 8)), KC, xsrc, XNKB, BLK3, cons_xbc)
        xck = [f"xc{i}" for i in range(4)]
        BTk = ["BT4", "BT5"]
        CTk = ["CT6", "CT7"]
        for t in range(NT):
            b = bank()
            psb = ps[b][:, :].bitcast(BF16)
            for i in range(4):
                tr(psb[:, i * 128:(i + 1) * 128], xc[:, i, t * 128:(t + 1) * 128], IDB, xck + ["CBF"], [f"ps{b}"])
            cp(x_tm[:, t, :], psb[:, 0:512], [f"ps{b}"], [f"x_tm{t}"])
            b = bank()
            psb = ps[b][:, :].bitcast(BF16)
            for g in range(2):
                tr(psb[:, g * 128:(g + 1) * 128], BT[:, g, t * 128:(t + 1) * 128], IDB, BTk + ["CBF"], [f"ps{b}"])
            cp(B_tm[:, t, :], psb[:, 0:256], [f"ps{b}"], [f"B_tm{t}"])
        linear_tm(Wtm, 1024, 8, lambda k, t: xn[:, k, HALO + t * 128: HALO + (t + 1) * 128], XNK, range(NT),
                  lambda t, pa, pk: cp(smD[:, t, :], pa, [pk], ["smD"]))
        softplus_neg_scaled(dtv[:, :, :], smD[:, :, :], rowp(l, 16, 8), None, NT, 8, sm_t[:, :, :], ["smD"], "dtv")
        act(Arow[:, :], rowp(l, 8, 8), AF.Exp, ["ROWP"], ["Arow"])
        ts(Arow[:, :], Arow[:, :], -1.0, None, ALU.mult, None, ["Arow"], ["Arow"])
        tt(av[:, :, :], dtv[:, :, :], bc(Arow[:, :], 1, [128, NT, 8]), ALU.mult, ["dtv", "Arow"], ["av"])
        tt(DI[:, :, :], bc(IDF, 1, [128, 8, 128]), bc(rowp(l, 24, 8), 2, [128, 8, 128]), ALU.mult, ["CF", "ROWP"], ["DI"])
        def zd_thread():
            for hf in range(4):
                yield from linear_tm_gen(Wtm, hf * 128, 128, lambda k, t: xn[:, k, HALO + t * 128: HALO + (t + 1) * 128], XNK, range(NT),
                                         lambda t, pa, pk, hf=hf: act(zsD[:, t, hf * 128:(hf + 1) * 128], pa, AF.Silu, [pk], [f"zsD{t}_{hf}"]))

        def states_thread():
            for t in range(NT):
                b = bank()
                mm(ps[b][:, 0:8], UF, av[:, t, :], True, True, ["CF", "av"], [f"ps{b}"])
                mm(ps[b][:, 8:16], ONESF, av[:, t, :], True, True, ["CF", "av"], [f"ps{b}"])
                yield
                cp(acst[:, :], ps[b][:, 0:16], [f"ps{b}"], ["acst"], eng='dve')
                yield
                act(eA[:, t, :], acst[:, 0:8], AF.Exp, ["acst"], [f"eA{t}"])
                act(dA[:, t, :], acst[:, 8:16], AF.Exp, ["acst"], [f"dA{t}"])
                tt(te[:, :], acst[:, 8:16], acst[:, 0:8], ALU.subtract, ["acst"], ["te"])
                yield
                act(te[:, :], te[:, :], AF.Exp, ["te"], ["te"])
                yield
                tt(dec[:, :], te[:, :], dtv[:, t, :], ALU.mult, ["te", "dtv"], ["dec"])
                yield
                tt(Xdec[:, :].rearrange("p (h q) -> p h q", h=8), x_tm[:, t, :].rearrange("p (h q) -> p h q", h=8),
                   bc(dec[:, :], 2, [128, 8, 64]), ALU.mult, ["dec", f"x_tm{t}"], ["Xdec"])
                yield
                b = bank()
                for g in range(2):
                    mm(ps[b][:, g * 256:(g + 1) * 256], B_tm[:, t, g * 128:(g + 1) * 128], Xdec[:, g * 256:(g + 1) * 256], True, True,
                       [f"B_tm{t}", "Xdec"], [f"ps{b}"])
                yield
                cp(statesT[:, t, :], ps[b][:, :], [f"ps{b}"], [f"st{t}"])
        run_interleaved([states_thread(), zd_thread()])

        def h_step(t):
            tt(hst[:, :].rearrange("p (h q) -> p h q", h=8), hst[:, :].rearrange("p (h q) -> p h q", h=8),
               bc(dA[:, t, :], 2, [128, 8, 64]), ALU.mult, ["hst", f"dA{t}"], ["hst"])
            tt(hst[:, :], hst[:, :], statesT[:, t, :], ALU.add, ["hst", f"st{t}"], ["hst"])
        memset(hst[:, :], 0.0, ["hst"])
        for t in range(NT):
            h_step(t)
        ex = 2 * l
        dma(snd[ex].ap()[:, :], hst[:, :], f"snd{ex}", ["hst"], [f"snd{ex}"])
        exchange(ex, [f"snd{ex}"], [f"rcv{ex}"])

        SA = Seq(ar, S.cur, SCR_END)
        ra = [SA.a(f"ra{i}", [128, TT], F32) for i in range(3)]
        p.barrier()
        R1 = Seq(ar, S.off["craw0"], S.off["craw0"] + 8320)
        R2 = Seq(ar, S.off["xc"], S.off["xc"] + 8192)
        R3 = Seq(ar, S.off["B_tm"], S.off["B_tm"] + 4096)
        TS = [dict(gtri=gtri, E=E, CBTm=CBTm, WTb=WTb, t1=t1, yv=yv, yb=yb, hb=hb, ssq=ssq, rs=rs),
              dict(gtri=R1.a("gtri1", [128, 8, 128], F32), E=R1.a("E1", [128, 8, 128], F32),
                   CBTm=R2.a("CBTm1", [128, 2, 128], F32), WTb=R2.a("WTb1", [128, 8, 128], BF16), t1=R2.a("t11", [128, 512], F32),
                   yv=R2.a("yv1", [128, 512], F32), yb=R2.a("yb1", [128, 512], BF16), hb=R3.a("hb1", [128, 512], BF16),
                   ssq=R3.a("ssq1", [128, 4], F32), rs=R3.a("rs1", [128, 4], F32))]

        def mixer_a():
            for j in range(4):
                def cons_a(i, bi, pa, pk, j=j):
                    t0, n = BLK3[bi]
                    cp(ra[i][:, t0:t0 + n], pa, [pk], [f"ra{i}_{bi}"])
                yield from linear_fm_gen(Wl, [20 + 3 * j, 21 + 3 * j, 22 + 3 * j], KC, xsrc, XNKB, BLK3, cons_a)
                rk = [f"ra{i}_{b_}" for i in range(3) for b_ in range(3)]
                tt(ra[0][:, :], ra[0][:, :], ra[1][:, :], ALU.mult, rk, ["ra0_0", "ra0_1", "ra0_2"])
                yield
                ck = conv_fm(ra[0], rk, LC + 112 + j * 3, 3, None, cacc, None, None, None)
                tt(mixed[:, j, :], cacc[:, :], ra[2][:, HALO:TT], ALU.mult, [ck] + rk, [f"mixed{j}"])
                yield

        def ssd_tile(t):
            f_ = t % 2
            Bf = TS[f_]
            gtri, E, CBTm, WTb, t1, yv, yb, hb, ssq, rs = (Bf[k_] for k_ in ("gtri", "E", "CBTm", "WTb", "t1", "yv", "yb", "hb", "ssq", "rs"))
            K = lambda n_: f"{n_}#{f_}"
            tsl = slice(t * 128, (t + 1) * 128)
            tt(gtri[:, :, :], bc(UF, 1, [128, 8, 128]), bc(av[:, t, :], 2, [128, 8, 128]), ALU.mult, ["CF", "av"], [K("gtri")])
            b = bank()
            for g in range(2):
                mm(ps[b][:, g * 128:(g + 1) * 128], BT[:, g, tsl], CT[:, g, tsl], True, True, BTk + CTk, [f"ps{b}"])
            yield
            tt(CBTm[:, :, :], ps[b][:, 0:256].rearrange("p (g l) -> p g l", g=2), bc(UF, 1, [128, 2, 128]), ALU.mult,
               [f"ps{b}", "CF"], [K("CBTm")])
            bq_ = []
            for q in range(2):
                b = bank()
                bq_.append(b)
                mm(ps[b][:, :], SLF, gtri[:, 4 * q:4 * q + 4, :].rearrange("p h l -> p (h l)"), True, True, ["CF", K("gtri")], [f"ps{b}"])
            yield
            for q in range(2):
                act(E[:, 4 * q:4 * q + 4, :].rearrange("p h l -> p (h l)"), ps[bq_[q]][:, :], AF.Exp, [f"ps{bq_[q]}"], [K(f"E{q}")])
            yield
            for g in range(2):
                tt(E[:, 4 * g:4 * g + 4, :], E[:, 4 * g:4 * g + 4, :], bc(CBTm[:, g, :], 1, [128, 4, 128]), ALU.mult,
                   [K(f"E{g}"), K("CBTm")], [K(f"E{g}")])
            yield
            tt(E[:, :, :], E[:, :, :], bc(dtv[:, t, :], 2, [128, 8, 128]), ALU.mult, [K("E0"), K("E1"), "dtv"], [K("E0"), K("E1")])
            yield
            tt(WTb[:, :, :], E[:, :, :], DI[:, :, :], ALU.add, [K("E0"), K("E1"), "DI"], [K("WTb")])
            yield
            by = bank()
            for h in range(8):
                mm(ps[by][:, h * 64:(h + 1) * 64], WTb[:, h, :], x_tm[:, t, h * 64:(h + 1) * 64], True, True, [K("WTb"), f"x_tm{t}"], [f"ps{by}"])
            if t == 0:
                dma(hst[:, :], rcv[ex].ap()[0:128, :], "hst", [f"rcv{ex}"], ["hst"])
                ts(hst[:, :], hst[:, :], FLAG, None, ALU.mult, None, ["hst", "MISC"], ["hst"])
            cp(hb[:, :], hst[:, :], ["hst"], [K("hb")], eng='act')
            h_step(t)
            yield
            bo = bank()
            for g in range(2):
                mm(ps[bo][:, g * 256:(g + 1) * 256], CT[:, g, tsl], hb[:, g * 256:(g + 1) * 256], True, True, CTk + [K("hb")], [f"ps{bo}"])
            yield
            tt(t1[:, :].rearrange("p (h q) -> p h q", h=8), ps[bo][:, :].rearrange("p (h q) -> p h q", h=8),
               bc(eA[:, t, :], 2, [128, 8, 64]), ALU.mult, [f"ps{bo}", f"eA{t}"], [K("t1")])
            yield
            tt(yv[:, :], ps[by][:, :], t1[:, :], ALU.add, [f"ps{by}", K("t1")], [K("yv")])
            yield
            tt(yv[:, :], yv[:, :], zsD[:, t, :], ALU.mult, [K("yv")] + [f"zsD{t}_{hf}" for hf in range(4)], [K("yv")])
            memset(ssq[:, :], 0.0, [K("ssq")])
            yield
            for g in range(2):
                act(t1[:, 0:256], yv[:, g * 256:(g + 1) * 256], AF.Square, [K("yv"), K("ssq")], [K("t1"), K("ssq")], accum=ssq[:, g:g + 1])
            yield
            act(rs[:, 0:2], ssq[:, 0:2], AF.Sqrt, [K("ssq"), "EPSC"], [K("rs")], bias=EPSC[:, 0:1], scale=1.0 / 256)
            yield
            recip(rs[:, 0:2], rs[:, 0:2], [K("rs")], [K("rs")])
            yield
            for g in range(2):
                stt(yb[:, g * 256:(g + 1) * 256], yv[:, g * 256:(g + 1) * 256], rs[:, g:g + 1], rowp(l, 160 + g * 256, 256),
                    ALU.mult, ALU.mult, [K("yv"), K("rs"), "ROWP"], [K("yb")])
            yield
            b = bank()
            psb = ps[b][:, :].bitcast(BF16)
            for c in range(4):
                tr(psb[:, c * 128:(c + 1) * 128], yb[:, c * 128:(c + 1) * 128], IDB, [K("yb"), "CBF"], [f"ps{b}"])
            yield
            cp(mixed[:, 12:16, tsl], psb[:, 0:512].rearrange("p (c n) -> p c n", c=4), [f"ps{b}"], [f"mixedD{t}"])

        def ssd_pairs():
            for t0_ in range(0, NT, 2):
                yield from interleaved_gen([ssd_tile(t0_), ssd_tile(t0_ + 1)])
        run_interleaved([ssd_pairs(), mixer_a()])

        p.barrier()
        G2 = Seq(ar, M_SCR, M_SCR + 50688)
        G1 = Seq(ar, M_SCR + 50688, SCR_END)
        qT = G2.a("qT", [128, 4, T], BF16)
        vb = G2.a("vb", [128, NT, 512], BF16)
        TTb = G2.a("TTb", [128, NT, 512], BF16)
        wT = G2.a("wT", [128, NT, 512], BF16)
        qkTm = G2.a("qkTm", [128, NT, 512], BF16)
        k_end = G2.a("k_end", [128, NT, 512], BF16)
        eG = G2.a("eG", [128, NT, 4], F32)
        dG = G2.a("dG", [128, NT, 4], F32)
        smC = G2.a("smC", [128, NT, 8], F32)
        smt = G2.a("smt", [128, NT, 4], F32)
        gv = G2.a("gv", [128, NT, 4], F32)
        beta = G2.a("beta", [128, NT, 4], F32)
        gArow = G2.a("gArow", [128, 4], F32)
        gt = G2.a("gt", [128, 8], F32)
        ke = G2.a("ke", [128, 4], F32)
        bg = G2.a("bg", [128, 4], F32)
        kT = G1.a("kT", [128, 4, T], BF16)
        k_tm = G1.a("k_tm", [128, NT, 512], BF16)
        G1_TMP = G1.cur
        craw = [G1.a(f"gcraw{i}", [128, TT], F32) for i in range(2)]
        caccs = [G1.a(f"gcacc{i}", [128, T], F32) for i in range(2)]
        qfs = [G1.a(f"qf{i}", [128, T], F32) for i in range(2)]
        vT = G1.a("vT", [128, 4, T], BF16)
        sqg = [G1.a(f"sqg{i}", [128, 512], BF16) for i in range(2)]
        rtgs = [G1.a(f"rtg{i}", [128, 512], F32) for i in range(2)]
        rsgs = [G1.a(f"rsg{i}", [128, 512], F32) for i in range(2)]

        pend = []

        def flush_pend():
            while pend:
                pend.pop(0)()

        def cons_qkv(i, bi, pa, pk):
            t0, n = BLK3[bi]
            r = craw[i % 2]
            cacc = caccs[i % 2]
            qf = qfs[i % 2]
            qk_ = f"qf{i % 2}"
            cp(r[:, t0:t0 + n], pa, [pk], [f"gcraw{i % 2}_{bi}"])
            if bi != 2:
                return
            flush_pend()
            rk = [f"gcraw{i % 2}_{b_}" for b_ in range(3)]
            h = i % 4
            if i >= 8:
                conv_fm(r, rk, LC + 128 + i * 4, 4, None, cacc, vT[:, h, :], AF.Silu, [f"vT{h}"])
                return
            conv_fm(r, rk, LC + 128 + i * 4, 4, None, cacc, qf[:, :], AF.Silu, [qk_])
            for hb_ in range(2):
                sl = slice(hb_ * 512, (hb_ + 1) * 512)
                act(sqg[hb_][:, :], qf[:, sl], AF.Square, [qk_], [f"sqg{hb_}"])

            def l2n(i=i, h=h, qf=qf, qk_=qk_):
                for hb_ in range(2):
                    sl = slice(hb_ * 512, (hb_ + 1) * 512)
                    sq = sqg[hb_]
                    rtg, rsg = rtgs[hb_], rsgs[hb_]
                    b = bank()
                    mm(ps[b][:, :], ONESB, sq[:, :], True, True, ["CBF", f"sqg{hb_}"], [f"ps{b}"])
                    act(rtg[:, :], ps[b][:, :], AF.Ln, [f"ps{b}", "EPSC"], [f"rtg{hb_}"], bias=EPSC[:, 0:1], scale=1.0)
                    act(rsg[:, :], rtg[:, :], AF.Exp, [f"rtg{hb_}"], [f"rsg{hb_}"], scale=-0.5)
                    if i < 4:
                        stt(qT[:, h, sl], qf[:, sl], float(128 ** -0.5), rsg[:, :], ALU.mult, ALU.mult, [qk_, f"rsg{hb_}"], [f"qT{h}"])
                    else:
                        tt(kT[:, h, sl], qf[:, sl], rsg[:, :], ALU.mult, [qk_, f"rsg{hb_}"], [f"kT{h}"])
            pend.append(l2n)
        linear_fm(Wl, list(range(8, 20)), KC, xsrc, XNKB, BLK3, cons_qkv)
        flush_pend()
        qTk = [f"qT{h}" for h in range(4)]
        kTk = [f"kT{h}" for h in range(4)]
        vTk = [f"vT{h}" for h in range(4)]
        for t in range(NT):
            for (src_, sk, dst_, dk) in ((kT, kTk, k_tm, "k_tm"), (vT, vTk, vb, "vb")):
                b = bank()
                psb = ps[b][:, :].bitcast(BF16)
                for h in range(4):
                    tr(psb[:, h * 128:(h + 1) * 128], src_[:, h, t * 128:(t + 1) * 128], IDB, sk + ["CBF"], [f"ps{b}"])
                cp(dst_[:, t, :], psb[:, 0:512], [f"ps{b}"], [f"{dk}{t}"])
        linear_tm(Wtm, 1032, 8, lambda k, t: xn[:, k, HALO + t * 128: HALO + (t + 1) * 128], XNK, range(NT),
                  lambda t, pa, pk: cp(smC[:, t, :], pa, [pk], ["smC"]))
        softplus_neg_scaled(gv[:, :, :], smC[:, :, 0:4], rowp(l, 4, 4), None, NT, 4, smt[:, :, :], ["smC"], "gv")
        act(gArow[:, :], rowp(l, 0, 4), AF.Exp, ["ROWP"], ["gArow"])
        ts(gArow[:, :], gArow[:, :], -1.0, None, ALU.mult, None, ["gArow"], ["gArow"])
        tt(gv[:, :, :], gv[:, :, :], bc(gArow[:, :], 1, [128, NT, 4]), ALU.mult, ["gv", "gArow"], ["gv"])
        act(beta[:, :, :], smC[:, :, 4:8], AF.Sigmoid, ["smC"], ["beta"])

        def v4(ap):
            return ap.rearrange("p (h n) -> p h n", h=4)
        p.barrier()
        G1t = Seq(ar, G1_TMP, SCR_END)
        NFL = 2
        TB = []
        for f_ in range(NFL):
            TB.append(dict(
                gtU=G1t.a(f"gtU{f_}", [128, 4, 128], F32), gtS=G1t.a(f"gtS{f_}", [128, 4, 128], F32),
                Eg=G1t.a(f"Eg{f_}", [128, 4, 128], F32), ETg=G1t.a(f"ETg{f_}", [128, 4, 128], F32),
                Pm=[G1t.a(f"Pm{f_}_{i}", [128, 4, 128], F32) for i in range(2)],
                PTm=[G1t.a(f"PTm{f_}_{i}", [128, 4, 128], F32) for i in range(2)],
                RT=G1t.a(f"RT{f_}", [128, 4, 128], F32), kbe=G1t.a(f"kbe{f_}", [128, 512], BF16),
                gt=G1t.a(f"gt{f_}", [128, 8], F32), ke=G1t.a(f"ke{f_}", [128, 4], F32), bg=G1t.a(f"bg{f_}", [128, 4], F32)))

        def gdn_tile(t):
            f_ = t % NFL
            Bf = TB[f_]
            gtU, gtS, Eg, ETg, Pm, PTm, RT, kbe, gt, ke, bg = (Bf[k_] for k_ in ("gtU", "gtS", "Eg", "ETg", "Pm", "PTm", "RT", "kbe", "gt", "ke", "bg"))
            K = lambda n_: f"{n_}#{f_}"
            tsl = slice(t * 128, (t + 1) * 128)
            b = bank()
            mm(ps[b][:, 0:4], UF, gv[:, t, :], True, True, ["CF", "gv"], [f"ps{b}"])
            mm(ps[b][:, 4:8], ONESF, gv[:, t, :], True, True, ["CF", "gv"], [f"ps{b}"])
            cp(gt[:, :], ps[b][:, 0:8], [f"ps{b}"], [K("gt")], eng='dve')
            tt(gtU[:, :, :], bc(UF, 1, [128, 4, 128]), bc(gv[:, t, :], 2, [128, 4, 128]), ALU.mult, ["CF", "gv"], [K("gtU")])
            tt(gtS[:, :, :], bc(SLF, 1, [128, 4, 128]), bc(gv[:, t, :], 2, [128, 4, 128]), ALU.mult, ["CF", "gv"], [K("gtS")])
            yield
            act(eG[:, t, :], gt[:, 0:4], AF.Exp, [K("gt")], [f"eG{t}"])
            act(dG[:, t, :], gt[:, 4:8], AF.Exp, [K("gt")], [f"dG{t}"])
            tt(ke[:, :], gt[:, 4:8], gt[:, 0:4], ALU.subtract, [K("gt")], [K("ke")])
            act(ke[:, :], ke[:, :], AF.Exp, [K("ke")], [K("ke")])
            b1 = bank()
            mm(ps[b1][:, :], SLF, gtU[:, :, :].rearrange("p h l -> p (h l)"), True, True, ["CF", K("gtU")], [f"ps{b1}"])
            b2 = bank()
            mm(ps[b2][:, :], UF, gtS[:, :, :].rearrange("p h l -> p (h l)"), True, True, ["CF", K("gtS")], [f"ps{b2}"])
            bk = bank()
            for h in range(4):
                mm(ps[bk][:, h * 128:(h + 1) * 128], kT[:, h, tsl], kT[:, h, tsl], True, True, kTk, [f"ps{bk}"])
            bq = bank()
            for h in range(4):
                mm(ps[bq][:, h * 128:(h + 1) * 128], kT[:, h, tsl], qT[:, h, tsl], True, True, kTk + qTk, [f"ps{bq}"])
            yield
            tt(bg[:, :], beta[:, t, :], eG[:, t, :], ALU.mult, ["beta", f"eG{t}"], [K("bg")])
            act(ETg[:, :, :].rearrange("p h l -> p (h l)"), ps[b1][:, :], AF.Exp, [f"ps{b1}"], [K("ETg")])
            act(Eg[:, :, :].rearrange("p h l -> p (h l)"), ps[b2][:, :], AF.Exp, [f"ps{b2}"], [K("Eg")])
            yield
            tt(Eg[:, :, :], Eg[:, :, :], bc(SLF, 1, [128, 4, 128]), ALU.mult, [K("Eg"), "CF"], [K("Eg")])
            tt(Eg[:, :, :], Eg[:, :, :], bc(beta[:, t, :], 2, [128, 4, 128]), ALU.mult, [K("Eg"), "beta"], [K("Eg")])
            tt(Pm[0][:, :, :], v4(ps[bk][:, :]), Eg[:, :, :], ALU.mult, [f"ps{bk}", K("Eg")], [K("Pm0")])
            yield
            bt_ = bank()
            for h in range(4):
                tr(ps[bt_][:, h * 128:(h + 1) * 128], Pm[0][:, h, :], IDF, [K("Pm0"), "CF"], [f"ps{bt_}"])
            tt(ETg[:, :, :], ETg[:, :, :], bc(UF, 1, [128, 4, 128]), ALU.mult, [K("ETg"), "CF"], [K("ETg")])
            tt(v4(qkTm[:, t, :]), v4(ps[bq][:, :]), ETg[:, :, :], ALU.mult, [f"ps{bq}", K("ETg")], [f"qkTm{t}"])
            yield
            cp(PTm[0][:, :, :], v4(ps[bt_][:, :]), [f"ps{bt_}"], [K("PTm0")], eng='act')
            tt(RT[:, :, :], bc(IDF, 1, [128, 4, 128]), v4(ps[bt_][:, :]), ALU.subtract, ["CF", f"ps{bt_}"], [K("RT")])
            yield
            cur = 0
            for kstep in range(6):
                nxt = 1 - cur
                ba = bank()
                for h in range(4):
                    mm(ps[ba][:, h * 128:(h + 1) * 128], PTm[cur][:, h, :], Pm[cur][:, h, :], True, True,
                       [K(f"Pm{cur}"), K(f"PTm{cur}")], [f"ps{ba}"])
                if kstep < 5:
                    bb = bank()
                    for h in range(4):
                        mm(ps[bb][:, h * 128:(h + 1) * 128], Pm[cur][:, h, :], PTm[cur][:, h, :], True, True,
                           [K(f"Pm{cur}"), K(f"PTm{cur}")], [f"ps{bb}"])
                yield
                cp(Pm[nxt][:, :, :], v4(ps[ba][:, :]), [f"ps{ba}"], [K(f"Pm{nxt}")], eng='act')
                if kstep < 5:
                    cp(PTm[nxt][:, :, :], v4(ps[bb][:, :]), [f"ps{bb}"], [K(f"PTm{nxt}")], eng='dve')
                yield
                bc_ = bank()
                for h in range(4):
                    mm(ps[bc_][:, h * 128:(h + 1) * 128], Pm[nxt][:, h, :], RT[:, h, :], True, True, [K(f"Pm{nxt}"), K("RT")], [f"ps{bc_}"])
                yield
                tt(RT[:, :, :], RT[:, :, :], v4(ps[bc_][:, :]), ALU.add, [K("RT"), f"ps{bc_}"], [K("RT")])
                cur = nxt
            cp(v4(TTb[:, t, :]), RT[:, :, :], [K("RT")], [f"TTb{t}"], eng='act')
            tt(v4(vb[:, t, :]), v4(vb[:, t, :]), bc(beta[:, t, :], 2, [128, 4, 128]), ALU.mult, [f"vb{t}", "beta"], [f"vb{t}"])
            tt(v4(kbe[:, :]), v4(k_tm[:, t, :]), bc(bg[:, :], 2, [128, 4, 128]), ALU.mult, [f"k_tm{t}", K("bg")], [K("kbe")])
            tt(v4(k_end[:, t, :]), v4(k_tm[:, t, :]), bc(ke[:, :], 2, [128, 4, 128]), ALU.mult, [f"k_tm{t}", K("ke")], [f"k_end{t}"])
            yield
            bw = bank()
            for h in range(4):
                mm(ps[bw][:, h * 128:(h + 1) * 128], kbe[:, h * 128:(h + 1) * 128], TTb[:, t, h * 128:(h + 1) * 128], True, True,
                   [K("kbe"), f"TTb{t}"], [f"ps{bw}"])
            yield
            act(wT[:, t, :], ps[bw][:, :], AF.Copy, [f"ps{bw}"], [f"wT{t}"], scale=-1.0)

        for t0_ in range(0, NT, NFL):
            run_interleaved([gdn_tile(t0_ + i_) for i_ in range(NFL)])

        p.barrier()
        G1b = Seq(ar, M_SCR + 50688, SCR_END)
        gz = G1b.a("gz", [128, NT, 512], F32)
        Sst = G1b.a("Sst", [128, 512], F32)
        Sb = G1b.a("Sb", [128, 512], BF16)
        Sn = G1b.a("Sn", [128, 512], BF16)
        vnew = G1b.a("vnew", [128, 512], BF16)
        ovs = [G1b.a(f"ov{i}", [128, 512], F32) for i in range(2)]
        otmp = G1b.a("otmp", [128, 512], F32)
        ob = G1b.a("ob", [128, 512], BF16)
        ssq4 = G1b.a("ssq4", [128, 4], F32)
        rs4 = G1b.a("rs4", [128, 4], F32)
        junk2 = G1b.a("junk2", [128, 128], F32)
        def gz_thread():
            for hf in range(4):
                def cons_z(t, pa, pk, hf=hf):
                    act(gz[:, t, hf * 128:(hf + 1) * 128], pa, AF.Silu, [pk], [f"gz{t}_{hf}"])
                    tt(gz[:, t, hf * 128:(hf + 1) * 128], gz[:, t, hf * 128:(hf + 1) * 128], rowp(l, 32, 128), ALU.mult,
                       [f"gz{t}_{hf}", "ROWP"], [f"gz{t}_{hf}"])
                yield from linear_tm_gen(Wtm, 512 + hf * 128, 128, lambda k, t: xn[:, k, HALO + t * 128: HALO + (t + 1) * 128], XNK,
                                         range(NT), cons_z)

        def s_refresh():
            cp(Sb[:, :], Sst[:, :], ["Sst"], ["Sb"], eng='dve')

        def scan_gen(full, from_rcv=None):
            if from_rcv is not None:
                dma(Sst[:, :], rcv[from_rcv].ap()[0:128, :], "Sst", [f"rcv{from_rcv}"], ["Sst"])
                ts(Sst[:, :], Sst[:, :], FLAG, None, ALU.mult, None, ["Sst", "MISC"], ["Sst"])
                s_refresh()
                yield
            for t in range(NT):
                tsl = slice(t * 128, (t + 1) * 128)
                bv = bank()
                for h in range(4):
                    hs = slice(h * 128, (h + 1) * 128)
                    mm(ps[bv][:, hs], TTb[:, t, hs], vb[:, t, hs], True, False, [f"TTb{t}", f"vb{t}"], [f"ps{bv}"])
                    mm(ps[bv][:, hs], wT[:, t, hs], Sb[:, hs], False, True, [f"wT{t}", "Sb"], [f"ps{bv}"])
                if full:
                    bo = bank()
                    for h in range(4):
                        hs = slice(h * 128, (h + 1) * 128)
                        mm(ps[bo][:, hs], qT[:, h, tsl], Sb[:, hs], True, True, qTk + ["Sb"], [f"ps{bo}"])
                yield
                cp(vnew[:, :], ps[bv][:, :], [f"ps{bv}"], ["vnew"], eng='dve')
                yield
                bs = bank()
                for h in range(4):
                    hs = slice(h * 128, (h + 1) * 128)
                    mm(ps[bs][:, hs], k_end[:, t, hs], vnew[:, hs], True, True, [f"k_end{t}", "vnew"], [f"ps{bs}"])
                if full:
                    bo2 = bank()
                    for h in range(4):
                        hs = slice(h * 128, (h + 1) * 128)
                        mm(ps[bo2][:, hs], qkTm[:, t, hs], vnew[:, hs], True, True, [f"qkTm{t}", "vnew"], [f"ps{bo2}"])
                yield
                for h in range(4):
                    hs = slice(h * 128, (h + 1) * 128)
                    stt(Sst[:, hs], Sst[:, hs], dG[:, t, h:h + 1], ps[bs][:, hs], ALU.mult, ALU.add, ["Sst", f"dG{t}", f"ps{bs}"], ["Sst"])
                yield
                s_refresh()
                if full:
                    ovt = ovs[t % 2]
                    tt(v4(otmp[:, :]), v4(ps[bo][:, :]), bc(eG[:, t, :], 2, [128, 4, 128]), ALU.mult, [f"ps{bo}", f"eG{t}"], ["otmp"])
                    yield
                    while fin_state['done'] < t - 1:
                        yield
                    tt(ovt[:, :], ps[bo2][:, :], otmp[:, :], ALU.add, [f"ps{bo2}", "otmp"], [f"ov{t % 2}"])
                    fin_state['ready'] = t + 1
                yield

        fin_state = {'ready': 0, 'done': 0}

        def fin_thread():
            for t in range(NT):
                while fin_state['ready'] <= t:
                    yield
                tsl = slice(t * 128, (t + 1) * 128)
                ovt = ovs[t % 2]
                ok_ = f"ov{t % 2}"
                memset(ssq4[:, :], 0.0, ["ssq4"])
                for h in range(4):
                    act(junk2[:, :], ovt[:, h * 128:(h + 1) * 128], AF.Square, [ok_, "ssq4"], ["junk2", "ssq4"], accum=ssq4[:, h:h + 1])
                yield
                act(rs4[:, :], ssq4[:, :], AF.Sqrt, ["ssq4", "EPSC"], ["rs4"], bias=EPSC[:, 0:1], scale=1.0 / 128)
                yield
                recip(rs4[:, :], rs4[:, :], ["rs4"], ["rs4"])
                yield
                for h in range(4):
                    hs = slice(h * 128, (h + 1) * 128)
                    stt(ob[:, hs], ovt[:, hs], rs4[:, h:h + 1], gz[:, t, hs], ALU.mult, ALU.mult,
                        [ok_, "rs4"] + [f"gz{t}_{hf}" for hf in range(4)], ["ob"])
                fin_state['done'] = t + 1
                yield
                b = bank()
                psb = ps[b][:, :].bitcast(BF16)
                for c in range(4):
                    tr(psb[:, c * 128:(c + 1) * 128], ob[:, c * 128:(c + 1) * 128], IDB, ["ob", "CBF"], [f"ps{b}"])
                yield
                cp(mixed[:, 8:12, tsl], psb[:, 0:512].rearrange("p (c n) -> p c n", c=4), [f"ps{b}"], [f"mixedC{t}"])
                yield

        memset(Sst[:, :], 0.0, ["Sst"])
        s_refresh()
        run_interleaved([scan_gen(False), gz_thread()])
        ex = 2 * l + 1
        dma(snd[ex].ap()[:, :], Sst[:, :], f"snd{ex}", ["Sst"], [f"snd{ex}"])
        exchange(ex, [f"snd{ex}"], [f"rcv{ex}"])

        SB = Seq(ar, G1b.cur, SCR_END)
        rb = SB.a("rb", [128, TT], F32)
        lv = [SB.a(f"lv{i}", [128, TT], F32) for i in range(2)]
        pl = SB.a("pl", [128, T], F32)
        plb = SB.a("plb", [128, T], BF16)
        pwb = SB.a("pwb", [128, 128], BF16)
        pwf = SB.a("pwf", [128, 128], F32)

        def mixer_b():
            for gi in range(4):
                def cons_b(i, bi, pa, pk):
                    t0, n = BLK3[bi]
                    cp(rb[:, t0:t0 + n], pa, [pk], [f"rb_{bi}"])
                yield from linear_fm_gen(Wl, [32 + gi], KC, xsrc, XNKB, BLK3, cons_b)
                rk = [f"rb_{b_}" for b_ in range(3)]
                w = 2 ** (gi + 1)
                srcb, sk, lo = rb, rk, 0
                for lev in range(gi + 1):
                    sh = 2 ** lev
                    dstb = lv[lev % 2]
                    nlo = lo + sh
                    tt(dstb[:, nlo:TT], srcb[:, nlo:TT], srcb[:, nlo - sh:TT - sh], ALU.add, sk, [f"lv{lev % 2}"])
                    srcb, sk, lo = dstb, [f"lv{lev % 2}"], nlo
                    yield
                stt(pl[:, :], srcb[:, HALO:TT], 1.0 / w, rb[:, HALO:TT], ALU.mult, ALU.subtract, sk + rk, ["pl"])
                tt(lv[(gi + 1) % 2][:, 0:HALO], srcb[:, HALO:2 * HALO], MISC[:, 16 + gi * 16: 32 + gi * 16], ALU.mult, sk + ["MISC"],
                   [f"lv{(gi + 1) % 2}"])
                yield
                tt(pl[:, 0:HALO], lv[(gi + 1) % 2][:, 0:HALO], rb[:, HALO:2 * HALO], ALU.subtract, [f"lv{(gi + 1) % 2}", "pl"] + rk, ["pl"])
                dma(pwf[:, :], pool_w[l, gi], "pwf", [], ["pwf"])
                yield
                cp(plb[:, :], pl[:, :], ["pl"], ["plb"], eng='act')
                cp(pwb[:, :], pwf[:, :], ["pwf"], ["pwb"], eng='act')
                yield
                for hb_ in range(2):
                    b = bank()
                    mm(ps[b][:, :], pwb[:, :], plb[:, hb_ * 512:(hb_ + 1) * 512], True, True, ["pwb", "plb"], [f"ps{b}"])
                    act(mixed[:, 4 + gi, hb_ * 512:(hb_ + 1) * 512], ps[b][:, :], AF.Identity, [f"ps{b}", "COLP"], [f"mixedB{gi}_{hb_}"],
                        scale=COLP[:, LC + 124 + gi: LC + 125 + gi])
                yield

        run_interleaved([delay_gen(scan_gen(True, from_rcv=ex), 24), fin_thread(), slow_gen(mixer_b(), 2)],
                        pools=[[0, 1, 2, 3], [4], [5, 6, 7]])

        p.barrier()
        R = Seq(ar, M_SCR, SCR_END)
        x = R.a("x", [128, KC, T], F32)
        y = R.a("y", [128, KC, 512], F32)
        sqb = [R.a(f"rsqb{i}", [128, 512], BF16) for i in range(3)]
        rtmp = R.a("rrtmp", [128, 512], F32)
        rstd = R.a("rrstd", [128, 512], F32)
        mnT = R.a("mnT", [128, KC, MEM], BF16)
        eT = [ar.at(f"eT{i}", [128, 2, 512], BF16, R.off["mnT"] + 2048 * i) for i in range(2)]
        rden = ar.at("rden", [128, 512], F32, R.off["mnT"] + 4096)
        vTf = [R.a(f"vTf{i}", [128, MEM], BF16) for i in range(2)]
        xn2 = ar.at("xn2", [128, KC, 512], BF16, M_XN)
        q = ar.at("q", [128, KC, 512], BF16, M_XN + 16384)
        memst = ar.at("memst", [128, KC, MEM], F32, M_XN + 16384)
        o = ar.at("o", [128, KC, 512], BF16, M_MIX)
        kTm = ar.at("kTm", [128, KC, MEM], BF16, M_MIX + 16384)
        vm = ar.at("vm", [128, 2, D], BF16, M_MIX + 24576)
        hff = ar.at("hff", [128, FKC, 512], BF16, M_XN + 16384)
        sg = rden
        xv = x_src.rearrange("(k p) t -> p k t", p=128)
        xdv = x_dst.rearrange("(k p) t -> p k t", p=128)

        def post_norm_add(row, tb):
            rk = sumsq_rstd(lambda k: y[:, k, :], KC, 512, sqb, rstd, rtmp, "y")
            for k in range(KC):
                stt(y[:, k, :], y[:, k, :], gcol(l, row, k), rstd[:, :], ALU.mult, ALU.mult, ["y", rk, f"GS{l}"], ["y"])
                tt(x[:, k, tb * 512:(tb + 1) * 512], x[:, k, tb * 512:(tb + 1) * 512], y[:, k, :], ALU.add, [f"x{tb}", "y"], [f"x{tb}"])

        def pre_norm(row, tb, dst, dkey):
            rk = sumsq_rstd(lambda k: x[:, k, tb * 512:(tb + 1) * 512], KC, 512, sqb, rstd, rtmp, f"x{tb}")
            for k in range(KC):
                stt(dst[:, k, :], x[:, k, tb * 512:(tb + 1) * 512], gcol(l, row, k), rstd[:, :], ALU.mult, ALU.mult,
                    [f"x{tb}", rk, f"GS{l}"], [dkey])

        def cons_y(i, bi, pa, pk):
            cp(y[:, i, :], pa, [pk], ["y"])

        for tb in range(2):
            dma(x[:, :, tb * 512:(tb + 1) * 512], xv[:, :, tb * 512:(tb + 1) * 512], f"xld{tb}", [], [f"x{tb}"])
        dma(memst[:, :, :], memT_in.rearrange("(k p) t -> p k t", p=128), "memst", [], ["memst"])
        rk = sumsq_rstd(lambda k: memst[:, k, :], KC, MEM, sqb, rstd, rtmp, "memst")
        for k in range(KC):
            stt(mnT[:, k, :], memst[:, k, :], gcol(l, 4, k), rstd[:, 0:MEM], ALU.mult, ALU.mult, ["memst", rk, f"GS{l}"], ["mnT"])
        def w_out_tb(tb):
            linear_fm(w_out[l], list(range(KC)), KC, lambda k, t0, n, tb=tb: mixed[:, k, tb * 512:(tb + 1) * 512], ["mixedR"],
                      [(0, 512)], cons_y)
        w_out_tb(0)
        post_norm_add(1, 0)
        w_out_tb(1)

        linear_fm(xa_wk[l], list(range(KC)), KC, lambda k, t0, n: mnT[:, k, :], ["mnT"], [(0, MEM)],
                  lambda i, bi, pa, pk: cp(kTm[:, i, :], pa, [pk], ["kTm", "mixedR"]))
        post_norm_add(1, 1)
        pre_norm(2, 0, xn2, "xn2")
        pendv = []

        def cons_v(i, bi, pa, pk):
            buf = vTf[i % 2]
            cp(buf[:, :], pa, [pk], [f"vTf{i % 2}"])
            while pendv:
                pendv.pop(0)()

            def trs(i=i, buf=buf):
                b = bank()
                psb = ps[b][:, :].bitcast(BF16)
                for mt in range(2):
                    tr(psb[:, mt * 128:(mt + 1) * 128], buf[:, mt * 128:(mt + 1) * 128], IDB, [f"vTf{i % 2}", "CBF"], [f"ps{b}"])
                cp(vm[:, :, i * 128:(i + 1) * 128], psb[:, 0:256].rearrange("p (m n) -> p m n", m=2), [f"ps{b}"], ["vm", "mixedR"])
            pendv.append(trs)
        linear_fm(xa_wv[l], list(range(KC)), KC, lambda k, t0, n: mnT[:, k, :], ["mnT"], [(0, MEM)], cons_v)
        while pendv:
            pendv.pop(0)()

        p.barrier()
        for tb in range(2):
            linear_fm(xa_wq[l], list(range(KC)), KC, lambda k, t0, n: xn2[:, k, :], ["xn2"], [(0, 512)],
                      lambda i, bi, pa, pk: cp(q[:, i, :], pa, [pk], ["q"]))
            if tb == 0:
                pre_norm(2, 1, xn2, "xn2")
            else:
                pre_norm(5, 0, xn2, "xn2")
            def scores(h):
                e_ = eT[h % 2]
                ek = f"eT{h % 2}"
                for mt in range(2):
                    b = bank()
                    for c in range(4):
                        mm(ps[b][:, :], kTm[:, h * 4 + c, mt * 128:(mt + 1) * 128], q[:, h * 4 + c, :], c == 0, c == 3,
                           ["kTm", "q"], [f"ps{b}"])
                    act(e_[:, mt, :], ps[b][:, :], AF.Exp, [f"ps{b}"], [ek], scale=float(512 ** -0.5))
            scores(0)
            for h in range(4):
                e_ = eT[h % 2]
                ek = f"eT{h % 2}"
                if h + 1 < 4:
                    scores(h + 1)
                bd = bank()
                for mt in range(2):
                    mm(ps[bd][:, :], ONESB, e_[:, mt, :], mt == 0, mt == 1, ["CBF", ek], [f"ps{bd}"])
                act(rden[:, :], ps[bd][:, :], AF.Ln, [f"ps{bd}"], ["rden"])
                act(rden[:, :], rden[:, :], AF.Exp, ["rden"], ["rden"], scale=-1.0)
                for c in range(4):
                    b = bank()
                    for mt in range(2):
                        mm(ps[b][:, :], vm[:, mt, h * 512 + c * 128: h * 512 + (c + 1) * 128], e_[:, mt, :], mt == 0, mt == 1,
                           ["vm", ek], [f"ps{b}"])
                    tt(o[:, h * 4 + c, :], ps[b][:, :], rden[:, :], ALU.mult, [f"ps{b}", "rden"], ["o"])
            linear_fm(xa_wo[l], list(range(KC)), KC, lambda k, t0, n: o[:, k, :], ["o"], [(0, 512)], cons_y)
            post_norm_add(3, tb)

        for tb in range(2):
            for j in range(FKC):
                vg, kg = load_unit(w_gu[l][j], KC)
                vu, ku = load_unit(w_gu[l][FKC + j], KC)
                bg_ = bank()
                for k in range(KC):
                    mm(ps[bg_][:, :], vg[:, k, :], xn2[:, k, :], k == 0, k == KC - 1, [kg, "xn2"], [f"ps{bg_}"])
                bu_ = bank()
                for k in range(KC):
                    mm(ps[bu_][:, :], vu[:, k, :], xn2[:, k, :], k == 0, k == KC - 1, [ku, "xn2"], [f"ps{bu_}"])
                act(sg[:, :], ps[bg_][:, :], AF.Silu, [f"ps{bg_}"], ["sg", "rden"])
                tt(hff[:, j, :], sg[:, :], ps[bu_][:, :], ALU.mult, ["sg", f"ps{bu_}"], ["hff", "q", "o", "kTm", "vm"])
            if tb == 0:
                pre_norm(5, 1, xn2, "xn2")
            linear_fm(w_down[l], list(range(KC)), FKC, lambda k, t0, n: hff[:, k, :], ["hff"], [(0, 512)], cons_y)
            post_norm_add(6, tb)
            dma(xdv[:, :, tb * 512:(tb + 1) * 512], x[:, :, tb * 512:(tb + 1) * 512], f"xst{tb}", [f"x{tb}"], [f"xdst{tb}"])
        if l + 1 < n_layers:
            dma(snd[4].ap()[:, 0:256].rearrange("p (k t) -> p k t", k=KC), x[:, :, T - HALO:T], "snd4", ["x1"], ["snd4"])
            exchange(4, ["snd4"], ["rcv4"])
        return [f"xst{tb}" for tb in range(2)]

    final_streams = None
    for l in range(n_layers):
        last = (l == n_layers - 1)
        final_streams = layer(l, xT_in if l == 0 else xs, None if l == 0 else 4, out if last else xs)

    p.plan()
    with contextlib.ExitStack() as es:
        sems = {e: es.enter_context(nc.semaphore(f"s_{e}")) for e in p.ENG}
        ssems = {s: es.enter_context(nc.semaphore(f"d_{s}")) for s in p.stream_cnt}
        block = es.enter_context(nc.Block())

        @block.tensor
        def _(e):
            p.emit_engine('pe', e, sems, ssems)

        @block.scalar
        def _(e):
            p.emit_engine('act', e, sems, ssems)

        @block.vector
        def _(e):
            p.emit_engine('dve', e, sems, ssems)

        @block.gpsimd
        def _(e):
            p.emit_engine('pool', e, sems, ssems)

        @block.sync
        def _(e):
            p.emit_engine('sp', e, sems, ssems)
            for s in p.stream_cnt:
                if s.startswith("xst") or s.startswith("xld"):
                    e.wait_ge(ssems[s], p.stream_cnt[s])
    return nc, p


_CACHE = {}


def _host_prep(inputs):
    f = lambda a: np.ascontiguousarray(np.asarray(a, dtype=np.float32))
    x = f(inputs['x'])
    mem = f(inputs['mem'])
    cf = np.zeros((128, 512), np.float32)
    cf[:, 0:128] = np.eye(128)
    cf[:, 128:256] = 1.0
    cf[:, 256:384] = np.triu(np.ones((128, 128), np.float32))
    cf[:, 384:512] = np.tril(np.ones((128, 128), np.float32), -1)
    rowp = np.zeros((1, 2 * 672), np.float32)
    colp = np.zeros((128, 2 * 216), np.float32)
    for l in range(2):
        r = rowp[0, l * 672:(l + 1) * 672]
        r[0:4] = inputs['gdn_A_log'][l]
        r[4:8] = inputs['gdn_dt_bias'][l]
        r[8:16] = inputs['ssm_A_log'][l]
        r[16:24] = inputs['ssm_dt_bias'][l]
        r[24:32] = inputs['ssm_D'][l]
        r[32:160] = inputs['gdn_norm_g'][l]
        r[160:672] = inputs['ssm_norm_g'][l]
        c = colp[:, l * 216:(l + 1) * 216]
        c[:, 0:112] = np.asarray(inputs['norm_g'][l]).reshape(7, 16, 128).transpose(2, 0, 1).reshape(128, 112)
        c[:, 112:124] = np.asarray(inputs['conv_a_w'][l]).reshape(3, 4, 128).transpose(2, 1, 0).reshape(128, 12)
        c[:, 124:128] = np.asarray(inputs['pool_scale'][l]).reshape(4, 128).T
        c[:, 128:176] = np.asarray(inputs['gdn_conv_w'][l]).reshape(4, 12, 128).transpose(2, 1, 0).reshape(128, 48)
        c[:, 176:208] = np.asarray(inputs['ssm_conv_w'][l]).reshape(4, 8, 128).transpose(2, 1, 0).reshape(128, 32)
        c[:, 208:216] = np.asarray(inputs['ssm_conv_b'][l]).reshape(8, 128).T
    rowp = np.ascontiguousarray(np.broadcast_to(rowp, (128, 2 * 672)))
    shared = dict(cf=cf, rowp=rowp, colp=colp)

    def tile_w(w):
        w = f(w)
        L, K, N = w.shape
        return np.ascontiguousarray(w.reshape(L, K // 128, 128, N // 128, 128).transpose(0, 3, 2, 1, 4))
    w_in_ = f(inputs['w_in'])
    a_cols = []
    for j in range(4):
        for base in (A_C, A_H, A_B):
            a_cols.append(np.arange(base + 128 * j, base + 128 * (j + 1)))
    fm_cols = np.concatenate([np.arange(D_XBC, D_XBC + 1024), np.arange(C_QKV, C_QKV + 1536)] + a_cols + [np.arange(B_U, B_U + 512)])
    tm_cols = np.concatenate([np.arange(D_Z, D_Z + 512), np.arange(C_Z, C_Z + 512), np.arange(D_DT, D_DT + 8), np.arange(C_AB, C_AB + 8)])
    shared['w_in_fm'] = tile_w(w_in_[:, :, fm_cols])
    shared['w_in_tm'] = np.ascontiguousarray(w_in_[:, :, tm_cols])
    shared['pool_w'] = f(inputs['pool_w'])
    shared['w_out_t'] = tile_w(inputs['w_out'])
    shared['xa_wq_t'] = tile_w(inputs['xa_wq'])
    wkv = f(inputs['xa_wkv'])
    shared['xa_wk_t'] = tile_w(wkv[:, :, :D])
    shared['xa_wv_t'] = tile_w(wkv[:, :, D:])
    shared['xa_wo_t'] = tile_w(inputs['xa_wo'])
    shared['ffn_w_gu_t'] = tile_w(inputs['ffn_w_gu'])
    shared['ffn_w_down_t'] = tile_w(inputs['ffn_w_down'])
    in_maps = []
    for core in range(8):
        b, s = core // 2, core % 2
        m = dict(shared)
        m['xT'] = np.ascontiguousarray(x[b, s * T:(s + 1) * T, :].T)
        if s == 0:
            m['xh'] = np.zeros((D, HALO), np.float32)
        else:
            m['xh'] = np.ascontiguousarray(x[b, T - HALO:T, :].T)
        m['memT'] = np.ascontiguousarray(mem[b].T)
        misc = np.zeros((128, 80), np.float32)
        misc[:, 0] = float(s)
        for gi in range(4):
            w = 2 ** (gi + 1)
            for t in range(HALO):
                misc[:, 16 + gi * 16 + t] = 1.0 / (min(t + 1, w) if s == 0 else w)
        m['misc'] = misc
        in_maps.append(m)
    return in_maps


def kernel(**inputs):
    if 'nc' not in _CACHE:
        _CACHE['nc'] = build()[0]
    nc = _CACHE['nc']
    in_maps = _host_prep(inputs)
    res = run_bass_kernel_spmd(nc, in_maps, core_ids=list(range(8)))
    outp = np.zeros((4, 2 * T, D), np.float32)
    for core in range(8):
        b, s = core // 2, core % 2
        outp[b, s * T:(s + 1) * T, :] = res.results[core]['out'].T
    return outp
```

```python
import contextlib
import numpy as np
import concourse.bass as bass
import concourse.mybir as mybir
from concourse.bass_utils import run_bass_kernel_spmd

F32 = mybir.dt.float32
BF16 = mybir.dt.bfloat16
ALU = mybir.AluOpType
AF = mybir.ActivationFunctionType

D = 2048
KC = 16
T = 1024
HALO = 16
TT = T + HALO
NT = 8
DFF = 5632
FKC = 44
MEM = 256
EPS = 1e-6
B0 = 16512
SBUF_END = 229376

A_B, A_C, A_H = 0, 512, 1024
B_U = 1536
C_QKV, C_Z, C_AB = 2048, 3584, 4096
D_Z, D_XBC, D_DT = 4104, 4616, 5640


class Prog:
    ENG = ('pe', 'act', 'dve', 'pool', 'sp')

    def __init__(self, nc):
        self.nc = nc
        self.ops = {e: [] for e in self.ENG}
        self.lastw = {}
        self.readers = {}
        self.stream_cnt = {}
        self.pool_streams = set()

    def op(self, eng, emit, reads=(), writes=(), stream=None, inc=16):
        if eng == 'pool' and stream is not None:
            self.pool_streams.add(stream)
        deps = []
        for k in reads:
            t = self.lastw.get(k)
            if t is not None:
                deps.append(t)
        for k in writes:
            t = self.lastw.get(k)
            if t is not None:
                deps.append(t)
            deps.extend(self.readers.get(k, ()))
        idx = len(self.ops[eng])
        if stream is not None:
            c = self.stream_cnt.get(stream, 0) + inc
            self.stream_cnt[stream] = c
            tok = ('s', stream, c)
        else:
            tok = ('e', eng, idx)
        waits = [d for d in deps if not (d[0] == 'e' and d[1] == eng and eng == 'pe')]
        self.ops[eng].append(dict(emit=emit, waits=waits, stream=stream, inc=inc))
        for k in writes:
            self.lastw[k] = tok
            self.readers[k] = []
        for k in reads:
            self.readers.setdefault(k, []).append(tok)
        return tok

    def barrier(self):
        toks = []
        for e in self.ENG:
            real = [i for i, o in enumerate(self.ops[e]) if o['emit'] is not None and o['stream'] is None]
            if real:
                toks.append(('e', e, real[-1]))
        for s, c in self.stream_cnt.items():
            if s not in self.pool_streams:
                toks.append(('s', s, c))
        for e in self.ENG:
            if e != 'pool':
                self.ops[e].append(dict(emit=None, waits=list(toks), stream=None, inc=0))
        keep = ("W", "snd", "rcv")
        self.lastw = {k: v for k, v in self.lastw.items() if k.startswith(keep)}
        self.readers = {k: v for k, v in self.readers.items() if k.startswith(keep)}

    def plan(self):
        needed = {e: set() for e in self.ENG}
        plan = {}
        for e in self.ENG:
            waited = {}
            out = []
            for o in self.ops[e]:
                best = {}
                for d in o['waits']:
                    key = (d[0], d[1])
                    if d[2] > best.get(key, -1):
                        best[key] = d[2]
                w = []
                for key, v in best.items():
                    if v > waited.get(key, -1):
                        waited[key] = v
                        w.append((key[0], key[1], v))
                        if key[0] == 'e':
                            needed[key[1]].add(v)
                out.append(w)
            plan[e] = out
        self.rank = {}
        for e in self.ENG:
            self.rank[e] = {i: c + 1 for c, i in enumerate(sorted(needed[e]))}
        self._plan = plan
        self.needed = needed

    def emit_engine(self, e, eng, sems, ssems):
        plan = self._plan[e]
        for i, o in enumerate(self.ops[e]):
            for (kind, src, v) in plan[i]:
                if kind == 'e':
                    eng.wait_ge(sems[src], self.rank[src][v])
                else:
                    eng.wait_ge(ssems[src], v)
            if o['emit'] is None:
                continue
            ins = o['emit'](eng)
            if o['stream'] is not None:
                ins.then_inc(ssems[o['stream']], o['inc'])
            elif i in self.needed[e]:
                ins.then_inc(sems[e], 1)


class Arena:
    def __init__(self, nc):
        self.nc = nc
        self.n = 0

    def at(self, name, shape, dtype, off):
        esz = 4 if dtype == F32 else 2
        size = esz
        for s in shape[1:]:
            size *= s
        assert B0 + off + size <= SBUF_END, (name, off, size)
        self.n += 1
        return self.nc.alloc_sbuf_tensor_at(f"{name}_{self.n}", list(shape), dtype, offset=B0 + off)


class Seq:
    def __init__(self, ar, lo, hi):
        self.ar, self.lo, self.hi, self.cur = ar, lo, hi, lo

    def a(self, name, shape, dtype):
        esz = 4 if dtype == F32 else 2
        size = esz
        for s in shape[1:]:
            size *= s
        size = (size + 63) // 64 * 64
        off = self.cur
        self.cur += size
        assert self.cur <= self.hi, (name, self.cur, self.hi)
        if not hasattr(self, 'off'):
            self.off = {}
        self.off[name] = off
        return self.ar.at(name, shape, dtype, off)


def build(n_layers=2, debug=False):
    nc = bass.Bass("TRN2", target_bir_lowering=False)
    p = Prog(nc)
    ar = Arena(nc)

    def din(name, shape):
        return nc.dram_tensor(name, list(shape), F32, kind="ExternalInput").ap()

    xT_in = din("xT", [D, T])
    xh_in = din("xh", [D, HALO])
    memT_in = din("memT", [D, MEM])
    cf_in = din("cf", [128, 512])
    rowp_in = din("rowp", [128, 2 * 672])
    colp_in = din("colp", [128, 2 * 216])
    misc_in = din("misc", [128, 80])
    w_in_fm = din("w_in_fm", [2, 36, 128, KC, 128])
    w_in_tm = din("w_in_tm", [2, D, 1040])
    pool_w = din("pool_w", [2, 4, 128, 128])
    w_out = din("w_out_t", [2, KC, 128, KC, 128])
    xa_wq = din("xa_wq_t", [2, KC, 128, KC, 128])
    xa_wk = din("xa_wk_t", [2, KC, 128, KC, 128])
    xa_wv = din("xa_wv_t", [2, KC, 128, KC, 128])
    xa_wo = din("xa_wo_t", [2, KC, 128, KC, 128])
    w_gu = din("ffn_w_gu_t", [2, 2 * FKC, 128, KC, 128])
    w_down = din("ffn_w_down_t", [2, KC, 128, FKC, 128])
    out = nc.dram_tensor("out", [D, T], F32, kind="ExternalOutput").ap()
    xs = nc.dram_tensor("xs", [D, T], F32).ap()
    snd = [nc.dram_tensor(f"snd{i}", [128, 512 if i < 4 else 256], F32) for i in range(5)]
    rcv = [nc.dram_tensor(f"rcv{i}", [256, 512 if i < 4 else 256], F32) for i in range(5)]
    dbg = {}
    if debug:
        dbg['mixed'] = nc.dram_tensor("dbg_mixed", [D, T], F32, kind="ExternalOutput").ap()

    P = Seq(ar, 0, 34304)
    CF = P.a("cf", [128, 512], F32)
    IDF, ONESF, UF, SLF = CF[:, 0:128], CF[:, 128:256], CF[:, 256:384], CF[:, 384:512]
    CBF = P.a("cbf", [128, 256], BF16)
    IDB, ONESB = CBF[:, 0:128], CBF[:, 128:256]
    ROWP = P.a("rowp", [128, 1344], F32)
    COLP = P.a("colp", [128, 432], F32)
    MISC = P.a("misc", [128, 80], F32)
    FLAG = MISC[:, 0:1]
    EPSC = P.a("epsc", [128, 4], F32)
    GS = P.a("gs", [128, 224], F32)
    NSLOT = 5
    WS = [P.a(f"wslot{i}", [128, 2048], BF16) for i in range(NSLOT)]
    PEND = P.cur
    ps = [nc.alloc_psum_tensor(f"ps{i}", [128, 512], F32) for i in range(8)]
    st = dict(bank=0, wslot=0, ev=0, pool=None, pidx={})

    def bank():
        b = st['bank']
        st['bank'] = (b + 1) % 8
        pool = st.get('pool')
        if pool is not None:
            i = st['pidx'].get(id(pool), 0)
            st['pidx'][id(pool)] = i + 1
            return pool[i % len(pool)]
        return b

    def wslot():
        s = st['wslot']
        st['wslot'] = (s + 1) % NSLOT
        return s

    def mm(o, lhsT, rhs, start, stop, r, w):
        p.op('pe', lambda e: e.matmul(o, lhsT=lhsT, rhs=rhs, start=start, stop=stop), reads=r, writes=w)

    def tr(o, i, ident, r, w):
        p.op('pe', lambda e: e.transpose(o, i, ident), reads=r, writes=w)

    def act(o, i, func, r, w, bias=None, scale=None, accum=None):
        kw = {}
        if bias is not None:
            kw['bias'] = bias
        if scale is not None:
            kw['scale'] = scale
        if accum is not None:
            kw['accum_out'] = accum
        p.op('act', lambda e: e.activation(out=o, in_=i, func=func, **kw), reads=r, writes=w)

    def tt(o, a, b, op, r, w, eng='dve'):
        p.op(eng, lambda e: e.tensor_tensor(out=o, in0=a, in1=b, op=op), reads=r, writes=w)

    def ts(o, a, s1, s2, op0, op1, r, w, eng='dve'):
        if op1 is None:
            p.op(eng, lambda e: e.tensor_scalar(out=o, in0=a, scalar1=s1, scalar2=None, op0=op0), reads=r, writes=w)
        else:
            p.op(eng, lambda e: e.tensor_scalar(out=o, in0=a, scalar1=s1, scalar2=s2, op0=op0, op1=op1), reads=r, writes=w)

    def stt(o, a, s, b, op0, op1, r, w, eng='dve'):
        p.op(eng, lambda e: e.scalar_tensor_tensor(out=o, in0=a, scalar=s, in1=b, op0=op0, op1=op1), reads=r, writes=w)

    def cp(o, i, r, w, eng=None):
        if eng is None:
            st['ev'] ^= 1
            eng = 'act' if st['ev'] else 'dve'
        if eng == 'act':
            act(o, i, AF.Copy, r, w)
        else:
            p.op('dve', lambda e: e.tensor_copy(out=o, in_=i), reads=r, writes=w)

    def recip(o, i, r, w):
        p.op('dve', lambda e: e.reciprocal(out=o, in_=i), reads=r, writes=w)

    def memset(o, v, w):
        p.op('dve', lambda e: e.memset(o, v), writes=w)

    def dma(o, i, stream, r, w, q='sp'):
        p.op(q, lambda e: e.dma_start(out=o, in_=i), reads=r, writes=w, stream=stream)

    def exchange(i, r, w):
        p.op('pool', lambda e: e.collective_compute("AllGather", ALU.bypass,
                                                     replica_groups=[[0, 1], [2, 3], [4, 5], [6, 7]],
                                                     ins=[snd[i].ap()], outs=[rcv[i].ap()]),
             reads=r, writes=w, stream=f"cc{i}", inc=1)

    dma(CF[:, :], cf_in, "c0", [], ["CF"])
    dma(ROWP[:, :], rowp_in, "c1", [], ["ROWP"])
    dma(COLP[:, :], colp_in, "c2", [], ["COLP"])
    dma(MISC[:, :], misc_in, "c3", [], ["MISC"])
    cp(CBF[:, :], CF[:, 0:256], ["CF"], ["CBF"], eng='dve')
    memset(EPSC[:, 0:1], EPS, ["EPSC"])
    memset(EPSC[:, 1:2], D * EPS, ["EPSC1"])
    for l in range(2):
        ts(GS[:, l * 112:(l + 1) * 112], COLP[:, l * 216:l * 216 + 112], float(np.sqrt(D)), None, ALU.mult, None, ["COLP"], [f"GS{l}"])

    def gcol(l, row, kc):
        return GS[:, l * 112 + row * 16 + kc: l * 112 + row * 16 + kc + 1]

    def colp(l, off, n=1):
        return COLP[:, l * 216 + off: l * 216 + off + n]

    def rowp(l, off, n):
        return ROWP[:, l * 672 + off: l * 672 + off + n]

    def load_unit(src_ap, kcn, ncols=128):
        s = wslot()
        view = WS[s][:, 0:kcn * ncols].rearrange("p (k n) -> p k n", k=kcn)
        p.op('pool', lambda e: e.dma_start(out=view, in_=src_ap), reads=[], writes=[f"W{s}"], stream=f"w{s}")
        return view, f"W{s}"

    def linear_fm_gen(Wt, tiles, kcn, src, src_keys, blocks, consume, every=8):
        cnt = 0
        for i, nt in enumerate(tiles):
            units = []
            for k0 in range(0, kcn, 16):
                kn = min(16, kcn - k0)
                view, key = load_unit(Wt[nt][:, k0:k0 + kn, :], kn)
                units.append((k0, kn, view, key))
            for bi, (t0, n) in enumerate(blocks):
                b = bank()
                for (k0, kn, view, key) in units:
                    for k in range(kn):
                        mm(ps[b][:, 0:n], view[:, k, :], src(k0 + k, t0, n), (k0 + k) == 0, (k0 + k) == kcn - 1,
                           [key] + (src_keys(bi) if callable(src_keys) else src_keys), [f"ps{b}"])
                        cnt += 1
                        if cnt % every == 0:
                            yield
                consume(i, bi, ps[b][:, 0:n], f"ps{b}")

    def linear_fm(*a, **kw):
        for _ in linear_fm_gen(*a, **kw):
            pass

    def interleaved_gen(gens, pools=None):
        gens = [(g_, (pools[j_] if pools else None)) for j_, g_ in enumerate(gens)]
        while gens:
            nxt_ = []
            for g_, pl_ in gens:
                st['pool'] = pl_
                try:
                    next(g_)
                    nxt_.append((g_, pl_))
                except StopIteration:
                    pass
                st['pool'] = None
            gens = nxt_
            yield

    def run_interleaved(gens, pools=None):
        for _ in interleaved_gen(gens, pools):
            pass

    def linear_tm_gen(W2d, c0, ncols, src, src_keys, tiles, consume, every=8):
        view, key = load_unit(W2d.rearrange("(k p) n -> p k n", p=128)[:, :, c0:c0 + ncols], KC, ncols)
        cnt = 0
        for t in tiles:
            b = bank()
            for k in range(KC):
                mm(ps[b][:, 0:ncols], src(k, t), view[:, k, :], k == 0, k == KC - 1, [key] + src_keys, [f"ps{b}"])
                cnt += 1
                if cnt % every == 0:
                    yield
            consume(t, ps[b][:, 0:ncols], f"ps{b}")

    def linear_tm(*a, **kw):
        for _ in linear_tm_gen(*a, **kw):
            pass

    def slow_gen(gen, k):
        while True:
            for _ in range(k - 1):
                yield
            try:
                next(gen)
            except StopIteration:
                return
            yield

    def delay_gen(gen, n):
        for _ in range(n):
            yield
        yield from gen

    def sumsq_rstd(srcf, nkc, n, sqb, rstd, tmp, key, div_eps_scale=True):
        b = bank()
        for k in range(nkc):
            sq = sqb[k % len(sqb)]
            act(sq[:, 0:n], srcf(k), AF.Square, [key], [f"sq{id(sq)}"])
            mm(ps[b][:, 0:n], ONESB, sq[:, 0:n], k == 0, k == nkc - 1, ["CBF", f"sq{id(sq)}"], [f"ps{b}"])
        act(tmp[:, 0:n], ps[b][:, 0:n], AF.Ln, [f"ps{b}", "EPSC1"], [f"t{id(tmp)}"], bias=EPSC[:, 1:2], scale=1.0)
        act(rstd[:, 0:n], tmp[:, 0:n], AF.Exp, [f"t{id(tmp)}"], [f"r{id(rstd)}"], scale=-0.5)
        return f"r{id(rstd)}"

    M_XN = PEND
    M_MIX = M_XN + 33280
    M_SCR = M_MIX + 32768
    SCR_END = SBUF_END - B0
    xn = ar.at("xn", [128, KC, TT], BF16, M_XN)
    mixed = ar.at("mixed", [128, KC, T], BF16, M_MIX)
    XNK = [f"xn{i}" for i in range(5)]
    BLK3 = [(0, HALO), (HALO, 512), (HALO + 512, 512)]

    def XNKB(bi):
        return [["xn0"], ["xn1", "xn2"], ["xn3", "xn4"]][bi]

    def xsrc(k, t0, n):
        return xn[:, k, t0:t0 + n]

    def bc(ap, axis, shape):
        return ap.unsqueeze(axis).to_broadcast(list(shape))

    def conv_fm(raw, rkey, wcol0, ntap, bias, cacc, dst, func, dkey):
        ck = f"cacc{id(cacc)}"
        if bias is None:
            act(cacc[:, :], raw[:, HALO:TT], AF.Identity, rkey + ["COLP"], [ck], scale=COLP[:, wcol0 + ntap - 1: wcol0 + ntap])
        else:
            act(cacc[:, :], raw[:, HALO:TT], AF.Identity, rkey + ["COLP"], [ck], scale=COLP[:, wcol0 + ntap - 1: wcol0 + ntap], bias=bias)
        for k in range(ntap - 1):
            sh = HALO - (ntap - 1) + k
            stt(cacc[:, :], raw[:, sh:sh + T], COLP[:, wcol0 + k: wcol0 + k + 1], cacc[:, :], ALU.mult, ALU.add, rkey + [ck, "COLP"], [ck])
        if func is not None:
            act(dst, cacc[:, :], func, [ck], dkey)
        return ck

    def softplus_neg_scaled(dst, src, brow, arow, n_t, nh, tmp, keys_r, key_w):
        tt(tmp, src, bc(brow, 1, [128, n_t, nh]), ALU.add, keys_r + ["ROWP"], [key_w + "_t"])
        act(tmp, tmp, AF.Exp, [key_w + "_t"], [key_w + "_t"])
        act(dst, tmp, AF.Ln, [key_w + "_t"], [key_w], bias=1.0)

    def layer(l, x_src, halo_from_rcv, x_dst):
        Wl = w_in_fm[l]
        Wtm = w_in_tm[l]
        LC = l * 216
        p.barrier()
        S = Seq(ar, M_SCR, SCR_END)
        xstb = [S.a(f"xst{i}", [128, KC, 256], F32) for i in range(4)]
        sqb = [S.a(f"sqb{i}", [128, 512], BF16) for i in range(4)]
        rtmpb = [S.a(f"rtmp{i}", [128, 256], F32) for i in range(2)]
        rstdb = [S.a(f"rstd{i}", [128, 256], F32) for i in range(2)]
        xv = x_src.rearrange("(k p) t -> p k t", p=128)
        NB1 = [(0, HALO)] + [(HALO + 256 * i, 256) for i in range(4)]
        for bi, (t0, n) in enumerate(NB1):
            xst = xstb[bi % 4]
            xk = f"xst{bi % 4}"
            if bi == 0:
                if halo_from_rcv is None:
                    dma(xst[:, :, 0:HALO], xh_in.rearrange("(k p) t -> p k t", p=128), xk, [], [xk])
                else:
                    dma(xst[:, :, 0:HALO], rcv[halo_from_rcv].ap()[0:128, 0:256].rearrange("p (k t) -> p k t", k=KC), xk,
                        [f"rcv{halo_from_rcv}"], [xk])
                    ts(xst[:, :, 0:HALO], xst[:, :, 0:HALO], FLAG, None, ALU.mult, None, [xk, "MISC"], [xk])
            else:
                dma(xst[:, :, :], xv[:, :, (bi - 1) * 256: bi * 256], xk, [], [xk])
            rk = sumsq_rstd(lambda k: xst[:, k, 0:n], KC, n, sqb[2 * (bi % 2): 2 * (bi % 2) + 2], rstdb[bi % 2], rtmpb[bi % 2], xk)
            for k in range(KC):
                stt(xn[:, k, t0:t0 + n], xst[:, k, 0:n], gcol(l, 0, k), rstdb[bi % 2][:, 0:n], ALU.mult, ALU.mult,
                    [xk, rk, f"GS{l}"], [f"xn{bi}"])

        p.barrier()
        S = Seq(ar, M_SCR, SCR_END)
        craw = [S.a(f"craw{i}", [128, TT], F32) for i in range(2)]
        cacc = S.a("cacc", [128, T], F32)
        xc = S.a("xc", [128, 4, T], BF16)
        BT = S.a("BT", [128, 2, T], BF16)
        CT = S.a("CT", [128, 2, T], BF16)
        x_tm = S.a("x_tm", [128, NT, 512], BF16)
        B_tm = S.a("B_tm", [128, NT, 256], BF16)
        smD = S.a("smD", [128, NT, 8], F32)
        sm_t = S.a("sm_t", [128, NT, 8], F32)
        dtv = S.a("dtv", [128, NT, 8], F32)
        av = S.a("av", [128, NT, 8], F32)
        Arow = S.a("Arow", [128, 8], F32)
        eA = S.a("eA", [128, NT, 8], F32)
        dA = S.a("dA", [128, NT, 8], F32)
        statesT = S.a("statesT", [128, NT, 512], F32)
        zsD = S.a("zsD", [128, NT, 512], F32)
        DI = S.a("DI", [128, 8, 128], F32)
        hst = S.a("hst", [128, 512], F32)
        hb = S.a("hb", [128, 512], BF16)
        acst = S.a("acst", [128, 16], F32)
        te = S.a("te", [128, 8], F32)
        dec = S.a("dec", [128, 8], F32)
        Xdec = S.a("Xdec", [128, 512], BF16)
        gtri = S.a("gtri", [128, 8, 128], F32)
        E = S.a("E", [128, 8, 128], F32)
        CBTm = S.a("CBTm", [128, 2, 128], F32)
        WTb = S.a("WTb", [128, 8, 128], BF16)
        t1 = S.a("t1", [128, 512], F32)
        yv = S.a("yv", [128, 512], F32)
        yb = S.a("yb", [128, 512], BF16)
        ssq = S.a("ssq", [128, 4], F32)
        rs = S.a("rs", [128, 4], F32)

        def cons_xbc(i, bi, pa, pk):
            t0, n = BLK3[bi]
            r = craw[i % 2]
            cp(r[:, t0:t0 + n], pa, [pk], [f"craw{i % 2}_{bi}"])
            if bi == 2:
                dst, dk = (xc[:, i, :], "xc") if i < 4 else ((BT[:, i - 4, :], "BT") if i < 6 else (CT[:, i - 6, :], "CT"))
                conv_fm(r, [f"craw{i % 2}_{b_}" for b_ in range(3)], LC + 176 + i * 4, 4,
                        COLP[:, LC + 208 + i: LC + 209 + i], cacc, dst, AF.Silu, [dk + str(i)])
        linear_fm(Wl, list(range(0, 8)), KC, xsrc, XNKB, BLK3, cons_xbc)
        xck = [f"xc{i}" for i in range(4)]
        BTk = ["BT4", "BT5"]
        CTk = ["CT6", "CT7"]
        for t in range(NT):
            b = bank()
            psb = ps[b][:, :].bitcast(BF16)
            for i in range(4):
                tr(psb[:, i * 128:(i + 1) * 128], xc[:, i, t * 128:(t + 1) * 128], IDB, xck + ["CBF"], [f"ps{b}"])
            cp(x_tm[:, t, :], psb[:, 0:512], [f"ps{b}"], [f"x_tm{t}"])
            b = bank()
            psb = ps[b][:, :].bitcast(BF16)
            for g in range(2):
                tr(psb[:, g * 128:(g + 1) * 128], BT[:, g, t * 128:(t + 1) * 128], IDB, BTk + ["CBF"], [f"ps{b}"])
            cp(B_tm[:, t, :], psb[:, 0:256], [f"ps{b}"], [f"B_tm{t}"])
        linear_tm(Wtm, 1024, 8, lambda k, t: xn[:, k, HALO + t * 128: HALO + (t + 1) * 128], XNK, range(NT),
                  lambda t, pa, pk: cp(smD[:, t, :], pa, [pk], ["smD"]))
        softplus_neg_scaled(dtv[:, :, :], smD[:, :, :], rowp(l, 16, 8), None, NT, 8, sm_t[:, :, :], ["smD"], "dtv")
        act(Arow[:, :], rowp(l, 8, 8), AF.Exp, ["ROWP"], ["Arow"])
        ts(Arow[:, :], Arow[:, :], -1.0, None, ALU.mult, None, ["Arow"], ["Arow"])
        tt(av[:, :, :], dtv[:, :, :], bc(Arow[:, :], 1, [128, NT, 8]), ALU.mult, ["dtv", "Arow"], ["av"])
        tt(DI[:, :, :], bc(IDF, 1, [128, 8, 128]), bc(rowp(l, 24, 8), 2, [128, 8, 128]), ALU.mult, ["CF", "ROWP"], ["DI"])
        def zd_thread():
            for hf in range(4):
                yield from linear_tm_gen(Wtm, hf * 128, 128, lambda k, t: xn[:, k, HALO + t * 128: HALO + (t + 1) * 128], XNK, range(NT),
                                         lambda t, pa, pk, hf=hf: act(zsD[:, t, hf * 128:(hf + 1) * 128], pa, AF.Silu, [pk], [f"zsD{t}_{hf}"]))

        def states_thread():
            for t in range(NT):
                b = bank()
                mm(ps[b][:, 0:8], UF, av[:, t, :], True, True, ["CF", "av"], [f"ps{b}"])
                mm(ps[b][:, 8:16], ONESF, av[:, t, :], True, True, ["CF", "av"], [f"ps{b}"])
                yield
                cp(acst[:, :], ps[b][:, 0:16], [f"ps{b}"], ["acst"], eng='dve')
                yield
                act(eA[:, t, :], acst[:, 0:8], AF.Exp, ["acst"], [f"eA{t}"])
                act(dA[:, t, :], acst[:, 8:16], AF.Exp, ["acst"], [f"dA{t}"])
                tt(te[:, :], acst[:, 8:16], acst[:, 0:8], ALU.subtract, ["acst"], ["te"])
                yield
                act(te[:, :], te[:, :], AF.Exp, ["te"], ["te"])
                yield
                tt(dec[:, :], te[:, :], dtv[:, t, :], ALU.mult, ["te", "dtv"], ["dec"])
                yield
                tt(Xdec[:, :].rearrange("p (h q) -> p h q", h=8), x_tm[:, t, :].rearrange("p (h q) -> p h q", h=8),
                   bc(dec[:, :], 2, [128, 8, 64]), ALU.mult, ["dec", f"x_tm{t}"], ["Xdec"])
                yield
                b = bank()
                for g in range(2):
                    mm(ps[b][:, g * 256:(g + 1) * 256], B_tm[:, t, g * 128:(g + 1) * 128], Xdec[:, g * 256:(g + 1) * 256], True, True,
                       [f"B_tm{t}", "Xdec"], [f"ps{b}"])
                yield
                cp(statesT[:, t, :], ps[b][:, :], [f"ps{b}"], [f"st{t}"])
        run_interleaved([states_thread(), zd_thread()])

        def h_step(t):
            tt(hst[:, :].rearrange("p (h q) -> p h q", h=8), hst[:, :].rearrange("p (h q) -> p h q", h=8),
               bc(dA[:, t, :], 2, [128, 8, 64]), ALU.mult, ["hst", f"dA{t}"], ["hst"])
            tt(hst[:, :], hst[:, :], statesT[:, t, :], ALU.add, ["hst", f"st{t}"], ["hst"])
        memset(hst[:, :], 0.0, ["hst"])
        for t in range(NT):
            h_step(t)
        ex = 2 * l
        dma(snd[ex].ap()[:, :], hst[:, :], f"snd{ex}", ["hst"], [f"snd{ex}"])
        exchange(ex, [f"snd{ex}"], [f"rcv{ex}"])

        SA = Seq(ar, S.cur, SCR_END)
        ra = [SA.a(f"ra{i}", [128, TT], F32) for i in range(3)]
        p.barrier()
        R1 = Seq(ar, S.off["craw0"], S.off["craw0"] + 8320)
        R2 = Seq(ar, S.off["xc"], S.off["xc"] + 8192)
        R3 = Seq(ar, S.off["B_tm"], S.off["B_tm"] + 4096)
        TS = [dict(gtri=gtri, E=E, CBTm=CBTm, WTb=WTb, t1=t1, yv=yv, yb=yb, hb=hb, ssq=ssq, rs=rs),
              dict(gtri=R1.a("gtri1", [128, 8, 128], F32), E=R1.a("E1", [128, 8, 128], F32),
                   CBTm=R2.a("CBTm1", [128, 2, 128], F32), WTb=R2.a("WTb1", [128, 8, 128], BF16), t1=R2.a("t11", [128, 512], F32),
                   yv=R2.a("yv1", [128, 512], F32), yb=R2.a("yb1", [128, 512], BF16), hb=R3.a("hb1", [128, 512], BF16),
                   ssq=R3.a("ssq1", [128, 4], F32), rs=R3.a("rs1", [128, 4], F32))]

        def mixer_a():
            for j in range(4):
                def cons_a(i, bi, pa, pk, j=j):
                    t0, n = BLK3[bi]
                    cp(ra[i][:, t0:t0 + n], pa, [pk], [f"ra{i}_{bi}"])
                yield from linear_fm_gen(Wl, [20 + 3 * j, 21 + 3 * j, 22 + 3 * j], KC, xsrc, XNKB, BLK3, cons_a)
                rk = [f"ra{i}_{b_}" for i in range(3) for b_ in range(3)]
                tt(ra[0][:, :], ra[0][:, :], ra[1][:, :], ALU.mult, rk, ["ra0_0", "ra0_1", "ra0_2"])
                yield
                ck = conv_fm(ra[0], rk, LC + 112 + j * 3, 3, None, cacc, None, None, None)
                tt(mixed[:, j, :], cacc[:, :], ra[2][:, HALO:TT], ALU.mult, [ck] + rk, [f"mixed{j}"])
                yield

        def ssd_tile(t):
            f_ = t % 2
            Bf = TS[f_]
            gtri, E, CBTm, WTb, t1, yv, yb, hb, ssq, rs = (Bf[k_] for k_ in ("gtri", "E", "CBTm", "WTb", "t1", "yv", "yb", "hb", "ssq", "rs"))
            K = lambda n_: f"{n_}#{f_}"
            tsl = slice(t * 128, (t + 1) * 128)
            tt(gtri[:, :, :], bc(UF, 1, [128, 8, 128]), bc(av[:, t, :], 2, [128, 8, 128]), ALU.mult, ["CF", "av"], [K("gtri")])
            b = bank()
            for g in range(2):
                mm(ps[b][:, g * 128:(g + 1) * 128], BT[:, g, tsl], CT[:, g, tsl], True, True, BTk + CTk, [f"ps{b}"])
            yield
            tt(CBTm[:, :, :], ps[b][:, 0:256].rearrange("p (g l) -> p g l", g=2), bc(UF, 1, [128, 2, 128]), ALU.mult,
               [f"ps{b}", "CF"], [K("CBTm")])
            bq_ = []
            for q in range(2):
                b = bank()
                bq_.append(b)
                mm(ps[b][:, :], SLF, gtri[:, 4 * q:4 * q + 4, :].rearrange("p h l -> p (h l)"), True, True, ["CF", K("gtri")], [f"ps{b}"])
            yield
            for q in range(2):
                act(E[:, 4 * q:4 * q + 4, :].rearrange("p h l -> p (h l)"), ps[bq_[q]][:, :], AF.Exp, [f"ps{bq_[q]}"], [K(f"E{q}")])
            yield
            for g in range(2):
                tt(E[:, 4 * g:4 * g + 4, :], E[:, 4 * g:4 * g + 4, :], bc(CBTm[:, g, :], 1, [128, 4, 128]), ALU.mult,
                   [K(f"E{g}"), K("CBTm")], [K(f"E{g}")])
            yield
            tt(E[:, :, :], E[:, :, :], bc(dtv[:, t, :], 2, [128, 8, 128]), ALU.mult, [K("E0"), K("E1"), "dtv"], [K("E0"), K("E1")])
            yield
            tt(WTb[:, :, :], E[:, :, :], DI[:, :, :], ALU.add, [K("E0"), K("E1"), "DI"], [K("WTb")])
            yield
            by = bank()
            for h in range(8):
                mm(ps[by][:, h * 64:(h + 1) * 64], WTb[:, h, :], x_tm[:, t, h * 64:(h + 1) * 64], True, True, [K("WTb"), f"x_tm{t}"], [f"ps{by}"])
            if t == 0:
                dma(hst[:, :], rcv[ex].ap()[0:128, :], "hst", [f"rcv{ex}"], ["hst"])
                ts(hst[:, :], hst[:, :], FLAG, None, ALU.mult, None, ["hst", "MISC"], ["hst"])
            cp(hb[:, :], hst[:, :], ["hst"], [K("hb")], eng='act')
            h_step(t)
            yield
            bo = bank()
            for g in range(2):
                mm(ps[bo][:, g * 256:(g + 1) * 256], CT[:, g, tsl], hb[:, g * 256:(g + 1) * 256], True, True, CTk + [K("hb")], [f"ps{bo}"])
            yield
            tt(t1[:, :].rearrange("p (h q) -> p h q", h=8), ps[bo][:, :].rearrange("p (h q) -> p h q", h=8),
               bc(eA[:, t, :], 2, [128, 8, 64]), ALU.mult, [f"ps{bo}", f"eA{t}"], [K("t1")])
            yield
            tt(yv[:, :], ps[by][:, :], t1[:, :], ALU.add, [f"ps{by}", K("t1")], [K("yv")])
            yield
            tt(yv[:, :], yv[:, :], zsD[:, t, :], ALU.mult, [K("yv")] + [f"zsD{t}_{hf}" for hf in range(4)], [K("yv")])
            memset(ssq[:, :], 0.0, [K("ssq")])
            yield
            for g in range(2):
                act(t1[:, 0:256], yv[:, g * 256:(g + 1) * 256], AF.Square, [K("yv"), K("ssq")], [K("t1"), K("ssq")], accum=ssq[:, g:g + 1])
            yield
            act(rs[:, 0:2], ssq[:, 0:2], AF.Sqrt, [K("ssq"), "EPSC"], [K("rs")], bias=EPSC[:, 0:1], scale=1.0 / 256)
            yield
            recip(rs[:, 0:2], rs[:, 0:2], [K("rs")], [K("rs")])
            yield
            for g in range(2):
                stt(yb[:, g * 256:(g + 1) * 256], yv[:, g * 256:(g + 1) * 256], rs[:, g:g + 1], rowp(l, 160 + g * 256, 256),
                    ALU.mult, ALU.mult, [K("yv"), K("rs"), "ROWP"], [K("yb")])
            yield
            b = bank()
            psb = ps[b][:, :].bitcast(BF16)
            for c in range(4):
                tr(psb[:, c * 128:(c + 1) * 128], yb[:, c * 128:(c + 1) * 128], IDB, [K("yb"), "CBF"], [f"ps{b}"])
            yield
            cp(mixed[:, 12:16, tsl], psb[:, 0:512].rearrange("p (c n) -> p c n", c=4), [f"ps{b}"], [f"mixedD{t}"])

        def ssd_pairs():
            for t0_ in range(0, NT, 2):
                yield from interleaved_gen([ssd_tile(t0_), ssd_tile(t0_ + 1)])
        run_interleaved([ssd_pairs(), mixer_a()])

        p.barrier()
        G2 = Seq(ar, M_SCR, M_SCR + 50688)
        G1 = Seq(ar, M_SCR + 50688, SCR_END)
        qT = G2.a("qT", [128, 4, T], BF16)
        vb = G2.a("vb", [128, NT, 512], BF16)
        TTb = G2.a("TTb", [128, NT, 512], BF16)
        wT = G2.a("wT", [128, NT, 512], BF16)
        qkTm = G2.a("qkTm", [128, NT, 512], BF16)
        k_end = G2.a("k_end", [128, NT, 512], BF16)
        eG = G2.a("eG", [128, NT, 4], F32)
        dG = G2.a("dG", [128, NT, 4], F32)
        smC = G2.a("smC", [128, NT, 8], F32)
        smt = G2.a("smt", [128, NT, 4], F32)
        gv = G2.a("gv", [128, NT, 4], F32)
        beta = G2.a("beta", [128, NT, 4], F32)
        gArow = G2.a("gArow", [128, 4], F32)
        gt = G2.a("gt", [128, 8], F32)
        ke = G2.a("ke", [128, 4], F32)
        bg = G2.a("bg", [128, 4], F32)
        kT = G1.a("kT", [128, 4, T], BF16)
        k_tm = G1.a("k_tm", [128, NT, 512], BF16)
        G1_TMP = G1.cur
        craw = [G1.a(f"gcraw{i}", [128, TT], F32) for i in range(2)]
        caccs = [G1.a(f"gcacc{i}", [128, T], F32) for i in range(2)]
        qfs = [G1.a(f"qf{i}", [128, T], F32) for i in range(2)]
        vT = G1.a("vT", [128, 4, T], BF16)
        sqg = [G1.a(f"sqg{i}", [128, 512], BF16) for i in range(2)]
        rtgs = [G1.a(f"rtg{i}", [128, 512], F32) for i in range(2)]
        rsgs = [G1.a(f"rsg{i}", [128, 512], F32) for i in range(2)]

        pend = []

        def flush_pend():
            while pend:
                pend.pop(0)()

        def cons_qkv(i, bi, pa, pk):
            t0, n = BLK3[bi]
            r = craw[i % 2]
            cacc = caccs[i % 2]
            qf = qfs[i % 2]
            qk_ = f"qf{i % 2}"
            cp(r[:, t0:t0 + n], pa, [pk], [f"gcraw{i % 2}_{bi}"])
            if bi != 2:
                return
            flush_pend()
            rk = [f"gcraw{i % 2}_{b_}" for b_ in range(3)]
            h = i % 4
            if i >= 8:
                conv_fm(r, rk, LC + 128 + i * 4, 4, None, cacc, vT[:, h, :], AF.Silu, [f"vT{h}"])
                return
            conv_fm(r, rk, LC + 128 + i * 4, 4, None, cacc, qf[:, :], AF.Silu, [qk_])
            for hb_ in range(2):
                sl = slice(hb_ * 512, (hb_ + 1) * 512)
                act(sqg[hb_][:, :], qf[:, sl], AF.Square, [qk_], [f"sqg{hb_}"])

            def l2n(i=i, h=h, qf=qf, qk_=qk_):
                for hb_ in range(2):
                    sl = slice(hb_ * 512, (hb_ + 1) * 512)
                    sq = sqg[hb_]
                    rtg, rsg = rtgs[hb_], rsgs[hb_]
                    b = bank()
                    mm(ps[b][:, :], ONESB, sq[:, :], True, True, ["CBF", f"sqg{hb_}"], [f"ps{b}"])
                    act(rtg[:, :], ps[b][:, :], AF.Ln, [f"ps{b}", "EPSC"], [f"rtg{hb_}"], bias=EPSC[:, 0:1], scale=1.0)
                    act(rsg[:, :], rtg[:, :], AF.Exp, [f"rtg{hb_}"], [f"rsg{hb_}"], scale=-0.5)
                    if i < 4:
                        stt(qT[:, h, sl], qf[:, sl], float(128 ** -0.5), rsg[:, :], ALU.mult, ALU.mult, [qk_, f"rsg{hb_}"], [f"qT{h}"])
                    else:
                        tt(kT[:, h, sl], qf[:, sl], rsg[:, :], ALU.mult, [qk_, f"rsg{hb_}"], [f"kT{h}"])
            pend.append(l2n)
        linear_fm(Wl, list(range(8, 20)), KC, xsrc, XNKB, BLK3, cons_qkv)
        flush_pend()
        qTk = [f"qT{h}" for h in range(4)]
        kTk = [f"kT{h}" for h in range(4)]
        vTk = [f"vT{h}" for h in range(4)]
        for t in range(NT):
            for (src_, sk, dst_, dk) in ((kT, kTk, k_tm, "k_tm"), (vT, vTk, vb, "vb")):
                b = bank()
                psb = ps[b][:, :].bitcast(BF16)
                for h in range(4):
                    tr(psb[:, h * 128:(h + 1) * 128], src_[:, h, t * 128:(t + 1) * 128], IDB, sk + ["CBF"], [f"ps{b}"])
                cp(dst_[:, t, :], psb[:, 0:512], [f"ps{b}"], [f"{dk}{t}"])
        linear_tm(Wtm, 1032, 8, lambda k, t: xn[:, k, HALO + t * 128: HALO + (t + 1) * 128], XNK, range(NT),
                  lambda t, pa, pk: cp(smC[:, t, :], pa, [pk], ["smC"]))
        softplus_neg_scaled(gv[:, :, :], smC[:, :, 0:4], rowp(l, 4, 4), None, NT, 4, smt[:, :, :], ["smC"], "gv")
        act(gArow[:, :], rowp(l, 0, 4), AF.Exp, ["ROWP"], ["gArow"])
        ts(gArow[:, :], gArow[:, :], -1.0, None, ALU.mult, None, ["gArow"], ["gArow"])
        tt(gv[:, :, :], gv[:, :, :], bc(gArow[:, :], 1, [128, NT, 4]), ALU.mult, ["gv", "gArow"], ["gv"])
        act(beta[:, :, :], smC[:, :, 4:8], AF.Sigmoid, ["smC"], ["beta"])

        def v4(ap):
            return ap.rearrange("p (h n) -> p h n", h=4)
        p.barrier()
        G1t = Seq(ar, G1_TMP, SCR_END)
        NFL = 2
        TB = []
        for f_ in range(NFL):
            TB.append(dict(
                gtU=G1t.a(f"gtU{f_}", [128, 4, 128], F32), gtS=G1t.a(f"gtS{f_}", [128, 4, 128], F32),
                Eg=G1t.a(f"Eg{f_}", [128, 4, 128], F32), ETg=G1t.a(f"ETg{f_}", [128, 4, 128], F32),
                Pm=[G1t.a(f"Pm{f_}_{i}", [128, 4, 128], F32) for i in range(2)],
                PTm=[G1t.a(f"PTm{f_}_{i}", [128, 4, 128], F32) for i in range(2)],
                RT=G1t.a(f"RT{f_}", [128, 4, 128], F32), kbe=G1t.a(f"kbe{f_}", [128, 512], BF16),
                gt=G1t.a(f"gt{f_}", [128, 8], F32), ke=G1t.a(f"ke{f_}", [128, 4], F32), bg=G1t.a(f"bg{f_}", [128, 4], F32)))

        def gdn_tile(t):
            f_ = t % NFL
            Bf = TB[f_]
            gtU, gtS, Eg, ETg, Pm, PTm, RT, kbe, gt, ke, bg = (Bf[k_] for k_ in ("gtU", "gtS", "Eg", "ETg", "Pm", "PTm", "RT", "kbe", "gt", "ke", "bg"))
            K = lambda n_: f"{n_}#{f_}"
            tsl = slice(t * 128, (t + 1) * 128)
            b = bank()
            mm(ps[b][:, 0:4], UF, gv[:, t, :], True, True, ["CF", "gv"], [f"ps{b}"])
            mm(ps[b][:, 4:8], ONESF, gv[:, t, :], True, True, ["CF", "gv"], [f"ps{b}"])
            cp(gt[:, :], ps[b][:, 0:8], [f"ps{b}"], [K("gt")], eng='dve')
            tt(gtU[:, :, :], bc(UF, 1, [128, 4, 128]), bc(gv[:, t, :], 2, [128, 4, 128]), ALU.mult, ["CF", "gv"], [K("gtU")])
            tt(gtS[:, :, :], bc(SLF, 1, [128, 4, 128]), bc(gv[:, t, :], 2, [128, 4, 128]), ALU.mult, ["CF", "gv"], [K("gtS")])
            yield
            act(eG[:, t, :], gt[:, 0:4], AF.Exp, [K("gt")], [f"eG{t}"])
            act(dG[:, t, :], gt[:, 4:8], AF.Exp, [K("gt")], [f"dG{t}"])
            tt(ke[:, :], gt[:, 4:8], gt[:, 0:4], ALU.subtract, [K("gt")], [K("ke")])
            act(ke[:, :], ke[:, :], AF.Exp, [K("ke")], [K("ke")])
            b1 = bank()
            mm(ps[b1][:, :], SLF, gtU[:, :, :].rearrange("p h l -> p (h l)"), True, True, ["CF", K("gtU")], [f"ps{b1}"])
            b2 = bank()
            mm(ps[b2][:, :], UF, gtS[:, :, :].rearrange("p h l -> p (h l)"), True, True, ["CF", K("gtS")], [f"ps{b2}"])
            bk = bank()
            for h in range(4):
                mm(ps[bk][:, h * 128:(h + 1) * 128], kT[:, h, tsl], kT[:, h, tsl], True, True, kTk, [f"ps{bk}"])
            bq = bank()
            for h in range(4):
                mm(ps[bq][:, h * 128:(h + 1) * 128], kT[:, h, tsl], qT[:, h, tsl], True, True, kTk + qTk, [f"ps{bq}"])
            yield
            tt(bg[:, :], beta[:, t, :], eG[:, t, :], ALU.mult, ["beta", f"eG{t}"], [K("bg")])
            act(ETg[:, :, :].rearrange("p h l -> p (h l)"), ps[b1][:, :], AF.Exp, [f"ps{b1}"], [K("ETg")])
            act(Eg[:, :, :].rearrange("p h l -> p (h l)"), ps[b2][:, :], AF.Exp, [f"ps{b2}"], [K("Eg")])
            yield
            tt(Eg[:, :, :], Eg[:, :, :], bc(SLF, 1, [128, 4, 128]), ALU.mult, [K("Eg"), "CF"], [K("Eg")])
            tt(Eg[:, :, :], Eg[:, :, :], bc(beta[:, t, :], 2, [128, 4, 128]), ALU.mult, [K("Eg"), "beta"], [K("Eg")])
            tt(Pm[0][:, :, :], v4(ps[bk][:, :]), Eg[:, :, :], ALU.mult, [f"ps{bk}", K("Eg")], [K("Pm0")])
            yield
            bt_ = bank()
            for h in range(4):
                tr(ps[bt_][:, h * 128:(h + 1) * 128], Pm[0][:, h, :], IDF, [K("Pm0"), "CF"], [f"ps{bt_}"])
            tt(ETg[:, :, :], ETg[:, :, :], bc(UF, 1, [128, 4, 128]), ALU.mult, [K("ETg"), "CF"], [K("ETg")])
            tt(v4(qkTm[:, t, :]), v4(ps[bq][:, :]), ETg[:, :, :], ALU.mult, [f"ps{bq}", K("ETg")], [f"qkTm{t}"])
            yield
            cp(PTm[0][:, :, :], v4(ps[bt_][:, :]), [f"ps{bt_}"], [K("PTm0")], eng='act')
            tt(RT[:, :, :], bc(IDF, 1, [128, 4, 128]), v4(ps[bt_][:, :]), ALU.subtract, ["CF", f"ps{bt_}"], [K("RT")])
            yield
            cur = 0
            for kstep in range(6):
                nxt = 1 - cur
                ba = bank()
                for h in range(4):
                    mm(ps[ba][:, h * 128:(h + 1) * 128], PTm[cur][:, h, :], Pm[cur][:, h, :], True, True,
                       [K(f"Pm{cur}"), K(f"PTm{cur}")], [f"ps{ba}"])
                if kstep < 5:
                    bb = bank()
                    for h in range(4):
                        mm(ps[bb][:, h * 128:(h + 1) * 128], Pm[cur][:, h, :], PTm[cur][:, h, :], True, True,
                           [K(f"Pm{cur}"), K(f"PTm{cur}")], [f"ps{bb}"])
                yield
                cp(Pm[nxt][:, :, :], v4(ps[ba][:, :]), [f"ps{ba}"], [K(f"Pm{nxt}")], eng='act')
                if kstep < 5:
                    cp(PTm[nxt][:, :, :], v4(ps[bb][:, :]), [f"ps{bb}"], [K(f"PTm{nxt}")], eng='dve')
                yield
                bc_ = bank()
                for h in range(4):
                    mm(ps[bc_][:, h * 128:(h + 1) * 128], Pm[nxt][:, h, :], RT[:, h, :], True, True, [K(f"Pm{nxt}"), K("RT")], [f"ps{bc_}"])
                yield
                tt(RT[:, :, :], RT[:, :, :], v4(ps[bc_][:, :]), ALU.add, [K("RT"), f"ps{bc_}"], [K("RT")])
                cur = nxt
            cp(v4(TTb[:, t, :]), RT[:, :, :], [K("RT")], [f"TTb{t}"], eng='act')
            tt(v4(vb[:, t, :]), v4(vb[:, t, :]), bc(beta[:, t, :], 2, [128, 4, 128]), ALU.mult, [f"vb{t}", "beta"], [f"vb{t}"])
            tt(v4(kbe[:, :]), v4(k_tm[:, t, :]), bc(bg[:, :], 2, [128, 4, 128]), ALU.mult, [f"k_tm{t}", K("bg")], [K("kbe")])
            tt(v4(k_end[:, t, :]), v4(k_tm[:, t, :]), bc(ke[:, :], 2, [128, 4, 128]), ALU.mult, [f"k_tm{t}", K("ke")], [f"k_end{t}"])
            yield
            bw = bank()
            for h in range(4):
                mm(ps[bw][:, h * 128:(h + 1) * 128], kbe[:, h * 128:(h + 1) * 128], TTb[:, t, h * 128:(h + 1) * 128], True, True,
                   [K("kbe"), f"TTb{t}"], [f"ps{bw}"])
            yield
            act(wT[:, t, :], ps[bw][:, :], AF.Copy, [f"ps{bw}"], [f"wT{t}"], scale=-1.0)

        for t0_ in range(0, NT, NFL):
            run_interleaved([gdn_tile(t0_ + i_) for i_ in range(NFL)])

        p.barrier()
        G1b = Seq(ar, M_SCR + 50688, SCR_END)
        gz = G1b.a("gz", [128, NT, 512], F32)
        Sst = G1b.a("Sst", [128, 512], F32)
        Sb = G1b.a("Sb", [128, 512], BF16)
        Sn = G1b.a("Sn", [128, 512], BF16)
        vnew = G1b.a("vnew", [128, 512], BF16)
        ovs = [G1b.a(f"ov{i}", [128, 512], F32) for i in range(2)]
        otmp = G1b.a("otmp", [128, 512], F32)
        ob = G1b.a("ob", [128, 512], BF16)
        ssq4 = G1b.a("ssq4", [128, 4], F32)
        rs4 = G1b.a("rs4", [128, 4], F32)
        junk2 = G1b.a("junk2", [128, 128], F32)
        def gz_thread():
            for hf in range(4):
                def cons_z(t, pa, pk, hf=hf):
                    act(gz[:, t, hf * 128:(hf + 1) * 128], pa, AF.Silu, [pk], [f"gz{t}_{hf}"])
                    tt(gz[:, t, hf * 128:(hf + 1) * 128], gz[:, t, hf * 128:(hf + 1) * 128], rowp(l, 32, 128), ALU.mult,
                       [f"gz{t}_{hf}", "ROWP"], [f"gz{t}_{hf}"])
                yield from linear_tm_gen(Wtm, 512 + hf * 128, 128, lambda k, t: xn[:, k, HALO + t * 128: HALO + (t + 1) * 128], XNK,
                                         range(NT), cons_z)

        def s_refresh():
            cp(Sb[:, :], Sst[:, :], ["Sst"], ["Sb"], eng='act')

        def scan_gen(full, from_rcv=None):
            if from_rcv is not None:
                dma(Sst[:, :], rcv[from_rcv].ap()[0:128, :], "Sst", [f"rcv{from_rcv}"], ["Sst"])
                ts(Sst[:, :], Sst[:, :], FLAG, None, ALU.mult, None, ["Sst", "MISC"], ["Sst"])
                s_refresh()
                yield
            for t in range(NT):
                tsl = slice(t * 128, (t + 1) * 128)
                bv = bank()
                for h in range(4):
                    hs = slice(h * 128, (h + 1) * 128)
                    mm(ps[bv][:, hs], TTb[:, t, hs], vb[:, t, hs], True, False, [f"TTb{t}", f"vb{t}"], [f"ps{bv}"])
                    mm(ps[bv][:, hs], wT[:, t, hs], Sb[:, hs], False, True, [f"wT{t}", "Sb"], [f"ps{bv}"])
                if full:
                    bo = bank()
                    for h in range(4):
                        hs = slice(h * 128, (h + 1) * 128)
                        mm(ps[bo][:, hs], qT[:, h, tsl], Sb[:, hs], True, True, qTk + ["Sb"], [f"ps{bo}"])
                yield
                cp(vnew[:, :], ps[bv][:, :], [f"ps{bv}"], ["vnew"], eng='act')
                yield
                bs = bank()
                for h in range(4):
                    hs = slice(h * 128, (h + 1) * 128)
                    mm(ps[bs][:, hs], k_end[:, t, hs], vnew[:, hs], True, True, [f"k_end{t}", "vnew"], [f"ps{bs}"])
                if full:
                    bo2 = bank()
                    for h in range(4):
                        hs = slice(h * 128, (h + 1) * 128)
                        mm(ps[bo2][:, hs], qkTm[:, t, hs], vnew[:, hs], True, True, [f"qkTm{t}", "vnew"], [f"ps{bo2}"])
                yield
                for h in range(4):
                    hs = slice(h * 128, (h + 1) * 128)
                    stt(Sst[:, hs], Sst[:, hs], dG[:, t, h:h + 1], ps[bs][:, hs], ALU.mult, ALU.add, ["Sst", f"dG{t}", f"ps{bs}"], ["Sst"])
                yield
                s_refresh()
                if full:
                    ovt = ovs[t % 2]
                    tt(v4(otmp[:, :]), v4(ps[bo][:, :]), bc(eG[:, t, :], 2, [128, 4, 128]), ALU.mult, [f"ps{bo}", f"eG{t}"], ["otmp"])
                    yield
                    while fin_state['done'] < t - 1:
                        yield
                    tt(ovt[:, :], ps[bo2][:, :], otmp[:, :], ALU.add, [f"ps{bo2}", "otmp"], [f"ov{t % 2}"])
                    fin_state['ready'] = t + 1
                yield

        fin_state = {'ready': 0, 'done': 0}

        def fin_thread():
            for t in range(NT):
                while fin_state['ready'] <= t:
                    yield
                tsl = slice(t * 128, (t + 1) * 128)
                ovt = ovs[t % 2]
                ok_ = f"ov{t % 2}"
                memset(ssq4[:, :], 0.0, ["ssq4"])
                for h in range(4):
                    act(junk2[:, :], ovt[:, h * 128:(h + 1) * 128], AF.Square, [ok_, "ssq4"], ["junk2", "ssq4"], accum=ssq4[:, h:h + 1])
                yield
                act(rs4[:, :], ssq4[:, :], AF.Sqrt, ["ssq4", "EPSC"], ["rs4"], bias=EPSC[:, 0:1], scale=1.0 / 128)
                yield
                recip(rs4[:, :], rs4[:, :], ["rs4"], ["rs4"])
                yield
                for h in range(4):
                    hs = slice(h * 128, (h + 1) * 128)
                    stt(ob[:, hs], ovt[:, hs], rs4[:, h:h + 1], gz[:, t, hs], ALU.mult, ALU.mult,
                        [ok_, "rs4"] + [f"gz{t}_{hf}" for hf in range(4)], ["ob"])
                fin_state['done'] = t + 1
                yield
                b = bank()
                psb = ps[b][:, :].bitcast(BF16)
                for c in range(4):
                    tr(psb[:, c * 128:(c + 1) * 128], ob[:, c * 128:(c + 1) * 128], IDB, ["ob", "CBF"], [f"ps{b}"])
                yield
                cp(mixed[:, 8:12, tsl], psb[:, 0:512].rearrange("p (c n) -> p c n", c=4), [f"ps{b}"], [f"mixedC{t}"])
                yield

        memset(Sst[:, :], 0.0, ["Sst"])
        s_refresh()
        run_interleaved([scan_gen(False), gz_thread()])
        ex = 2 * l + 1
        dma(snd[ex].ap()[:, :], Sst[:, :], f"snd{ex}", ["Sst"], [f"snd{ex}"])
        exchange(ex, [f"snd{ex}"], [f"rcv{ex}"])

        SB = Seq(ar, G1b.cur, SCR_END)
        rb = SB.a("rb", [128, TT], F32)
        lv = [SB.a(f"lv{i}", [128, TT], F32) for i in range(2)]
        pl = SB.a("pl", [128, T], F32)
        plb = SB.a("plb", [128, T], BF16)
        pwb = SB.a("pwb", [128, 128], BF16)
        pwf = SB.a("pwf", [128, 128], F32)

        def mixer_b():
            for gi in range(4):
                def cons_b(i, bi, pa, pk):
                    t0, n = BLK3[bi]
                    cp(rb[:, t0:t0 + n], pa, [pk], [f"rb_{bi}"])
                yield from linear_fm_gen(Wl, [32 + gi], KC, xsrc, XNKB, BLK3, cons_b)
                rk = [f"rb_{b_}" for b_ in range(3)]
                w = 2 ** (gi + 1)
                srcb, sk, lo = rb, rk, 0
                for lev in range(gi + 1):
                    sh = 2 ** lev
                    dstb = lv[lev % 2]
                    nlo = lo + sh
                    tt(dstb[:, nlo:TT], srcb[:, nlo:TT], srcb[:, nlo - sh:TT - sh], ALU.add, sk, [f"lv{lev % 2}"])
                    srcb, sk, lo = dstb, [f"lv{lev % 2}"], nlo
                    yield
                stt(pl[:, :], srcb[:, HALO:TT], 1.0 / w, rb[:, HALO:TT], ALU.mult, ALU.subtract, sk + rk, ["pl"])
                tt(lv[(gi + 1) % 2][:, 0:HALO], srcb[:, HALO:2 * HALO], MISC[:, 16 + gi * 16: 32 + gi * 16], ALU.mult, sk + ["MISC"],
                   [f"lv{(gi + 1) % 2}"])
                yield
                tt(pl[:, 0:HALO], lv[(gi + 1) % 2][:, 0:HALO], rb[:, HALO:2 * HALO], ALU.subtract, [f"lv{(gi + 1) % 2}", "pl"] + rk, ["pl"])
                dma(pwf[:, :], pool_w[l, gi], "pwf", [], ["pwf"])
                yield
                cp(plb[:, :], pl[:, :], ["pl"], ["plb"], eng='act')
                cp(pwb[:, :], pwf[:, :], ["pwf"], ["pwb"], eng='act')
                yield
                for hb_ in range(2):
                    b = bank()
                    mm(ps[b][:, :], pwb[:, :], plb[:, hb_ * 512:(hb_ + 1) * 512], True, True, ["pwb", "plb"], [f"ps{b}"])
                    act(mixed[:, 4 + gi, hb_ * 512:(hb_ + 1) * 512], ps[b][:, :], AF.Identity, [f"ps{b}", "COLP"], [f"mixedB{gi}_{hb_}"],
                        scale=COLP[:, LC + 124 + gi: LC + 125 + gi])
                yield

        run_interleaved([delay_gen(scan_gen(True, from_rcv=ex), 24), fin_thread(), slow_gen(mixer_b(), 2)],
                        pools=[[0, 1, 2, 3], [4], [5, 6, 7]])

        p.barrier()
        R = Seq(ar, M_SCR, SCR_END)
        x = R.a("x", [128, KC, T], F32)
        y = R.a("y", [128, KC, 512], F32)
        sqb = [R.a(f"rsqb{i}", [128, 512], BF16) for i in range(3)]
        rtmp = R.a("rrtmp", [128, 512], F32)
        rstd = R.a("rrstd", [128, 512], F32)
        mnT = R.a("mnT", [128, KC, MEM], BF16)
        eT = [ar.at(f"eT{i}", [128, 2, 512], BF16, R.off["mnT"] + 2048 * i) for i in range(2)]
        rden = ar.at("rden", [128, 512], F32, R.off["mnT"] + 4096)
        vTf = [R.a(f"vTf{i}", [128, MEM], BF16) for i in range(2)]
        xn2 = ar.at("xn2", [128, KC, 512], BF16, M_XN)
        q = ar.at("q", [128, KC, 512], BF16, M_XN + 16384)
        memst = ar.at("memst", [128, KC, MEM], F32, M_XN + 16384)
        o = ar.at("o", [128, KC, 512], BF16, M_MIX)
        kTm = ar.at("kTm", [128, KC, MEM], BF16, M_MIX + 16384)
        vm = ar.at("vm", [128, 2, D], BF16, M_MIX + 24576)
        hff = ar.at("hff", [128, FKC, 512], BF16, M_XN + 16384)
        sg = rden
        xv = x_src.rearrange("(k p) t -> p k t", p=128)
        xdv = x_dst.rearrange("(k p) t -> p k t", p=128)

        def post_norm_add(row, tb):
            rk = sumsq_rstd(lambda k: y[:, k, :], KC, 512, sqb, rstd, rtmp, "y")
            for k in range(KC):
                stt(y[:, k, :], y[:, k, :], gcol(l, row, k), rstd[:, :], ALU.mult, ALU.mult, ["y", rk, f"GS{l}"], ["y"])
                tt(x[:, k, tb * 512:(tb + 1) * 512], x[:, k, tb * 512:(tb + 1) * 512], y[:, k, :], ALU.add, [f"x{tb}", "y"], [f"x{tb}"])

        def pre_norm(row, tb, dst, dkey):
            rk = sumsq_rstd(lambda k: x[:, k, tb * 512:(tb + 1) * 512], KC, 512, sqb, rstd, rtmp, f"x{tb}")
            for k in range(KC):
                stt(dst[:, k, :], x[:, k, tb * 512:(tb + 1) * 512], gcol(l, row, k), rstd[:, :], ALU.mult, ALU.mult,
                    [f"x{tb}", rk, f"GS{l}"], [dkey])

        def cons_y(i, bi, pa, pk):
            cp(y[:, i, :], pa, [pk], ["y"])

        for tb in range(2):
            dma(x[:, :, tb * 512:(tb + 1) * 512], xv[:, :, tb * 512:(tb + 1) * 512], f"xld{tb}", [], [f"x{tb}"])
        dma(memst[:, :, :], memT_in.rearrange("(k p) t -> p k t", p=128), "memst", [], ["memst"])
        rk = sumsq_rstd(lambda k: memst[:, k, :], KC, MEM, sqb, rstd, rtmp, "memst")
        for k in range(KC):
            stt(mnT[:, k, :], memst[:, k, :], gcol(l, 4, k), rstd[:, 0:MEM], ALU.mult, ALU.mult, ["memst", rk, f"GS{l}"], ["mnT"])
        def w_out_tb(tb):
            linear_fm(w_out[l], list(range(KC)), KC, lambda k, t0, n, tb=tb: mixed[:, k, tb * 512:(tb + 1) * 512], ["mixedR"],
                      [(0, 512)], cons_y)
        w_out_tb(0)
        post_norm_add(1, 0)
        w_out_tb(1)

        linear_fm(xa_wk[l], list(range(KC)), KC, lambda k, t0, n: mnT[:, k, :], ["mnT"], [(0, MEM)],
                  lambda i, bi, pa, pk: cp(kTm[:, i, :], pa, [pk], ["kTm", "mixedR"]))
        post_norm_add(1, 1)
        pre_norm(2, 0, xn2, "xn2")
        pendv = []

        def cons_v(i, bi, pa, pk):
            buf = vTf[i % 2]
            cp(buf[:, :], pa, [pk], [f"vTf{i % 2}"])
            while pendv:
                pendv.pop(0)()

            def trs(i=i, buf=buf):
                b = bank()
                psb = ps[b][:, :].bitcast(BF16)
                for mt in range(2):
                    tr(psb[:, mt * 128:(mt + 1) * 128], buf[:, mt * 128:(mt + 1) * 128], IDB, [f"vTf{i % 2}", "CBF"], [f"ps{b}"])
                cp(vm[:, :, i * 128:(i + 1) * 128], psb[:, 0:256].rearrange("p (m n) -> p m n", m=2), [f"ps{b}"], ["vm", "mixedR"])
            pendv.append(trs)
        linear_fm(xa_wv[l], list(range(KC)), KC, lambda k, t0, n: mnT[:, k, :], ["mnT"], [(0, MEM)], cons_v)
        while pendv:
            pendv.pop(0)()

        p.barrier()
        for tb in range(2):
            linear_fm(xa_wq[l], list(range(KC)), KC, lambda k, t0, n: xn2[:, k, :], ["xn2"], [(0, 512)],
                      lambda i, bi, pa, pk: cp(q[:, i, :], pa, [pk], ["q"]))
            if tb == 0:
                pre_norm(2, 1, xn2, "xn2")
            else:
                pre_norm(5, 0, xn2, "xn2")
            def scores(h):
                e_ = eT[h % 2]
                ek = f"eT{h % 2}"
                for mt in range(2):
                    b = bank()
                    for c in range(4):
                        mm(ps[b][:, :], kTm[:, h * 4 + c, mt * 128:(mt + 1) * 128], q[:, h * 4 + c, :], c == 0, c == 3,
                           ["kTm", "q"], [f"ps{b}"])
                    act(e_[:, mt, :], ps[b][:, :], AF.Exp, [f"ps{b}"], [ek], scale=float(512 ** -0.5))
            scores(0)
            for h in range(4):
                e_ = eT[h % 2]
                ek = f"eT{h % 2}"
                if h + 1 < 4:
                    scores(h + 1)
                bd = bank()
                for mt in range(2):
                    mm(ps[bd][:, :], ONESB, e_[:, mt, :], mt == 0, mt == 1, ["CBF", ek], [f"ps{bd}"])
                act(rden[:, :], ps[bd][:, :], AF.Ln, [f"ps{bd}"], ["rden"])
                act(rden[:, :], rden[:, :], AF.Exp, ["rden"], ["rden"], scale=-1.0)
                for c in range(4):
                    b = bank()
                    for mt in range(2):
                        mm(ps[b][:, :], vm[:, mt, h * 512 + c * 128: h * 512 + (c + 1) * 128], e_[:, mt, :], mt == 0, mt == 1,
                           ["vm", ek], [f"ps{b}"])
                    tt(o[:, h * 4 + c, :], ps[b][:, :], rden[:, :], ALU.mult, [f"ps{b}", "rden"], ["o"])
            linear_fm(xa_wo[l], list(range(KC)), KC, lambda k, t0, n: o[:, k, :], ["o"], [(0, 512)], cons_y)
            post_norm_add(3, tb)

        for tb in range(2):
            for j in range(FKC):
                vg, kg = load_unit(w_gu[l][j], KC)
                vu, ku = load_unit(w_gu[l][FKC + j], KC)
                bg_ = bank()
                for k in range(KC):
                    mm(ps[bg_][:, :], vg[:, k, :], xn2[:, k, :], k == 0, k == KC - 1, [kg, "xn2"], [f"ps{bg_}"])
                bu_ = bank()
                for k in range(KC):
                    mm(ps[bu_][:, :], vu[:, k, :], xn2[:, k, :], k == 0, k == KC - 1, [ku, "xn2"], [f"ps{bu_}"])
                act(sg[:, :], ps[bg_][:, :], AF.Silu, [f"ps{bg_}"], ["sg", "rden"])
                tt(hff[:, j, :], sg[:, :], ps[bu_][:, :], ALU.mult, ["sg", f"ps{bu_}"], ["hff", "q", "o", "kTm", "vm"])
            if tb == 0:
                pre_norm(5, 1, xn2, "xn2")
            linear_fm(w_down[l], list(range(KC)), FKC, lambda k, t0, n: hff[:, k, :], ["hff"], [(0, 512)], cons_y)
            post_norm_add(6, tb)
            dma(xdv[:, :, tb * 512:(tb + 1) * 512], x[:, :, tb * 512:(tb + 1) * 512], f"xst{tb}", [f"x{tb}"], [f"xdst{tb}"])
        if l + 1 < n_layers:
            dma(snd[4].ap()[:, 0:256].rearrange("p (k t) -> p k t", k=KC), x[:, :, T - HALO:T], "snd4", ["x1"], ["snd4"])
            exchange(4, ["snd4"], ["rcv4"])
        return [f"xst{tb}" for tb in range(2)]

    final_streams = None
    for l in range(n_layers):
        last = (l == n_layers - 1)
        final_streams = layer(l, xT_in if l == 0 else xs, None if l == 0 else 4, out if last else xs)

    p.plan()
    with contextlib.ExitStack() as es:
        sems = {e: es.enter_context(nc.semaphore(f"s_{e}")) for e in p.ENG}
        ssems = {s: es.enter_context(nc.semaphore(f"d_{s}")) for s in p.stream_cnt}
        block = es.enter_context(nc.Block())

        @block.tensor
        def _(e):
            p.emit_engine('pe', e, sems, ssems)

        @block.scalar
        def _(e):
            p.emit_engine('act', e, sems, ssems)

        @block.vector
        def _(e):
            p.emit_engine('dve', e, sems, ssems)

        @block.gpsimd
        def _(e):
            p.emit_engine('pool', e, sems, ssems)

        @block.sync
        def _(e):
            p.emit_engine('sp', e, sems, ssems)
            for s in p.stream_cnt:
                if s.startswith("xst") or s.startswith("xld"):
                    e.wait_ge(ssems[s], p.stream_cnt[s])
    return nc, p


_CACHE = {}


def _host_prep(inputs):
    f = lambda a: np.ascontiguousarray(np.asarray(a, dtype=np.float32))
    x = f(inputs['x'])
    mem = f(inputs['mem'])
    cf = np.zeros((128, 512), np.float32)
    cf[:, 0:128] = np.eye(128)
    cf[:, 128:256] = 1.0
    cf[:, 256:384] = np.triu(np.ones((128, 128), np.float32))
    cf[:, 384:512] = np.tril(np.ones((128, 128), np.float32), -1)
    rowp = np.zeros((1, 2 * 672), np.float32)
    colp = np.zeros((128, 2 * 216), np.float32)
    for l in range(2):
        r = rowp[0, l * 672:(l + 1) * 672]
        r[0:4] = inputs['gdn_A_log'][l]
        r[4:8] = inputs['gdn_dt_bias'][l]
        r[8:16] = inputs['ssm_A_log'][l]
        r[16:24] = inputs['ssm_dt_bias'][l]
        r[24:32] = inputs['ssm_D'][l]
        r[32:160] = inputs['gdn_norm_g'][l]
        r[160:672] = inputs['ssm_norm_g'][l]
        c = colp[:, l * 216:(l + 1) * 216]
        c[:, 0:112] = np.asarray(inputs['norm_g'][l]).reshape(7, 16, 128).transpose(2, 0, 1).reshape(128, 112)
        c[:, 112:124] = np.asarray(inputs['conv_a_w'][l]).reshape(3, 4, 128).transpose(2, 1, 0).reshape(128, 12)
        c[:, 124:128] = np.asarray(inputs['pool_scale'][l]).reshape(4, 128).T
        c[:, 128:176] = np.asarray(inputs['gdn_conv_w'][l]).reshape(4, 12, 128).transpose(2, 1, 0).reshape(128, 48)
        c[:, 176:208] = np.asarray(inputs['ssm_conv_w'][l]).reshape(4, 8, 128).transpose(2, 1, 0).reshape(128, 32)
        c[:, 208:216] = np.asarray(inputs['ssm_conv_b'][l]).reshape(8, 128).T
    rowp = np.ascontiguousarray(np.broadcast_to(rowp, (128, 2 * 672)))
    shared = dict(cf=cf, rowp=rowp, colp=colp)

    def tile_w(w):
        w = f(w)
        L, K, N = w.shape
        return np.ascontiguousarray(w.reshape(L, K // 128, 128, N // 128, 128).transpose(0, 3, 2, 1, 4))
    w_in_ = f(inputs['w_in'])
    a_cols = []
    for j in range(4):
        for base in (A_C, A_H, A_B):
            a_cols.append(np.arange(base + 128 * j, base + 128 * (j + 1)))
    fm_cols = np.concatenate([np.arange(D_XBC, D_XBC + 1024), np.arange(C_QKV, C_QKV + 1536)] + a_cols + [np.arange(B_U, B_U + 512)])
    tm_cols = np.concatenate([np.arange(D_Z, D_Z + 512), np.arange(C_Z, C_Z + 512), np.arange(D_DT, D_DT + 8), np.arange(C_AB, C_AB + 8)])
    shared['w_in_fm'] = tile_w(w_in_[:, :, fm_cols])
    shared['w_in_tm'] = np.ascontiguousarray(w_in_[:, :, tm_cols])
    shared['pool_w'] = f(inputs['pool_w'])
    shared['w_out_t'] = tile_w(inputs['w_out'])
    shared['xa_wq_t'] = tile_w(inputs['xa_wq'])
    wkv = f(inputs['xa_wkv'])
    shared['xa_wk_t'] = tile_w(wkv[:, :, :D])
    shared['xa_wv_t'] = tile_w(wkv[:, :, D:])
    shared['xa_wo_t'] = tile_w(inputs['xa_wo'])
    shared['ffn_w_gu_t'] = tile_w(inputs['ffn_w_gu'])
    shared['ffn_w_down_t'] = tile_w(inputs['ffn_w_down'])
    in_maps = []
    for core in range(8):
        b, s = core // 2, core % 2
        m = dict(shared)
        m['xT'] = np.ascontiguousarray(x[b, s * T:(s + 1) * T, :].T)
        if s == 0:
            m['xh'] = np.zeros((D, HALO), np.float32)
        else:
            m['xh'] = np.ascontiguousarray(x[b, T - HALO:T, :].T)
        m['memT'] = np.ascontiguousarray(mem[b].T)
        misc = np.zeros((128, 80), np.float32)
        misc[:, 0] = float(s)
        for gi in range(4):
            w = 2 ** (gi + 1)
            for t in range(HALO):
                misc[:, 16 + gi * 16 + t] = 1.0 / (min(t + 1, w) if s == 0 else w)
        m['misc'] = misc
        in_maps.append(m)
    return in_maps


def kernel(**inputs):
    if 'nc' not in _CACHE:
        _CACHE['nc'] = build()[0]
    nc = _CACHE['nc']
    in_maps = _host_prep(inputs)
    res = run_bass_kernel_spmd(nc, in_maps, core_ids=list(range(8)))
    outp = np.zeros((4, 2 * T, D), np.float32)
    for core in range(8):
        b, s = core // 2, core % 2
        outp[b, s * T:(s + 1) * T, :] = res.results[core]['out'].T
    return outp
```
